# Optimizing a Trainium2 kernel written in Bass

```python
import jax, jax.numpy as jnp
from jax import lax
import numpy as np

D_MODEL = 2048
BATCH = 4
SEQ = 2048
DEPTH = 1
DEC_BATCH = 128
DEC_SEQ = 8
PAST_LEN = 16384
PAGE_SIZE = 128

D_MIX = D_MODEL
C_A = D_MIX // 2
C_B = D_MIX - C_A
N_GROUPS_A = 8
N_HEADS_B = 8
HEAD_B = C_B // N_HEADS_B
CONF_KW = 31
LRU_KW = 4
FFN_KW = 3
D_FF = 3 * D_MODEL
LRU_C = 8.0
EPS = 1e-6

kernel_name = "hybrid_conformerconv_rglru_convffn_step"


def rms_norm(x, g):
    x32 = x.astype(jnp.float32)
    y = x32 * lax.rsqrt(jnp.mean(x32 * x32, axis=-1, keepdims=True) + EPS) * g.astype(jnp.float32)
    return y.astype(x.dtype)


def layer_norm(x, g, b):
    x32 = x.astype(jnp.float32)
    mu = jnp.mean(x32, axis=-1, keepdims=True)
    var = jnp.mean(jnp.square(x32 - mu), axis=-1, keepdims=True)
    y = (x32 - mu) * lax.rsqrt(var + EPS) * g.astype(jnp.float32) + b.astype(jnp.float32)
    return y.astype(x.dtype)


def causal_dwconv(u, buf, w, b):
    width = w.shape[0]
    ext = jnp.concatenate([buf.astype(u.dtype), u], axis=1)
    y = lax.conv_general_dilated(ext, w[:, None, :].astype(u.dtype), (1,), 'VALID',
                                 dimension_numbers=('NWC', 'WIO', 'NWC'),
                                 feature_group_count=u.shape[-1])
    return y + b.astype(u.dtype), ext[:, ext.shape[1] - (width - 1):]


def _lin_combine(left, right):
    a1, b1 = left
    a2, b2 = right
    return a1 * a2, a2 * b1 + b2


def rglru(xc, h0, wa, ba, wx, bx, lam, reset_first):
    bsz, t, c = xc.shape
    xf = xc.astype(jnp.float32)
    xh = xf.reshape(bsz, t, N_HEADS_B, HEAD_B)
    r = jax.nn.sigmoid(jnp.einsum('bthi,hij->bthj', xh, wa.astype(jnp.float32)).reshape(bsz, t, c)
                       + ba.astype(jnp.float32))
    i = jax.nn.sigmoid(jnp.einsum('bthi,hij->bthj', xh, wx.astype(jnp.float32)).reshape(bsz, t, c)
                       + bx.astype(jnp.float32))
    log_a = -LRU_C * r * jax.nn.softplus(-lam.astype(jnp.float32))
    a = jnp.exp(log_a)
    mult = jnp.sqrt(-jnp.expm1(2.0 * log_a))
    if reset_first:
        mult = mult.at[:, 0].set(1.0)
    bterm = mult * (i * xf)
    a_cum, b_cum = lax.associative_scan(_lin_combine, (a, bterm), axis=1)
    h = a_cum * h0.astype(jnp.float32)[:, None, :] + b_cum
    return h, h[:, -1]


def mixer(xn, conf_buf, lru_buf, h0, p, reset_first):
    proj = xn @ p['w_in']
    a_val, a_gate, b_x, b_gate = jnp.split(proj, [C_A, 2 * C_A, 2 * C_A + C_B], axis=-1)
    u = a_val * jax.nn.sigmoid(a_gate)
    ca, new_conf = causal_dwconv(u, conf_buf, p['conf_dw_w'], p['conf_dw_b'])
    ca = jax.nn.silu(layer_norm(ca, p['conf_ln_g'], p['conf_ln_b']))
    xc, new_lru = causal_dwconv(b_x, lru_buf, p['lru_conv_w'], p['lru_conv_b'])
    h, h_last = rglru(xc, h0, p['lru_wa'], p['lru_ba'], p['lru_wx'], p['lru_bx'], p['lru_lambda'], reset_first)
    cb = h.astype(xn.dtype) * jax.nn.gelu(b_gate)
    out = jnp.concatenate([ca, cb], axis=-1) @ p['w_out']
    return out, new_conf, new_lru, h_last


def conv_ffn(xn, buf, p):
    up = xn @ p['w_up']
    upc, new_buf = causal_dwconv(up, buf, p['ffn_dw_w'], p['ffn_dw_b'])
    g, v = jnp.split(upc, 2, axis=-1)
    return (jax.nn.gelu(g) * v) @ p['w_down'], new_buf


def layer(x, conf_buf, lru_buf, h0, ffn_buf, p, reset_first):
    m, nc, nl, nh = mixer(rms_norm(x, p['g_mix_pre']), conf_buf, lru_buf, h0, p, reset_first)
    x = x + rms_norm(m, p['g_mix_post'])
    f, nf = conv_ffn(rms_norm(x, p['g_ffn_pre']), ffn_buf, p)
    x = x + rms_norm(f, p['g_ffn_post'])
    return x, nc, nl, nh, nf


def setup_inputs(seed: int = 0) -> dict:
    key = jax.random.key(seed)
    ks = iter(jax.random.split(key, 32))
    f32 = jnp.float32

    def nrm(shape, scale):
        return jax.random.normal(next(ks), shape, f32) * scale

    def gain(shape):
        return 1.0 + 0.05 * jax.random.normal(next(ks), shape, f32)

    L = DEPTH
    inp = {}
    inp['x_prompt'] = jax.random.normal(next(ks), (BATCH, SEQ, D_MODEL), f32)
    inp['x_sample'] = jax.random.normal(next(ks), (DEC_BATCH, DEC_SEQ, D_MODEL), f32)
    inp['state_conf_conv'] = nrm((L, DEC_BATCH, CONF_KW - 1, C_A), 0.5)
    inp['state_lru_conv'] = nrm((L, DEC_BATCH, LRU_KW - 1, C_B), 1.0)
    inp['state_lru_h'] = nrm((L, DEC_BATCH, C_B), 0.5)
    inp['state_ffn_conv'] = nrm((L, DEC_BATCH, FFN_KW - 1, 2 * D_FF), 1.0)
    inp['g_mix_pre'] = gain((L, D_MODEL))
    inp['g_mix_post'] = gain((L, D_MODEL))
    inp['w_in'] = nrm((L, D_MODEL, 2 * C_A + 2 * C_B), D_MODEL ** -0.5)
    inp['conf_dw_w'] = nrm((L, CONF_KW, C_A), CONF_KW ** -0.5)
    inp['conf_dw_b'] = nrm((L, C_A), 0.02)
    inp['conf_ln_g'] = gain((L, C_A))
    inp['conf_ln_b'] = nrm((L, C_A), 0.02)
    inp['lru_conv_w'] = nrm((L, LRU_KW, C_B), LRU_KW ** -0.5)
    inp['lru_conv_b'] = nrm((L, C_B), 0.02)
    inp['lru_wa'] = nrm((L, N_HEADS_B, HEAD_B, HEAD_B), HEAD_B ** -0.5)
    inp['lru_ba'] = nrm((L, C_B), 0.02)
    inp['lru_wx'] = nrm((L, N_HEADS_B, HEAD_B, HEAD_B), HEAD_B ** -0.5)
    inp['lru_bx'] = nrm((L, C_B), 0.02)
    a8 = jax.random.uniform(next(ks), (L, C_B), f32, 0.9, 0.999)
    s = a8 ** (1.0 / LRU_C)
    inp['lru_lambda'] = jnp.log(s) - jnp.log1p(-s)
    inp['w_out'] = nrm((L, C_A + C_B, D_MODEL), (C_A + C_B) ** -0.5)
    inp['g_ffn_pre'] = gain((L, D_MODEL))
    inp['g_ffn_post'] = gain((L, D_MODEL))
    inp['w_up'] = nrm((L, D_MODEL, 2 * D_FF), D_MODEL ** -0.5)
    inp['ffn_dw_w'] = nrm((L, FFN_KW, 2 * D_FF), FFN_KW ** -0.5)
    inp['ffn_dw_b'] = nrm((L, 2 * D_FF), 0.02)
    inp['w_down'] = nrm((L, D_FF, D_MODEL), D_FF ** -0.5)
    return inp


def reference(x_prompt, x_sample, state_conf_conv, state_lru_conv, state_lru_h, state_ffn_conv,
              g_mix_pre, g_mix_post, w_in, conf_dw_w, conf_dw_b, conf_ln_g, conf_ln_b,
              lru_conv_w, lru_conv_b, lru_wa, lru_ba, lru_wx, lru_bx, lru_lambda, w_out,
              g_ffn_pre, g_ffn_post, w_up, ffn_dw_w, ffn_dw_b, w_down):
    bp = x_prompt.shape[0]
    dt = x_prompt.dtype
    yp, ys = x_prompt, x_sample
    p_conf, p_lru, p_h, p_ffn = [], [], [], []
    s_conf, s_lru, s_h, s_ffn = [], [], [], []
    for l in range(DEPTH):
        p = dict(g_mix_pre=g_mix_pre[l], g_mix_post=g_mix_post[l], w_in=w_in[l],
                 conf_dw_w=conf_dw_w[l], conf_dw_b=conf_dw_b[l], conf_ln_g=conf_ln_g[l], conf_ln_b=conf_ln_b[l],
                 lru_conv_w=lru_conv_w[l], lru_conv_b=lru_conv_b[l], lru_wa=lru_wa[l], lru_ba=lru_ba[l],
                 lru_wx=lru_wx[l], lru_bx=lru_bx[l], lru_lambda=lru_lambda[l], w_out=w_out[l],
                 g_ffn_pre=g_ffn_pre[l], g_ffn_post=g_ffn_post[l], w_up=w_up[l],
                 ffn_dw_w=ffn_dw_w[l], ffn_dw_b=ffn_dw_b[l], w_down=w_down[l])
        yp, nc, nl, nh, nf = layer(
            yp, jnp.zeros((bp, CONF_KW - 1, C_A), dt), jnp.zeros((bp, LRU_KW - 1, C_B), dt),
            jnp.zeros((bp, C_B), jnp.float32), jnp.zeros((bp, FFN_KW - 1, 2 * D_FF), dt), p, True)
        p_conf.append(nc); p_lru.append(nl); p_h.append(nh); p_ffn.append(nf)
        ys, nc, nl, nh, nf = layer(ys, state_conf_conv[l], state_lru_conv[l], state_lru_h[l],
                                   state_ffn_conv[l], p, False)
        s_conf.append(nc); s_lru.append(nl); s_h.append(nh); s_ffn.append(nf)
    return (yp, ys,
            jnp.stack(p_conf), jnp.stack(p_lru), jnp.stack(p_h), jnp.stack(p_ffn),
            jnp.stack(s_conf), jnp.stack(s_lru), jnp.stack(s_h), jnp.stack(s_ffn))
```

```python
import numpy as np
from contextlib import ExitStack
import concourse.bass as bass
import concourse.mybir as mybir
from concourse.bass_utils import run_bass_kernel_spmd

F32 = mybir.dt.float32
BF16 = mybir.dt.bfloat16
AF = mybir.ActivationFunctionType
ALU = mybir.AluOpType

NCORES = 8
D = 2048
CA = 1024
DFF = 6144
NPRE = 992
NACT = 1184
HALO = 32
NALL = 2176
EPS = 1e-6

V_CW, V_CB, V_LG, V_LB, V_RW, V_RB, V_BA, V_BX, V_LAM, V_FW, V_FB, V_N = 0, 248, 256, 264, 272, 304, 312, 320, 328, 336, 624, 720


DEBUG = {}


class _Stop(Exception):
    pass


class Op:
    __slots__ = ("eng", "fn", "deps", "sem", "val", "cidx", "needs_inc", "is_dma")

    def __init__(self, eng, fn, is_dma):
        self.eng = eng
        self.fn = fn
        self.deps = []
        self.sem = None
        self.val = None
        self.cidx = -1
        self.needs_inc = False
        self.is_dma = is_dma


class SemRing:
    def __init__(self, nc, es, name, n):
        self.sems = [es.enter_context(nc.semaphore(f"{name}{i}")) for i in range(n)]
        self.n = 0
        self.last = [None] * n

    def next(self):
        i = self.n % len(self.sems)
        v = 16 * (self.n // len(self.sems) + 1)
        self.n += 1
        return i, self.sems[i], v


class Sched:
    NAMES = ["pe", "act", "dve", "pool", "sp"]

    def __init__(self, nc, es):
        self.nc = nc
        self.es = es
        self.streams = {n: [] for n in self.NAMES}
        self.last_w = {}
        self.readers = {}
        self.prog = {n: es.enter_context(nc.semaphore("prog_" + n)) for n in self.NAMES}
        self.ncomp = {n: 0 for n in self.NAMES}
        self.out_dmas = []
        self.stopped = False
        self.gen = 0
        self.fence_ops = []
        self.key_gen = {}
        self.dma_latest = {}

    def fence(self):
        f = []
        for n in self.NAMES:
            for o in reversed(self.streams[n]):
                if not o.is_dma:
                    f.append(o)
                    break
        f.extend(self.dma_latest.values())
        self.fence_ops = f
        self.gen += 1

    def _collect(self, reads, writes):
        deps = []
        for k in list(reads) + list(writes):
            if k not in self.key_gen:
                self.key_gen[k] = self.gen
                if self.gen > 0:
                    for p in self.fence_ops:
                        deps.append((p, "raw"))
        for k in reads:
            w = self.last_w.get(k)
            if w is not None:
                deps.append((w, "raw"))
        for k in writes:
            w = self.last_w.get(k)
            if w is not None:
                deps.append((w, "waw"))
            for r in self.readers.get(k, ()):
                deps.append((r, "war"))
        return deps

    def _register(self, o, reads, writes):
        for k in reads:
            self.readers.setdefault(k, []).append(o)
        for k in writes:
            self.last_w[k] = o
            self.readers[k] = []

    def op(self, eng, fn, reads=(), writes=()):
        if self.stopped:
            return None
        o = Op(eng, fn, False)
        o.cidx = self.ncomp[eng]
        self.ncomp[eng] += 1
        for p, kind in self._collect(reads, writes):
            if p is o or p in o.deps:
                continue
            if (not p.is_dma) and p.eng == eng:
                if eng == "pe" or kind != "raw" or (o.cidx - p.cidx) > 2:
                    continue
            o.deps.append(p)
        self._register(o, reads, writes)
        self.streams[eng].append(o)
        return o

    def dma(self, queue, ring, fn, reads=(), writes=(), is_out=False):
        if self.stopped:
            return None
        o = Op(queue, fn, True)
        i, sem, val = ring.next()
        o.sem, o.val = sem, val
        for p, kind in self._collect(reads, writes):
            if p not in o.deps:
                o.deps.append(p)
        if ring.last[i] is not None and ring.last[i] not in o.deps:
            o.deps.append(ring.last[i])
        ring.last[i] = o
        self.dma_latest[sem.num] = o
        self._register(o, reads, writes)
        self.streams[queue].append(o)
        if is_out:
            self.out_dmas.append(o)
        return o

    def emit(self):
        nc = self.nc
        fin = Op("sp", None, False)
        fin.deps = [o for n in self.NAMES for o in self.streams[n] if o.is_dma]
        self.streams["sp"].append(fin)
        for n in self.NAMES:
            for o in self.streams[n]:
                for p in o.deps:
                    p.needs_inc = True
        for n in self.NAMES:
            c = 0
            for o in self.streams[n]:
                if (not o.is_dma) and o.needs_inc:
                    c += 1
                    o.sem, o.val = self.prog[n], c
        block = self.es.enter_context(nc.Block())
        deco = dict(pe=block.tensor, act=block.scalar, dve=block.vector, pool=block.gpsimd, sp=block.sync)
        for n in self.NAMES:
            def body(e, n=n):
                known = {}
                for o in self.streams[n]:
                    for p in o.deps:
                        key = p.sem.num
                        if known.get(key, 0) >= p.val:
                            continue
                        e.wait_ge(p.sem, p.val)
                        known[key] = p.val
                    if o.fn is None:
                        continue
                    ins = o.fn(e)
                    if o.is_dma:
                        ins.then_inc(o.sem, 16)
                    elif o.needs_inc:
                        ins.then_inc(o.sem, 1)
            deco[n](body)


def build_program():
    nc = bass.Bass("TRN2", target_bir_lowering=False)

    def din(name, shape):
        return nc.dram_tensor(name, list(shape), F32, kind="ExternalInput").ap()

    def dout(name, shape):
        return nc.dram_tensor(name, list(shape), F32, kind="ExternalOutput").ap()

    xall = din("xall", [NALL, D])
    w_in = din("w_in", [D, 4096])
    w_out = din("w_out", [D, D])
    w_up = din("w_up", [D, 2 * DFF])
    w_down = din("w_down", [DFF, D])
    wa_d = din("wa", [8, 128, 128])
    wx_d = din("wx", [8, 128, 128])
    vecs_d = din("vecs", [128, V_N])
    grows_d = din("grows", [4, D])
    flags_d = din("flags", [128, 8])
    ident_d = din("ident", [128, 128])
    st_conf_d = din("st_conf", [480, CA])
    st_lruc_d = din("st_lruc", [48, CA])
    st_h_d = din("st_h", [16, CA])
    st_ffn_d = din("st_ffn", [32, 2 * DFF])

    y_d = dout("y", [1152, D])
    p_conf_d = dout("p_conf", [30, CA])
    p_lruc_d = dout("p_lruc", [3, CA])
    p_h_d = dout("p_h", [1, CA])
    p_ffn_d = dout("p_ffn", [2, 2 * DFF])
    s_conf_d = dout("s_conf", [480, CA])
    s_lruc_d = dout("s_lruc", [48, CA])
    s_h_d = dout("s_h", [16, CA])
    s_ffn_d = dout("s_ffn", [32, 2 * DFF])

    h_scr = nc.dram_tensor("h_scr", [1152, D], F32, kind="Internal").ap()
    f_scr = nc.dram_tensor("f_scr", [1152, D], F32, kind="Internal").ap()

    with ExitStack() as es:
      S = Sched(nc, es)
      try:

        def sb(name, shape, dt=F32, stack=es):
            return stack.enter_context(nc.sbuf_tensor("sb_" + name, list(shape), dt))

        ps = [es.enter_context(nc.psum_tensor(f"ps{b}", [128, 512], F32)) for b in range(8)]
        pstate = {"b": 0}

        busy = [False] * 8

        def bank():
            for _ in range(8):
                b = pstate["b"]
                pstate["b"] = (b + 1) % 8
                if not busy[b]:
                    busy[b] = True
                    return b
            raise RuntimeError("no free PSUM bank: consumer of a previous tile must be registered first")

        def bank_at(b):
            assert not busy[b], f"bank {b} busy"
            busy[b] = True
            return b

        def rel(*bs):
            for b in bs:
                assert busy[b]
                busy[b] = False

        def PK(b):
            return ("ps", b)

        r_setup = SemRing(nc, es, "su", 16)
        r_x = SemRing(nc, es, "xl", 4)
        r_w = SemRing(nc, es, "wl", 8)
        r_st = SemRing(nc, es, "st", 4)
        r_ld = SemRing(nc, es, "ld", 4)
        r_out = SemRing(nc, es, "ot", 8)

        def act_copy(dst, src):
            return lambda e: e.copy(out=dst, in_=src)

        def dve_copy(dst, src):
            return lambda e: e.tensor_copy(dst, src)

        def evac(eng, dst, src):
            return act_copy(dst, src) if eng == "act" else dve_copy(dst, src)

        dbg_outs = []

        def dbg_dump(name, ap, shape, dt, keys):
            if not DEBUG.get("on") or S.stopped:
                return
            d = nc.dram_tensor("dbg_" + name, list(shape), dt, kind="ExternalOutput").ap()
            S.dma("sp", r_out, lambda e: e.dma_start(out=d, in_=ap), reads=keys, is_out=True)

        dbgf = sb("dbgf", [128, 2304], F32) if DEBUG.get("on") else None
        dcount = {"n": 0}

        def dbg_dump16(name, ap, n, keys):
            if not DEBUG.get("on") or S.stopped:
                return
            d = nc.dram_tensor("dbg_" + name, [128, n], F32, kind="ExternalOutput").ap()
            S.op("act", act_copy(dbgf[:, 0:n], ap), reads=list(keys), writes=["dbgf"])
            S.dma("sp", r_out, lambda e: e.dma_start(out=d, in_=dbgf[:, 0:n]), reads=["dbgf"], is_out=True)

        def dbg_stop(tag):
            if DEBUG.get("on") and DEBUG.get("stop") == tag:
                S.stopped = True

        ident = sb("ident", [128, 128])
        identb = sb("identb", [128, 128], BF16)
        ones = sb("ones", [128, 128])
        vecs = sb("vecs", [128, V_N])
        flags = sb("flags", [128, 8])
        cvec = sb("cvec", [128, 8])
        tmpv = sb("tmpv", [128, 8])
        xn_act = sb("xn_act", [128, 16, NACT], BF16)

        S.dma("sp", r_setup, lambda e: e.dma_start(out=ident[:], in_=ident_d), writes=["ident"])
        S.dma("sp", r_setup, lambda e: e.dma_start(out=vecs[:], in_=vecs_d), writes=["vecs"])
        S.dma("sp", r_setup, lambda e: e.dma_start(out=flags[:], in_=flags_d), writes=["flags"])
        S.op("dve", dve_copy(identb[:], ident[:]), reads=["ident"], writes=["identb"])
        S.op("dve", lambda e: e.memset(ones[:], 1.0), writes=["ones"])
        S.op("act", lambda e: e.activation(out=tmpv[:], in_=vecs[:, V_LAM:V_LAM + 8], func=AF.Exp, scale=-1.0),
             reads=["vecs"], writes=["tmpv"])
        S.op("act", lambda e: e.activation(out=cvec[:], in_=tmpv[:], func=AF.Ln, bias=1.0),
             reads=["tmpv"], writes=["cvec0"])
        S.op("dve", lambda e: e.tensor_scalar(cvec[:], cvec[:], -8.0, None, ALU.mult),
             reads=["cvec0"], writes=["cvec"])

        def tr_out(src_ap, M, dst_ap, src_keys, dst_key):
            b = bank()
            S.op("pe", lambda e: e.transpose(ps[b][0:M, 0:128], src_ap, ident[:]),
                 reads=list(src_keys) + ["ident"], writes=[PK(b)])
            S.op("act", act_copy(dst_ap, ps[b][0:M, 0:128]), reads=[PK(b)], writes=[dst_key])
            rel(b)

        PRE_T = [(0, 496), (496, 992)]
        ACT_T = [(0, 352), (352, 704), (704, 1056), (1056, 1184)]
        ALL_T = [(0, 512), (512, 1024), (1024, 1536), (1536, 2048), (2048, 2176)]

        def xn_keys_pre(c0, c1, k):
            return [("xn", i, k // 4) for i in range(c0 // 128, (c1 - 1) // 128 + 1)]

        def xn_keys_act(a0, a1, k):
            keys = []
            if a0 < 32:
                keys.append(("xn", 70, k // 4))
            lo = max(a0, 32)
            if a1 > 32:
                for i in range(8 + (lo - 32) // 128, 8 + (a1 - 1 - 32) // 128 + 1):
                    keys.append(("xn", i, k // 4))
            return keys

        def load_w_pair(slot, c0, c1, buf, key):
            for q, c in enumerate((c0, c1)):
                S.dma("pool", r_w, lambda e, q=q, c=c: e.dma_start(
                    out=buf[:, slot, q], in_=w_in[:, c:c + 128].rearrange("(k p) c -> p k c", p=128)),
                    writes=[(key, slot, q)])

        with ExitStack() as es12:
            catB = sb("catB", [128, 8, NACT], BF16, es12)

            def cat(k):
                return catA[:, k] if k < 8 else catB[:, k - 8]

            with ExitStack() as esB:
                xn_pre = sb("xn_pre", [128, 16, NPRE], BF16, esB)
                wbuf = sb("wbufB", [128, 3, 2, 16, 128], BF16, esB)

                def loadB(jj):
                    load_w_pair(jj % 3, 2048 + 128 * jj, 3072 + 128 * jj, wbuf, "wB")

                loadB(0)
                loadB(1)
                with ExitStack() as es0:
                    grow = sb("grow0", [128, D], F32, es0)
                    xt = [sb(f"p0xt{i}", [128, D], F32, es0) for i in range(4)]
                    xs = [sb(f"p0xs{i}", [128, D], BF16, es0) for i in range(4)]
                    ss = [sb(f"p0ss{i}", [128, 2], F32, es0) for i in range(4)]
                    S.dma("sp", r_setup, lambda e: e.dma_start(out=grow[:], in_=grows_d[0:1, :].partition_broadcast(128)),
                          writes=["grow0"])
                    def p0_A(i):
                        sl = i % 4
                        S.dma("sp", r_x, lambda e, i=i, sl=sl: e.dma_start(out=xt[sl][:], in_=xall[128 * i:128 * i + 128, :]),
                              writes=[("xt", sl)])
                        S.op("act", lambda e, sl=sl: e.activation(out=xs[sl][:], in_=xt[sl][:], func=AF.Square,
                                                                   accum_out=ss[sl][:, 0:1]),
                             reads=[("xt", sl)], writes=[("xs", sl), ("ss0", sl)])
                        S.op("act", lambda e, sl=sl: e.activation(out=ss[sl][:, 1:2], in_=ss[sl][:, 0:1], func=AF.Sqrt,
                                                                   scale=1.0 / D, bias=EPS),
                             reads=[("ss0", sl)], writes=[("ss1", sl)])
                        S.op("dve", lambda e, sl=sl: e.reciprocal(ss[sl][:, 1:2], ss[sl][:, 1:2]),
                             reads=[("ss1", sl)], writes=[("ss2", sl)])
                        S.op("dve", lambda e, sl=sl: e.scalar_tensor_tensor(xs[sl][:], xt[sl][:], ss[sl][:, 1:2], grow[:],
                                                                             ALU.mult, ALU.mult),
                             reads=[("xt", sl), ("ss2", sl), "grow0"], writes=[("xs", sl)])

                    def p0_B(i):
                        sl = i % 4
                        for kq in range(4):
                            b = bank()
                            pb = ps[b][:].bitcast(BF16)
                            for j in range(4):
                                k = kq * 4 + j
                                S.op("pe", lambda e, sl=sl, k=k, j=j, pb=pb: e.transpose(
                                    pb[:, j * 128:(j + 1) * 128], xs[sl][:, k * 128:(k + 1) * 128], identb[:]),
                                    reads=[("xs", sl), "identb"], writes=[PK(b)])
                            pv = pb[:, 0:512].rearrange("p (j t) -> p j t", t=128)
                            eng = "act" if kq % 2 == 0 else "dve"
                            kk = slice(kq * 4, kq * 4 + 4)
                            if i < 7:
                                S.op(eng, evac(eng, xn_pre[:, kk, 128 * i:128 * i + 128], pv),
                                     reads=[PK(b)], writes=[("xn", i, kq)])
                            elif i == 7:
                                S.op(eng, evac(eng, xn_pre[:, kk, 896:992], pv[:, :, 0:96]),
                                     reads=[PK(b)], writes=[("xn", 7, kq)])
                                S.op(eng, evac(eng, xn_act[:, kk, 0:32], pv[:, :, 96:128]),
                                     reads=[PK(b)], writes=[("xn", 70, kq)])
                            else:
                                a0 = 32 + 128 * (i - 8)
                                S.op(eng, evac(eng, xn_act[:, kk, a0:a0 + 128], pv),
                                     reads=[PK(b)], writes=[("xn", i, kq)])
                            rel(b)

                    p0_A(0)
                    for i in range(17):
                        if i + 1 < 17:
                            p0_A(i + 1)
                        p0_B(i)

                    if DEBUG.get("on") and DEBUG.get("stop") == "p0":
                        allxn = [("xn", i, kq) for i in list(range(17)) + [70] for kq in range(4)]
                        dbg_dump("xt0", xt[0][:], [128, D], F32, [("xt", 0)])
                        dbg_dump("ss0", ss[0][:], [128, 2], F32, [("ss2", 0)])
                        dbg_dump16("xs0", xs[0][:], D, [("xs", 0)])
                        dbg_dump("identf", ident[:], [128, 128], F32, ["ident"])
                        dbg_dump16("identb", identb[:], 128, ["identb"])
                        dbg_dump16("xn_act0", xn_act[:, 0, :], NACT, allxn)
                        dbg_dump16("xn_act5", xn_act[:, 5, :], NACT, allxn)
                        dbg_stop("p0")
                S.fence()
                with ExitStack() as esBw:
                    wab = sb("wab", [128, 8, 128], BF16, esBw)
                    wxb = sb("wxb", [128, 8, 128], BF16, esBw)
                    bxe = [sb(f"bxe{i}", [128, 2051 + 176], F32, esBw) for i in range(2)]
                    xcs_ = [sb(f"xc{i}", [128, NALL], F32, esBw) for i in range(2)]
                    xcb = sb("xcb", [128, NALL], BF16, esBw)
                    t1 = sb("t1", [128, NALL], F32, esBw)
                    t2 = sb("t2", [128, NALL], F32, esBw)
                    t3 = sb("t3", [128, NALL], F32, esBw)
                    gbg = [sb(f"gbg{i}", [128, NACT], F32, esBw) for i in range(2)]
                    small = sb("smallB", [128, 8], F32, esBw)
                    lruc_fm = sb("lruc_fm", [128, 8, 51], F32, esBw)
                    hst_fm = sb("hst_fm", [128, 8, 17], F32, esBw)
                    slruc_fm = sb("slruc_fm", [128, 8, 16, 3], F32, esBw)
                    slruh_fm = sb("slruh_fm", [128, 8, 16, 1], F32, esBw)
                    st_tm = sb("st_tmB", [64, CA], F32, esBw)
                    stg = sb("stgB", [64, CA], F32, esBw)

                    S.dma("pool", r_setup, lambda e: e.dma_start(out=wab[:], in_=wa_d.rearrange("h i j -> i h j")), writes=["wab"])
                    S.dma("pool", r_setup, lambda e: e.dma_start(out=wxb[:], in_=wx_d.rearrange("h i j -> i h j")), writes=["wxb"])
                    S.dma("sp", r_setup, lambda e: e.dma_start(out=st_tm[0:48, :], in_=st_lruc_d), writes=["st_tm_a"])
                    S.dma("sp", r_setup, lambda e: e.dma_start(out=st_tm[48:64, :], in_=st_h_d), writes=["st_tm_b"])
                    S.op("dve", lambda e: e.memset(bxe[0][:, 0:3], 0.0), writes=[("bxe", 0)])
                    S.op("dve", lambda e: e.memset(bxe[1][:, 0:3], 0.0), writes=[("bxe", 1)])
                    for j in range(8):
                        b = bank()
                        S.op("pe", lambda e, b=b, j=j: e.transpose(ps[b][0:128, 0:64], st_tm[0:64, j * 128:(j + 1) * 128],
                                                                   ident[0:64, 0:64]),
                             reads=["st_tm_a", "st_tm_b", "ident"], writes=[PK(b)])
                        S.op("act", act_copy(slruc_fm[:, j, :, :], ps[b][:, 0:48].rearrange("p (s k) -> p s k", k=3)),
                             reads=[PK(b)], writes=[("slruc", j)])
                        S.op("act", act_copy(slruh_fm[:, j, :, 0], ps[b][:, 48:64]),
                             reads=[PK(b)], writes=[("slruh", j)])
                        rel(b)

                    def lru_pe_in(j):
                        slot = j % 3
                        res = {"bx_pre": [], "bx_act": []}
                        for (c0, c1) in PRE_T:
                            b = bank()
                            for k in range(16):
                                S.op("pe", lambda e, b=b, k=k, c0=c0, c1=c1, slot=slot: e.matmul(
                                    ps[b][:, 0:c1 - c0], wbuf[:, slot, 0, k, :], xn_pre[:, k, c0:c1], start=(k == 0), stop=(k == 15)),
                                    reads=[("wB", slot, 0)] + xn_keys_pre(c0, c1, k), writes=[PK(b)])
                            res["bx_pre"].append((b, c0, c1))
                        for (a0, a1) in ACT_T:
                            b = bank()
                            for k in range(16):
                                S.op("pe", lambda e, b=b, k=k, a0=a0, a1=a1, slot=slot: e.matmul(
                                    ps[b][:, 0:a1 - a0], wbuf[:, slot, 0, k, :], xn_act[:, k, a0:a1], start=(k == 0), stop=(k == 15)),
                                    reads=[("wB", slot, 0)] + xn_keys_act(a0, a1, k), writes=[PK(b)])
                            res["bx_act"].append((b, a0, a1))
                        return res

                    def lru_pe_bg(j):
                        slot = j % 3
                        out = []
                        for (a0, a1) in ACT_T:
                            b = bank()
                            for k in range(16):
                                S.op("pe", lambda e, b=b, k=k, a0=a0, a1=a1, slot=slot: e.matmul(
                                    ps[b][:, 0:a1 - a0], wbuf[:, slot, 1, k, :], xn_act[:, k, a0:a1], start=(k == 0), stop=(k == 15)),
                                    reads=[("wB", slot, 1)] + xn_keys_act(a0, a1, k), writes=[PK(b)])
                            out.append((b, a0, a1))
                        return out

                    def lru_head(j, pe_in):
                        bx = bxe[j % 2]
                        bxs = bx[:, 2051:2227].rearrange("p (s t) -> p s t", t=11)
                        evk = [("bxe", j % 2)]
                        for n, (b, c0, c1) in enumerate(pe_in["bx_pre"]):
                            key = ("bxe_p", j % 2, n)
                            S.op("act", act_copy(bx[:, 3 + c0:3 + c1], ps[b][:, 0:c1 - c0]), reads=[PK(b)], writes=[key])
                            rel(b)
                            evk.append(key)
                        for n, (b, a0, a1) in enumerate(pe_in["bx_act"]):
                            key = ("bxe_a", j % 2, n)
                            if a1 <= 1056:
                                S.op("dve", dve_copy(bx[:, 3 + NPRE + a0:3 + NPRE + a1], ps[b][:, 0:a1 - a0]), reads=[PK(b)], writes=[key])
                            else:
                                S.op("dve", dve_copy(bxs[:, :, 3:11], ps[b][:, 0:128].rearrange("p (s t) -> p s t", t=8)),
                                     reads=[PK(b)], writes=[key])
                            rel(b)
                            evk.append(key)
                        S.op("dve", dve_copy(bxs[:, :, 0:3], slruc_fm[:, j, :, :]), reads=[("slruc", j)], writes=[("bxe_s", j % 2)])
                        evk.append(("bxe_s", j % 2))
                        S.op("act", act_copy(lruc_fm[:, j, 0:3], bx[:, 2048:2051]), reads=evk, writes=[("lruc_fm", j, 0)])
                        S.op("act", act_copy(lruc_fm[:, j, 3:51].rearrange("p (s k) -> p s k", k=3), bxs[:, :, 8:11]),
                             reads=evk, writes=[("lruc_fm", j, 1)])
                        return evk

                    def lru_conv(j, evk):
                        xc = xcs_[j % 2]
                        xkp, xks = ("xc_p", j % 2), ("xc_s", j % 2)
                        bx = bxe[j % 2]
                        bxs = bx[:, 2051:2227].rearrange("p (s t) -> p s t", t=11)
                        wv = lambda k: vecs[:, V_RW + j * 4 + k:V_RW + j * 4 + k + 1]
                        bv = vecs[:, V_RB + j:V_RB + j + 1]
                        xcs = xc[:, 2048:2176].rearrange("p (s t) -> p s t", t=8)
                        S.op("dve", lambda e: e.tensor_scalar(xc[:, 0:2048], bx[:, 0:2048], wv(0), bv, ALU.mult, ALU.add),
                             reads=evk + ["vecs"], writes=[xkp])
                        S.op("dve", lambda e: e.tensor_scalar(xcs, bxs[:, :, 0:8], wv(0), bv, ALU.mult, ALU.add),
                             reads=evk + ["vecs"], writes=[xks])
                        for k in range(1, 4):
                            S.op("dve", lambda e, k=k: e.scalar_tensor_tensor(xc[:, 0:2048], bx[:, k:k + 2048], wv(k), xc[:, 0:2048], ALU.mult, ALU.add),
                                 reads=evk + [xkp], writes=[xkp])
                            S.op("dve", lambda e, k=k: e.scalar_tensor_tensor(xcs, bxs[:, :, k:k + 8], wv(k), xcs, ALU.mult, ALU.add),
                                 reads=evk + [xks], writes=[xks])
                        S.op("act", act_copy(xcb[:], xc[:]), reads=[xkp, xks], writes=["xcb"])

                    def lru_gates(j):
                        for n, (c0, c1) in enumerate(ALL_T):
                            b = bank()
                            S.op("pe", lambda e, b=b, c0=c0, c1=c1: e.matmul(ps[b][:, 0:c1 - c0], wab[:, j, :], xcb[:, c0:c1], start=True, stop=True),
                                 reads=["wab", "xcb"], writes=[PK(b)])
                            S.op("act", lambda e, b=b, c0=c0, c1=c1: e.activation(out=t1[:, c0:c1], in_=ps[b][:, 0:c1 - c0], func=AF.Sigmoid,
                                                                                  bias=vecs[:, V_BA + j:V_BA + j + 1]),
                                 reads=[PK(b), "vecs"], writes=[("t1", n), "a"])
                            rel(b)
                        for n, (c0, c1) in enumerate(ALL_T):
                            b = bank()
                            S.op("pe", lambda e, b=b, c0=c0, c1=c1: e.matmul(ps[b][:, 0:c1 - c0], wxb[:, j, :], xcb[:, c0:c1], start=True, stop=True),
                                 reads=["wxb", "xcb"], writes=[PK(b)])
                            S.op("act", lambda e, b=b, c0=c0, c1=c1: e.activation(out=t3[:, c0:c1], in_=ps[b][:, 0:c1 - c0], func=AF.Sigmoid,
                                                                                  bias=vecs[:, V_BX + j:V_BX + j + 1]),
                                 reads=[PK(b), "vecs"], writes=[("t3", n), "bt", "ixc"])
                            rel(b)

                    def lru_gelu(j, bgb):
                        gb_ = gbg[j % 2]
                        for n, (b, a0, a1) in enumerate(bgb):
                            S.op("act", lambda e, b=b, a0=a0, a1=a1: e.activation(out=gb_[:, a0:a1], in_=ps[b][:, 0:a1 - a0], func=AF.Gelu_apprx_tanh),
                                 reads=[PK(b)], writes=[("gbg", j % 2, n)])
                            rel(b)

                    def lru_tail_act1(j):
                        t1k = [("t1", n) for n in range(5)]
                        S.op("act", lambda e: e.activation(out=t1[:], in_=t1[:], func=AF.Exp, scale=cvec[:, j:j + 1]),
                             reads=t1k + ["cvec"], writes=["a"])
                        S.op("act", lambda e: e.activation(out=t2[:], in_=t1[:], func=AF.Square), reads=["a", "t2"], writes=["t2"])
                        S.op("act", lambda e: e.activation(out=t2[:], in_=t2[:], func=AF.Relu, scale=-1.0, bias=1.0), reads=["t2"], writes=["t2"])
                        S.op("act", lambda e: e.activation(out=t2[:], in_=t2[:], func=AF.Sqrt), reads=["t2"], writes=["t2"])

                    def lru_tail(j):
                        xc = xcs_[j % 2]
                        t3k = [("t3", n) for n in range(5)]
                        S.op("dve", lambda e: e.tensor_tensor(t3[:], t3[:], xc[:], ALU.mult), reads=t3k + [("xc_p", j % 2), ("xc_s", j % 2)], writes=["ixc"])
                        S.op("dve", dve_copy(small[:, 0:1], t3[:, 0:1]), reads=["ixc"], writes=["sm0"])
                        S.op("dve", dve_copy(small[:, 1:2], t3[:, 1024:1025]), reads=["ixc"], writes=["sm1"])
                        S.op("dve", lambda e: e.tensor_tensor(t3[:], t3[:], t2[:], ALU.mult), reads=["ixc", "t2", "sm0", "sm1"], writes=["bt"])
                        S.op("dve", lambda e: e.tensor_scalar(small[:, 2:3], t2[:, 0:1], flags[:, 2:3], flags[:, 1:2], ALU.mult, ALU.add),
                             reads=["t2", "flags"], writes=["sm2"])
                        S.op("dve", lambda e: e.tensor_tensor(t3[:, 0:1], small[:, 0:1], small[:, 2:3], ALU.mult),
                             reads=["sm0", "sm2", "bt"], writes=["bt"])
                        S.op("dve", lambda e: e.tensor_scalar(t3[:, 0:1024], t3[:, 0:1024], flags[:, 0:1], None, ALU.mult),
                             reads=["bt", "flags"], writes=["bt"])
                        S.op("dve", lambda e: e.tensor_scalar(small[:, 3:4], t2[:, 1024:1025], flags[:, 4:5], flags[:, 3:4], ALU.mult, ALU.add),
                             reads=["t2", "flags"], writes=["sm3"])
                        S.op("dve", lambda e: e.tensor_tensor(t3[:, 1024:1025], small[:, 1:2], small[:, 3:4], ALU.mult),
                             reads=["sm1", "sm3", "bt"], writes=["bt"])
                        av = t1[:, 2048:2176].rearrange("p (s t) -> p s t", t=8)[:, :, 0:1]
                        btv = t3[:, 2048:2176].rearrange("p (s t) -> p s t", t=8)[:, :, 0:1]
                        h0 = slruh_fm[:, j, :, :]
                        S.op("dve", lambda e: e.tensor_tensor(av, av, h0, ALU.mult), reads=["a", ("slruh", j)], writes=["a"])
                        S.op("dve", lambda e: e.tensor_tensor(btv, btv, av, ALU.add), reads=["a", "bt"], writes=["bt"])
                        S.op("dve", lambda e: e.memset(av, 0.0), reads=["bt"], writes=["a"])
                        S.op("dve", lambda e: e.tensor_tensor_scan(t2[:], t1[:], t3[:], 0.0, ALU.mult, ALU.add),
                             reads=["a", "bt", "t2"], writes=["t2"])
                        S.op("dve", dve_copy(hst_fm[:, j, 0:1], t2[:, 2047:2048]), reads=["t2"], writes=[("hst", j, 0)])
                        S.op("dve", dve_copy(hst_fm[:, j, 1:17], t2[:, 2048:2176].rearrange("p (s t) -> p s t", t=8)[:, :, 7]),
                             reads=["t2"], writes=[("hst", j, 1)])
                        S.op("dve", lambda e: e.tensor_tensor(catB[:, j, :], t2[:, NPRE:NALL], gbg[j % 2][:], ALU.mult),
                             reads=["t2"] + [("gbg", j % 2, n) for n in range(4)], writes=[("cat", 8 + j)])

                    evks = {}
                    evks[0] = lru_head(0, lru_pe_in(0))
                    lru_gelu(0, lru_pe_bg(0))
                    lru_conv(0, evks[0])
                    for j in range(8):
                        if j + 2 < 8:
                            loadB(j + 2)
                        if j + 1 < 8:
                            evks[j + 1] = lru_head(j + 1, lru_pe_in(j + 1))
                            lru_gelu(j + 1, lru_pe_bg(j + 1))
                        lru_gates(j)
                        lru_tail_act1(j)
                        if j + 1 < 8 and not DEBUG.get("on"):
                            lru_conv(j + 1, evks[j + 1])
                        lru_tail(j)
                        if j == 0 and DEBUG.get("on"):
                            allxn = [("xn", i, kq) for i in list(range(17)) + [70] for kq in range(4)]
                            dbg_dump16("xn_act0", xn_act[:, 0, :], NACT, allxn)
                            dbg_dump16("xn_act15", xn_act[:, 15, :], NACT, allxn)
                            dbg_dump16("xn_pre3", xn_pre[:, 3, :], NPRE, allxn)
                            dbg_dump16("w0", wbuf[:, 0, 0].rearrange("p k c -> p (k c)"), 2048, [("wB", 0, 0)])
                            dbg_dump("bxe0", bxe[0][:], [128, 2227], F32, [("bxe", 0), ("bxe_s", 0)] + [("bxe_p", 0, n) for n in range(2)] + [("bxe_a", 0, n) for n in range(4)])
                            dbg_dump("xc", xcs_[0][:], [128, NALL], F32, [("xc_p", 0), ("xc_s", 0)])
                            dbg_dump("a", t1[:], [128, NALL], F32, ["a"])
                            dbg_dump("h", t2[:], [128, NALL], F32, ["t2"])
                            dbg_dump("bt", t3[:], [128, NALL], F32, ["bt"])
                            dbg_dump("gbg", gbg[0][:], [128, NACT], F32, [("gbg", 0, n) for n in range(4)])
                            dbg_dump16("catB0", catB[:, 0, :], NACT, [("cat", 8)])
                            dbg_stop("head0")
                        if j + 1 < 8 and DEBUG.get("on"):
                            lru_conv(j + 1, evks[j + 1])

                    for j in range(8):
                        tr_out(lruc_fm[:, j, :], 51, stg[0:51, j * 128:(j + 1) * 128], [("lruc_fm", j, 0), ("lruc_fm", j, 1)], ("stgB", j))
                    kk = [("stgB", j) for j in range(8)]
                    S.dma("sp", r_out, lambda e: e.dma_start(out=p_lruc_d, in_=stg[0:3, :]), reads=kk, is_out=True)
                    S.dma("sp", r_out, lambda e: e.dma_start(out=s_lruc_d, in_=stg[3:51, :]), reads=kk, is_out=True)
                    for j in range(8):
                        tr_out(hst_fm[:, j, :], 17, st_tm[0:17, j * 128:(j + 1) * 128], [("hst", j, 0), ("hst", j, 1)], ("stgH", j))
                    kk = [("stgH", j) for j in range(8)]
                    S.dma("sp", r_out, lambda e: e.dma_start(out=p_h_d, in_=st_tm[0:1, :]), reads=kk + ["st_tm_a", "st_tm_b"], is_out=True)
                    S.dma("sp", r_out, lambda e: e.dma_start(out=s_h_d, in_=st_tm[1:17, :]), reads=kk + ["st_tm_a", "st_tm_b"], is_out=True)

            S.fence()
            with ExitStack() as esA12:
                catA = sb("catA", [128, 8, NACT], BF16, esA12)
                with ExitStack() as esA:
                    caA = sb("caA", [128, 8, NACT], F32, esA)
                    wbufA = sb("wbufA", [128, 3, 2, 16, 128], BF16, esA)
                    A1 = sb("A1", [128, NACT], F32, esA)
                    A2 = sb("A2", [128, NACT], F32, esA)
                    A3 = sb("A3", [128, NACT], F32, esA)
                    A3b = sb("A3b", [128, NACT], F32, esA)
                    UE = 1086 + 608
                    ubf = [sb(f"ubf{i}", [128, UE], BF16, esA) for i in range(2)]
                    diag = sb("diag", [128, 31, 128], BF16, esA)
                    sconf_fm = sb("sconf_fm", [128, 8, 38, 16], F32, esA)
                    pconf_fm = sb("pconf_fm", [128, 8, 30], F32, esA)
                    st_tmA = sb("st_tmA", [120, CA], F32, esA)
                    stgS = sb("stgS", [128, CA], F32, esA)

                    st_tmA2 = sb("st_tmA2", [120, CA], F32, esA)

                    def conf_state_in():
                        for q in range(4):
                            stb = st_tmA if q % 2 == 0 else st_tmA2
                            sk = "st_tmA" if q % 2 == 0 else "st_tmA2"
                            S.dma("sp", r_ld, lambda e, q=q, stb=stb: e.dma_start(out=stb[:, :], in_=st_conf_d[120 * q:120 * q + 120, :]),
                                  writes=[sk])
                            for j in range(8):
                                b = bank()
                                S.op("pe", lambda e, b=b, j=j, stb=stb: e.transpose(ps[b][0:128, 0:120], stb[0:120, j * 128:(j + 1) * 128],
                                                                           ident[0:120, 0:120]),
                                     reads=[sk, "ident"], writes=[PK(b)])
                                S.op("act", act_copy(sconf_fm[:, j, 0:30, 4 * q:4 * q + 4].rearrange("p k s -> p s k"), ps[b][:, 0:120].rearrange("p (s k) -> p s k", k=30)),
                                     reads=[PK(b)], writes=[("sconf_old", j, q)])
                                rel(b)

                    for i in range(2):
                        S.op("dve", lambda e, i=i: e.memset(ubf[i][:, 0:30], 0.0), writes=[("ubf", i)])

                    def loadA(jj):
                        load_w_pair(jj % 3, 128 * jj, 1024 + 128 * jj, wbufA, "wA")

                    def convA_pe_in(j, q):
                        slot = j % 3
                        lst = []
                        if True:
                            for (a0, a1) in ACT_T:
                                b = bank()
                                for k in range(16):
                                    S.op("pe", lambda e, b=b, k=k, a0=a0, a1=a1, slot=slot, q=q: e.matmul(
                                        ps[b][:, 0:a1 - a0], wbufA[:, slot, q, k, :], xn_act[:, k, a0:a1], start=(k == 0), stop=(k == 15)),
                                        reads=[("wA", slot, q)] + xn_keys_act(a0, a1, k), writes=[PK(b)])
                                lst.append((b, a0, a1))
                        return lst

                    def convA_mid1(j, vb, gb):
                        for k in range(31):
                            S.op("dve", lambda e, k=k: e.tensor_scalar(diag[:, k, :], identb[:], vecs[:, V_CW + j * 31 + k:V_CW + j * 31 + k + 1], None, ALU.mult),
                                 reads=["identb", "vecs"], writes=["diag"])
                        for n, (b, a0, a1) in enumerate(gb):
                            S.op("act", lambda e, b=b, a0=a0, a1=a1: e.activation(out=A1[:, a0:a1], in_=ps[b][:, 0:a1 - a0], func=AF.Sigmoid),
                                 reads=[PK(b)], writes=[("A1", n)])
                            rel(b)
                        for n, (b, a0, a1) in enumerate(vb):
                            S.op("dve", lambda e, b=b, a0=a0, a1=a1: e.tensor_tensor(A2[:, a0:a1], ps[b][:, 0:a1 - a0], A1[:, a0:a1], ALU.mult),
                                 reads=[PK(b), ("A1", n)], writes=[("A2", n)])
                            rel(b)

                    def convA_mid2(j):
                        u = ubf[j % 2]
                        ufk = [("A2", n) for n in range(4)]
                        S.op("act", act_copy(sconf_fm[:, j, 30:38, :], A2[:, 1056:1184].rearrange("p (s t) -> p t s", t=8)),
                             reads=ufk, writes=[("sconf_new", j)])
                        S.op("act", act_copy(u[:, 30:1086], A2[:, 0:1056]), reads=ufk, writes=[("ubf_p", j % 2)])
                        S.op("act", act_copy(u[:, 1086:UE], sconf_fm[:, j, :, :].rearrange("p t s -> p (t s)")),
                             reads=[("sconf_new", j)] + [("sconf_old", j, q) for q in range(4)], writes=[("ubf_s", j % 2)])
                        S.op("act", act_copy(pconf_fm[:, j, :], A2[:, 1026:1056]), reads=ufk, writes=[("pconf", j)])

                    CONV_T = [(0, 352), (352, 704), (704, 1056)]

                    def convA_pe_conv(j):
                        u = ubf[j % 2]
                        outb = []
                        rk = ["diag", ("ubf_p", j % 2), ("ubf_s", j % 2), ("ubf", j % 2)]
                        for (a0, a1) in CONV_T:
                            b = bank()
                            for k in range(31):
                                S.op("pe", lambda e, b=b, k=k, a0=a0, a1=a1: e.matmul(
                                    ps[b][:, 0:a1 - a0], diag[:, k, :], u[:, a0 + k:a1 + k], start=(k == 0), stop=(k == 30)),
                                    reads=rk, writes=[PK(b)])
                            outb.append((b, a0, a1))
                        b = bank()
                        for k in range(31):
                            S.op("pe", lambda e, b=b, k=k: e.matmul(
                                ps[b][:, 0:128], diag[:, k, :], u[:, 1086 + 16 * k:1086 + 16 * k + 128], start=(k == 0), stop=(k == 30)),
                                reads=rk, writes=[PK(b)])
                        outb.append((b, 1056, 1184))
                        return outb

                    def convA_tail(j, outb):
                        for n, (b, a0, a1) in enumerate(outb):
                            if n < 3:
                                dst, src = caA[:, j, a0:a1], ps[b][:, 0:a1 - a0]
                            else:
                                dst = caA[:, j, a0:a1].rearrange("p (s t) -> p t s", t=8)
                                src = ps[b][:, 0:128].rearrange("p (t s) -> p t s", s=16)
                            S.op("act", lambda e, dst=dst, src=src: e.activation(out=dst, in_=src, func=AF.Identity,
                                                                                  bias=vecs[:, V_CB + j:V_CB + j + 1]),
                                 reads=[PK(b), "vecs"], writes=[("caA", j, n)])
                            rel(b)
                        cak = [("caA", j, n) for n in range(4)]
                        a1k = [("A1", n) for n in range(4)]
                        if j == 0:
                            S.op("dve", dve_copy(A3b[:], caA[:, 0, :]), reads=cak, writes=["A3b"])
                            S.op("act", lambda e: e.activation(out=A3[:], in_=caA[:, 0, :], func=AF.Square), reads=cak, writes=["A3"])
                        else:
                            S.op("act", lambda e: e.activation(out=A1[:], in_=caA[:, j, :], func=AF.Square), reads=cak, writes=a1k)
                            S.op("dve", lambda e: e.tensor_tensor(A3b[:], A3b[:], caA[:, j, :], ALU.add), reads=cak + ["A3b"], writes=["A3b"])
                            S.op("dve", lambda e: e.tensor_tensor(A3[:], A3[:], A1[:], ALU.add), reads=a1k + ["A3"], writes=["A3"])

                    loadA(0)
                    loadA(1)
                    gbs = {0: convA_pe_in(0, 1)}
                    for j in range(8):
                        if j + 2 < 8:
                            loadA(j + 2)
                        vb = convA_pe_in(j, 0)
                        convA_mid1(j, vb, gbs[j])
                        if j + 1 < 8:
                            gbs[j + 1] = convA_pe_in(j + 1, 1)
                        if j == 0:
                            conf_state_in()
                        convA_mid2(j)
                        convA_tail(j, convA_pe_conv(j))

                    for j in range(8):
                        tr_out(pconf_fm[:, j, :], 30, st_tmA[0:30, j * 128:(j + 1) * 128], [("pconf", j)], "st_tmA")
                    S.dma("sp", r_out, lambda e: e.dma_start(out=p_conf_d, in_=st_tmA[0:30, :]), reads=["st_tmA"], is_out=True)
                    s_conf_v = s_conf_d.rearrange("(s k) c -> k s c", k=30)
                    for bi in range(4):
                        nk = 8 if bi < 3 else 6
                        for j in range(8):
                            tr_out(sconf_fm[:, j, 8 + 8 * bi:8 + 8 * bi + nk, :].rearrange("p t s -> p (t s)"), nk * 16,
                                   stgS[0:nk * 16, j * 128:(j + 1) * 128],
                                   [("sconf_new", j)] + [("sconf_old", j, qq) for qq in range(4)], "stgS")
                        for kl in range(nk):
                            S.dma("sp", r_out, lambda e, bi=bi, kl=kl: e.dma_start(out=s_conf_v[8 * bi + kl], in_=stgS[16 * kl:16 * kl + 16, :]),
                                  reads=["stgS"], is_out=True)

                    sumb, sq_banks = [], []
                    for n, (a0, a1) in enumerate(ACT_T):
                        b = bank()
                        S.op("pe", lambda e, b=b, a0=a0, a1=a1: e.matmul(ps[b][:, 0:a1 - a0], ones[:], A3b[:, a0:a1], start=True, stop=True),
                             reads=["ones", "A3b"], writes=[PK(b)])
                        sumb.append(b)
                        b = bank()
                        S.op("pe", lambda e, b=b, a0=a0, a1=a1: e.matmul(ps[b][:, 0:a1 - a0], ones[:], A3[:, a0:a1], start=True, stop=True),
                             reads=["ones", "A3"], writes=[PK(b)])
                        sq_banks.append(b)
                    for n, (a0, a1) in enumerate(ACT_T):
                        b = sumb[n]
                        b2 = sq_banks[n]
                        S.op("act", lambda e, b=b, a0=a0, a1=a1: e.activation(out=A1[:, a0:a1], in_=ps[b][:, 0:a1 - a0], func=AF.Copy, scale=1.0 / CA),
                             reads=[PK(b)], writes=[("A1", n)])
                        S.op("dve", lambda e, a0=a0, a1=a1: e.tensor_tensor(A3[:, a0:a1], A1[:, a0:a1], A1[:, a0:a1], ALU.mult),
                             reads=[("A1", n)], writes=[("A3m", n), "A3"])
                        S.op("dve", lambda e, b2=b2, a0=a0, a1=a1: e.scalar_tensor_tensor(A2[:, a0:a1], ps[b2][:, 0:a1 - a0], 1.0 / CA, A3[:, a0:a1], ALU.mult, ALU.subtract),
                             reads=[PK(b2), ("A3m", n), ("A2", n)], writes=[("A2", n)])
                        rel(b, b2)
                        S.op("act", lambda e, a0=a0, a1=a1: e.activation(out=A2[:, a0:a1], in_=A2[:, a0:a1], func=AF.Sqrt, bias=EPS),
                             reads=[("A2", n)], writes=[("A2", n)])
                        S.op("dve", lambda e, a0=a0, a1=a1: e.reciprocal(A2[:, a0:a1], A2[:, a0:a1]),
                             reads=[("A2", n)], writes=[("A2", n)])
                    mk = [("A1", n) for n in range(4)]
                    rk2 = [("A2", n) for n in range(4)]
                    for j in range(8):
                        zb = A3 if j % 2 == 0 else A3b
                        zk = "A3" if j % 2 == 0 else "A3b"
                        S.op("dve", lambda e, j=j, zb=zb: e.tensor_tensor(zb[:], caA[:, j, :], A1[:], ALU.subtract),
                             reads=mk + [("caA", j, n) for n in range(4)] + [("A3m", n) for n in range(4)] + [zk], writes=[zk])
                        S.op("dve", lambda e, zb=zb: e.tensor_tensor(zb[:], zb[:], A2[:], ALU.mult), reads=rk2 + [zk], writes=[zk])
                        S.op("act", lambda e, j=j, zb=zb: e.activation(out=catA[:, j, :], in_=zb[:], func=AF.Silu,
                                                                scale=vecs[:, V_LG + j:V_LG + j + 1], bias=vecs[:, V_LB + j:V_LB + j + 1]),
                             reads=[zk, "vecs"], writes=[("cat", j)])

                S.fence()
                xn2 = xn_act
                with ExitStack() as es2:
                    wo = sb("wo", [128, 16, D], BF16, es2)
                    grow1 = sb("grow1", [128, D], F32, es2)
                    grow2 = sb("grow2", [128, D], F32, es2)
                    xt2 = [sb(f"xt2_{i}", [128, D], F32, es2) for i in range(2)]
                    ht = sb("ht", [128, D], F32, es2)
                    xs2 = sb("xs2", [128, D], BF16, es2)
                    ss2 = sb("ss2", [128, 8], F32, es2)
                    S.dma("sp", r_setup, lambda e: e.dma_start(out=grow1[:], in_=grows_d[1:2, :].partition_broadcast(128)), writes=["grow1"])
                    S.dma("sp", r_setup, lambda e: e.dma_start(out=grow2[:], in_=grows_d[2:3, :].partition_broadcast(128)), writes=["grow2"])
                    for fg in range(4):
                        for kh in range(2):
                            S.dma("pool", r_w, lambda e, fg=fg, kh=kh: e.dma_start(
                                out=wo[:, 8 * kh:8 * kh + 8, fg * 512:(fg + 1) * 512],
                                in_=w_out[1024 * kh:1024 * kh + 1024, fg * 512:(fg + 1) * 512].rearrange("(k p) c -> p k c", p=128)),
                                writes=[("wo", fg, kh)])
                    tiles = [(NPRE, 32, 0, None)] + [(1024 + 128 * t, 128, 32 + 128 * t, 128 * t) for t in range(9)]
                    xn_all_keys = [("xn", i, kq) for i in list(range(8, 17)) + [70] for kq in range(4)]
                    junk2 = sb("junk2", [128, D], BF16, es2)

                    def p2_mm(ti):
                        r0, M, a0, hr = tiles[ti]
                        sl = ti % 2
                        S.dma("sp", r_x, lambda e: e.dma_start(out=xt2[sl][0:M, :], in_=xall[r0:r0 + M, :]), writes=[("xt2", sl)])
                        mb = []
                        for fg in range(4):
                            b = bank_at((ti % 2) * 4 + fg)
                            for k in range(16):
                                S.op("pe", lambda e, b=b, k=k, fg=fg: e.matmul(
                                    ps[b][0:M, :], cat(k)[:, a0:a0 + M], wo[:, k, fg * 512:(fg + 1) * 512], start=(k == 0), stop=(k == 15)),
                                    reads=[("cat", k), ("wo", fg, k // 8)], writes=[PK(b)])
                            mb.append(b)
                        return mb

                    def p2_norm(ti, mb):
                        r0, M, a0, hr = tiles[ti]
                        sl = ti % 2
                        for fg, b in enumerate(mb):
                            S.op("act", lambda e, b=b, fg=fg: e.activation(out=junk2[0:M, fg * 512:(fg + 1) * 512], in_=ps[b][0:M, :], func=AF.Square,
                                                                            accum_out=ss2[0:M, fg:fg + 1]),
                                 reads=[PK(b)], writes=[("ss2a", fg)])
                        S.op("dve", lambda e: e.tensor_reduce(ss2[0:M, 4:5], ss2[0:M, 0:4], mybir.AxisListType.X, ALU.add),
                             reads=[("ss2a", fg) for fg in range(4)], writes=["ss2b"])
                        S.op("act", lambda e: e.activation(out=ss2[0:M, 5:6], in_=ss2[0:M, 4:5], func=AF.Sqrt, scale=1.0 / D, bias=EPS),
                             reads=["ss2b"], writes=["ss2c"])
                        S.op("dve", lambda e: e.reciprocal(ss2[0:M, 5:6], ss2[0:M, 5:6]), reads=["ss2c"], writes=["ss2d"])
                        for fg, b in enumerate(mb):
                            S.op("dve", lambda e, b=b, fg=fg: e.scalar_tensor_tensor(ht[0:M, fg * 512:(fg + 1) * 512], ps[b][0:M, :], ss2[0:M, 5:6],
                                                                                  grow1[0:M, fg * 512:(fg + 1) * 512], ALU.mult, ALU.mult),
                                 reads=[PK(b), "ss2d", "grow1"], writes=[("ht0", fg), "ht"])
                            rel(b)
                        S.op("dve", lambda e: e.tensor_tensor(ht[0:M, :], ht[0:M, :], xt2[sl][0:M, :], ALU.add),
                             reads=[("ht0", fg) for fg in range(4)] + [("xt2", sl)], writes=["ht"])
                        if hr is not None:
                            S.dma("sp", r_st, lambda e: e.dma_start(out=h_scr[hr:hr + 128, :], in_=ht[:, :]), reads=["ht"], writes=[("h_scr", hr)])

                    def p2_xn2(ti):
                        r0, M, a0, hr = tiles[ti]
                        S.op("act", lambda e: e.activation(out=junk2[0:M, :], in_=ht[0:M, :], func=AF.Square, accum_out=ss2[0:M, 6:7]),
                             reads=["ht"], writes=["ss2e"])
                        S.op("act", lambda e: e.activation(out=ss2[0:M, 7:8], in_=ss2[0:M, 6:7], func=AF.Sqrt, scale=1.0 / D, bias=EPS),
                             reads=["ss2e"], writes=["ss2f"])
                        S.op("dve", lambda e: e.reciprocal(ss2[0:M, 7:8], ss2[0:M, 7:8]), reads=["ss2f"], writes=["ss2g"])
                        S.op("dve", lambda e: e.scalar_tensor_tensor(xs2[0:M, :], ht[0:M, :], ss2[0:M, 7:8], grow2[0:M, :], ALU.mult, ALU.mult),
                             reads=["ht", "ss2g", "grow2"], writes=["xs2"])
                        for kq in range(4):
                            b = bank_at((ti % 2) * 4 + kq)
                            pb = ps[b][:].bitcast(BF16)
                            for jq in range(4):
                                k = kq * 4 + jq
                                S.op("pe", lambda e, k=k, jq=jq, pb=pb: e.transpose(pb[:, jq * 128:jq * 128 + M], xs2[0:M, k * 128:(k + 1) * 128], identb[0:M, 0:M]),
                                     reads=["xs2", "identb"], writes=[PK(b)])
                            pv = pb[:, 0:512].rearrange("p (j t) -> p j t", t=128)
                            eng = "act" if kq % 2 == 0 else "dve"
                            S.op(eng, evac(eng, xn2[:, kq * 4:kq * 4 + 4, a0:a0 + M], pv[:, :, 0:M]),
                                 reads=[PK(b)], writes=[("xn2", ti, kq)] + (xn_all_keys if ti == 0 and kq < 2 else []))
                            rel(b)

                    mbs = {0: p2_mm(0)}
                    for ti in range(len(tiles)):
                        p2_norm(ti, mbs[ti])
                        if ti + 1 < len(tiles):
                            mbs[ti + 1] = p2_mm(ti + 1)
                        p2_xn2(ti)

        def xn2_keys(a0, a1, k):
            keys = []
            if a0 < 32:
                keys.append(("xn2", 0, k // 4))
            lo = max(a0, 32)
            if a1 > 32:
                for t in range((lo - 32) // 128, (a1 - 1 - 32) // 128 + 1):
                    keys.append(("xn2", 1 + t, k // 4))
            return keys

        S.fence()
        with ExitStack() as es3:
            ffn_fm = sb("ffn_fm", [128, 96, 34], F32, es3)
            hmid = sb("hmid", [128, 24, 1152], BF16, es3)
            wpool = sb("wpool", [128, 6, 4096], BF16, es3)
            EXT = 1026 + 160
            ge = [sb(f"ge{i}", [128, EXT], F32, es3) for i in range(2)]
            ve = [sb(f"ve{i}", [128, EXT], F32, es3) for i in range(2)]
            og = sb("og", [128, 1152], F32, es3)
            ov = sb("ov", [128, 1152], F32, es3)
            dstg = [sb(f"dstg{i}", [128, 512], F32, es3) for i in range(3)]
            prv = [sb(f"prv{i}", [128, 512], F32, es3) for i in range(2)]
            fsts = [sb(f"fst{i}", [34, 1024], F32, es3) for i in range(2)]

            UP_T = [(30, 542), (542, 1054), (1054, 1184)]
            upstate = {"n": 0}

            def load_up(jg):
                unit = upstate["n"] % 6
                upstate["n"] += 1
                for q, c in enumerate((128 * jg, DFF + 128 * jg)):
                    S.dma("pool", r_w, lambda e, q=q, c=c, unit=unit: e.dma_start(
                        out=wpool[:, unit, q * 2048:(q + 1) * 2048].rearrange("p (k c) -> p k c", c=128),
                        in_=w_up[:, c:c + 128].rearrange("(k p) c -> p k c", p=128)),
                        writes=[("wp", unit, q)])
                return unit

            def up_pe(jg, unit):
                res = []
                for q in range(2):
                    wv = wpool[:, unit, q * 2048:(q + 1) * 2048].rearrange("p (k c) -> p k c", c=128)
                    lst = []
                    for (a0, a1) in UP_T:
                        b = bank()
                        for k in range(16):
                            S.op("pe", lambda e, b=b, k=k, a0=a0, a1=a1, wv=wv: e.matmul(
                                ps[b][:, 0:a1 - a0], wv[:, k, :], xn2[:, k, a0:a1], start=(k == 0), stop=(k == 15)),
                                reads=[("wp", unit, q)] + xn2_keys(a0, a1, k), writes=[PK(b)])
                        lst.append(b)
                    res.append(lst)
                return res

            def up_evac(jg, jj, banks):
                for q, (exts, ekey) in enumerate(((ge, "ge"), (ve, "ve"))):
                    ext = exts[jj % 2]
                    ch = jg + 48 * q
                    bA, bB, bC = banks[q]
                    exs = ext[:, 1026:EXT].rearrange("p (s t) -> p s t", t=10)
                    eng = "act" if q == 0 else "dve"
                    sfx = (ekey, jj % 2)
                    S.op(eng, evac(eng, ext[:, 0:512], ps[bA][:, 0:512]), reads=[PK(bA)], writes=[sfx + (0,)])
                    S.op(eng, evac(eng, ext[:, 512:1024], ps[bB][:, 0:512]), reads=[PK(bB)], writes=[sfx + (1,)])
                    S.op("act", act_copy(ext[:, 1024:1026], ps[bC][:, 0:2]), reads=[PK(bC)], writes=[sfx + (2,)])
                    S.op("act", act_copy(exs[:, :, 2:10], ps[bC][:, 2:130].rearrange("p (s t) -> p s t", t=8)), reads=[PK(bC)], writes=[sfx + (3,)])
                    rel(bA, bB, bC)
                    S.op("act", act_copy(exs[:, :, 0:2], ffn_fm[:, ch, 0:32].rearrange("p (s k) -> p s k", k=2)), reads=[("ffn_fm", ch)], writes=[sfx + (4,)])
                    S.op("dve", lambda e, ext=ext: e.tensor_scalar(ext[:, 0:2], ext[:, 0:2], flags[:, 0:1], None, ALU.mult),
                         reads=[sfx + (0,), "flags"], writes=[sfx + (0,)])
                    ek = [sfx + (n,) for n in range(5)]
                    S.op("act", act_copy(ffn_fm[:, ch, 0:32].rearrange("p (s k) -> p s k", k=2), exs[:, :, 8:10]), reads=ek, writes=[("ffn_new", ch)])
                    S.op("act", act_copy(ffn_fm[:, ch, 32:34], ext[:, 1024:1026]), reads=ek, writes=[("ffn_newp", ch)])

            def up_tail(jg, jj):
                for q, (exts, o, ekey, okey) in enumerate(((ge, og, "ge", "og"), (ve, ov, "ve", "ov"))):
                    ext = exts[jj % 2]
                    ch = jg + 48 * q
                    exs = ext[:, 1026:EXT].rearrange("p (s t) -> p s t", t=10)
                    ek = [(ekey, jj % 2, n) for n in range(5)]
                    wv = lambda k, ch=ch: vecs[:, V_FW + ch * 3 + k:V_FW + ch * 3 + k + 1]
                    bv = vecs[:, V_FB + ch:V_FB + ch + 1]
                    os_ = o[:, 1024:1152].rearrange("p (s t) -> p s t", t=8)
                    S.op("dve", lambda e, ext=ext, o=o, wv=wv, bv=bv: e.tensor_scalar(o[:, 0:1024], ext[:, 0:1024], wv(0), bv, ALU.mult, ALU.add),
                         reads=ek + ["vecs", okey], writes=[(okey, 0)])
                    S.op("dve", lambda e, exs=exs, os_=os_, wv=wv, bv=bv: e.tensor_scalar(os_, exs[:, :, 0:8], wv(0), bv, ALU.mult, ALU.add),
                         reads=ek + ["vecs", okey], writes=[(okey, 1)])
                    for k in (1, 2):
                        S.op("dve", lambda e, ext=ext, o=o, wv=wv, k=k: e.scalar_tensor_tensor(o[:, 0:1024], ext[:, k:k + 1024], wv(k), o[:, 0:1024], ALU.mult, ALU.add),
                             reads=ek + [(okey, 0)], writes=[(okey, 0)])
                        S.op("dve", lambda e, exs=exs, os_=os_, wv=wv, k=k: e.scalar_tensor_tensor(os_, exs[:, :, k:k + 8], wv(k), os_, ALU.mult, ALU.add),
                             reads=ek + [(okey, 1)], writes=[(okey, 1)])
                S.op("act", lambda e: e.activation(out=og[:], in_=og[:], func=AF.Gelu_apprx_tanh), reads=[("og", 0), ("og", 1)], writes=["og"])
                S.op("dve", lambda e, jj=jj: e.tensor_tensor(hmid[:, jj, :], og[:], ov[:], ALU.mult),
                     reads=["og", ("ov", 0), ("ov", 1)], writes=[("hmid", jj), "ov"])

            def load_down(g, fg, half):
                units = (0, 1, 2) if half == 0 else (3, 4, 5)
                for u3, unit in enumerate(units):
                    for hh in range(2):
                        kk0 = u3 * 8 + hh * 4
                        r0 = g * 3072 + kk0 * 128
                        S.dma("pool", r_w, lambda e, unit=unit, hh=hh, r0=r0, fg=fg: e.dma_start(
                            out=wpool[:, unit, hh * 2048:(hh + 1) * 2048].rearrange("p (k c) -> p k c", c=512),
                            in_=w_down[r0:r0 + 512, fg * 512:(fg + 1) * 512].rearrange("(k p) c -> p k c", p=128)),
                            writes=[("wp", unit, hh)])
                return units

            def down_phase(g):
                cnt = 0
                units = {0: load_down(g, 0, 0), 1: load_down(g, 1, 1)}
                for fg in range(4):
                    un = units[fg]
                    for tt in range(9):
                        b = bank()
                        for kk in range(24):
                            unit = un[kk // 8]
                            hh = (kk % 8) // 4
                            wv = wpool[:, unit, hh * 2048:(hh + 1) * 2048].rearrange("p (k c) -> p k c", c=512)
                            S.op("pe", lambda e, b=b, kk=kk, wv=wv, tt=tt: e.matmul(
                                ps[b][:, :], hmid[:, kk, tt * 128:(tt + 1) * 128], wv[:, kk % 4, :], start=(kk == 0), stop=(kk == 23)),
                                reads=[("wp", unit, hh), ("hmid", kk)], writes=[PK(b)])
                        ds = dstg[cnt % 3]
                        dkey = ("dstg", cnt % 3)
                        fkey = ("f_scr", tt, fg)
                        dst = f_scr[tt * 128:(tt + 1) * 128, fg * 512:(fg + 1) * 512]
                        if g == 0:
                            S.op("act", act_copy(ds[:], ps[b][:, :]), reads=[PK(b)], writes=[dkey])
                            rel(b)
                        else:
                            pv = prv[cnt % 2]
                            pkey = ("prv", cnt % 2)
                            S.dma("sp", r_ld, lambda e, pv=pv, dst=dst: e.dma_start(out=pv[:], in_=dst), reads=[fkey], writes=[pkey])
                            S.op("dve", lambda e, ds=ds, pv=pv, b=b: e.tensor_tensor(ds[:], ps[b][:, :], pv[:], ALU.add),
                                 reads=[PK(b), pkey], writes=[dkey])
                            rel(b)
                        S.dma("sp", r_st, lambda e, ds=ds, dst=dst: e.dma_start(out=dst, in_=ds[:]), reads=[dkey], writes=[fkey])
                        cnt += 1
                    if fg + 2 < 4:
                        units[fg + 2] = load_down(g, fg + 2, fg % 2)

            def ffn_state_in(qs):
                for qi, q in enumerate(qs):
                    fst = fsts[qi % 2]
                    fk = ("fst", qi % 2)
                    S.dma("sp", r_ld, lambda e, q=q, fst=fst: e.dma_start(out=fst[0:32, :], in_=st_ffn_d[:, 1024 * q:1024 * (q + 1)]), writes=[fk])
                    for jj in range(8):
                        ch = q * 8 + jj
                        b = bank()
                        S.op("pe", lambda e, b=b, jj=jj, fst=fst: e.transpose(ps[b][0:128, 0:32], fst[0:32, jj * 128:(jj + 1) * 128], ident[0:32, 0:32]),
                             reads=[fk, "ident"], writes=[PK(b)])
                        S.op("act", act_copy(ffn_fm[:, ch, 0:32], ps[b][:, 0:32]),
                             reads=[PK(b)], writes=[("ffn_fm", ch)])
                        rel(b)

            for g in range(2):
                pend = {}
                pend[0] = load_up(24 * g)
                pend[1] = load_up(24 * g + 1)
                pend[2] = load_up(24 * g + 2)
                banks = {0: up_pe(24 * g, pend[0])}
                if g == 0:
                    ffn_state_in((0, 6))
                for jj in range(24):
                    if g == 0 and jj in (1, 3, 5, 7, 9):
                        qq = (jj + 1) // 2
                        ffn_state_in((qq, 6 + qq))
                    if jj + 3 < 24:
                        pend[jj + 3] = load_up(24 * g + jj + 3)
                    up_evac(24 * g + jj, jj, banks[jj])
                    if jj + 1 < 24:
                        banks[jj + 1] = up_pe(24 * g + jj + 1, pend[jj + 1])
                    up_tail(24 * g + jj, jj)
                if g == 1:
                    for q in range(12):
                        fst = fsts[q % 2]
                        fk = ("fst", q % 2)
                        for jj in range(8):
                            ch = q * 8 + jj
                            tr_out(ffn_fm[:, ch, :], 34, fst[0:34, jj * 128:(jj + 1) * 128], [("ffn_new", ch), ("ffn_newp", ch)], fk)
                        S.dma("sp", r_out, lambda e, q=q, fst=fst: e.dma_start(out=s_ffn_d[:, 1024 * q:1024 * (q + 1)], in_=fst[0:32, :]), reads=[fk], is_out=True)
                        S.dma("sp", r_out, lambda e, q=q, fst=fst: e.dma_start(out=p_ffn_d[:, 1024 * q:1024 * (q + 1)], in_=fst[32:34, :]), reads=[fk], is_out=True)
                down_phase(g)

        S.fence()
        with ExitStack() as es5:
            grow3 = sb("grow3", [128, D], F32, es5)
            ft = [sb(f"ft{i}", [128, D], F32, es5) for i in range(4)]
            hh_ = [sb(f"hh{i}", [128, D], F32, es5) for i in range(4)]
            junk = sb("junk5", [128, D], BF16, es5)
            ss5 = sb("ss5", [128, 8], F32, es5)
            S.dma("sp", r_setup, lambda e: e.dma_start(out=grow3[:], in_=grows_d[3:4, :].partition_broadcast(128)), writes=["grow3"])
            def p5_load(tt):
                sl = tt % 4
                S.dma("sp", r_x, lambda e: e.dma_start(out=ft[sl][:], in_=f_scr[tt * 128:(tt + 1) * 128, :]),
                      reads=[("f_scr", tt, fg) for fg in range(4)], writes=[("ft", sl)])
                S.dma("sp", r_ld, lambda e: e.dma_start(out=hh_[sl][:], in_=h_scr[tt * 128:(tt + 1) * 128, :]),
                      reads=[("h_scr", 128 * tt)], writes=[("hh", sl)])

            def p5_comp(tt):
                sl = tt % 4
                c0 = 2 * sl
                S.op("act", lambda e: e.activation(out=junk[:], in_=ft[sl][:], func=AF.Square, accum_out=ss5[:, c0:c0 + 1]),
                     reads=[("ft", sl)], writes=[("ss5a", sl)])
                S.op("act", lambda e: e.activation(out=ss5[:, c0 + 1:c0 + 2], in_=ss5[:, c0:c0 + 1], func=AF.Sqrt, scale=1.0 / D, bias=EPS),
                     reads=[("ss5a", sl)], writes=[("ss5b", sl)])
                S.op("dve", lambda e: e.reciprocal(ss5[:, c0 + 1:c0 + 2], ss5[:, c0 + 1:c0 + 2]), reads=[("ss5b", sl)], writes=[("ss5c", sl)])
                S.op("dve", lambda e: e.scalar_tensor_tensor(ft[sl][:], ft[sl][:], ss5[:, c0 + 1:c0 + 2], grow3[:], ALU.mult, ALU.mult),
                     reads=[("ft", sl), ("ss5c", sl), "grow3"], writes=[("ft", sl)])
                S.op("dve", lambda e: e.tensor_tensor(ft[sl][:], ft[sl][:], hh_[sl][:], ALU.add),
                     reads=[("ft", sl), ("hh", sl)], writes=[("ft", sl)])
                S.dma("sp", r_out, lambda e: e.dma_start(out=y_d[tt * 128:(tt + 1) * 128, :], in_=ft[sl][:]),
                      reads=[("ft", sl)], is_out=True)

            for tt in range(3):
                p5_load(tt)
            for tt in range(9):
                if tt + 3 < 9:
                    p5_load(tt + 3)
                p5_comp(tt)

      except _Stop:
        pass
      S.emit()
    return nc


_CACHE = {}


def _fm(v):
    v = np.asarray(v, dtype=np.float32).reshape(-1, 128)
    return np.ascontiguousarray(v.T)


def kernel(x_prompt, x_sample, state_conf_conv, state_lru_conv, state_lru_h, state_ffn_conv,
           g_mix_pre, g_mix_post, w_in, conf_dw_w, conf_dw_b, conf_ln_g, conf_ln_b,
           lru_conv_w, lru_conv_b, lru_wa, lru_ba, lru_wx, lru_bx, lru_lambda, w_out,
           g_ffn_pre, g_ffn_post, w_up, ffn_dw_w, ffn_dw_b, w_down):
    f32 = np.float32
    x_prompt = np.asarray(x_prompt, f32)
    x_sample = np.asarray(x_sample, f32)
    vecs = np.zeros((128, V_N), f32)
    cw = np.asarray(conf_dw_w, f32)[0].reshape(31, 8, 128)
    vecs[:, V_CW:V_CW + 248] = cw.transpose(2, 1, 0).reshape(128, 248)
    vecs[:, V_CB:V_CB + 8] = _fm(conf_dw_b[0])
    vecs[:, V_LG:V_LG + 8] = _fm(conf_ln_g[0])
    vecs[:, V_LB:V_LB + 8] = _fm(conf_ln_b[0])
    rw = np.asarray(lru_conv_w, f32)[0].reshape(4, 8, 128)
    vecs[:, V_RW:V_RW + 32] = rw.transpose(2, 1, 0).reshape(128, 32)
    vecs[:, V_RB:V_RB + 8] = _fm(lru_conv_b[0])
    vecs[:, V_BA:V_BA + 8] = _fm(lru_ba[0])
    vecs[:, V_BX:V_BX + 8] = _fm(lru_bx[0])
    vecs[:, V_LAM:V_LAM + 8] = _fm(lru_lambda[0])
    fw = np.asarray(ffn_dw_w, f32)[0].reshape(3, 96, 128)
    vecs[:, V_FW:V_FW + 288] = fw.transpose(2, 1, 0).reshape(128, 288)
    vecs[:, V_FB:V_FB + 96] = _fm(ffn_dw_b[0])
    grows = np.ascontiguousarray(np.stack([np.asarray(g_mix_pre, f32)[0], np.asarray(g_mix_post, f32)[0],
                                           np.asarray(g_ffn_pre, f32)[0], np.asarray(g_ffn_post, f32)[0]]))
    ident = np.eye(128, dtype=f32)
    shared = dict(
        w_in=np.ascontiguousarray(np.asarray(w_in, f32)[0]), w_out=np.ascontiguousarray(np.asarray(w_out, f32)[0]),
        w_up=np.ascontiguousarray(np.asarray(w_up, f32)[0]), w_down=np.ascontiguousarray(np.asarray(w_down, f32)[0]),
        wa=np.ascontiguousarray(np.asarray(lru_wa, f32)[0]), wx=np.ascontiguousarray(np.asarray(lru_wx, f32)[0]),
        vecs=vecs, grows=grows, ident=ident)
    in_maps = []
    for c in range(NCORES):
        s, hf = c // 2, c % 2
        xall = np.zeros((NALL, D), f32)
        if hf == 1:
            xall[0:1024] = x_prompt[s, 0:1024]
        xall[1024:2048] = x_prompt[s, hf * 1024:(hf + 1) * 1024]
        xall[2048:2176] = x_sample[16 * c:16 * c + 16].reshape(128, D)
        fl = np.zeros((128, 8), f32)
        fl[:, 0] = float(hf)
        fl[:, 1] = float(hf)
        fl[:, 2] = 1.0 - float(hf)
        fl[:, 3] = 1.0 - float(hf)
        fl[:, 4] = float(hf)
        m = dict(shared)
        m.update(
            xall=xall, flags=fl,
            st_conf=np.ascontiguousarray(np.asarray(state_conf_conv, f32)[0, 16 * c:16 * c + 16].reshape(480, CA)),
            st_lruc=np.ascontiguousarray(np.asarray(state_lru_conv, f32)[0, 16 * c:16 * c + 16].reshape(48, CA)),
            st_h=np.ascontiguousarray(np.asarray(state_lru_h, f32)[0, 16 * c:16 * c + 16].reshape(16, CA)),
            st_ffn=np.ascontiguousarray(np.asarray(state_ffn_conv, f32)[0, 16 * c:16 * c + 16].reshape(32, 2 * DFF)),
        )
        in_maps.append(m)
    if "nc" not in _CACHE:
        _CACHE["nc"] = build_program()
    nc = _CACHE["nc"]
    res = run_bass_kernel_spmd(nc, in_maps, core_ids=list(range(NCORES)))
    R = res.results
    y_p = np.zeros((4, 2048, D), f32)
    y_s = np.zeros((128, 8, D), f32)
    p_conf = np.zeros((1, 4, 30, CA), f32)
    p_lruc = np.zeros((1, 4, 3, CA), f32)
    p_h = np.zeros((1, 4, CA), f32)
    p_ffn = np.zeros((1, 4, 2, 2 * DFF), f32)
    s_conf = np.zeros((1, 128, 30, CA), f32)
    s_lruc = np.zeros((1, 128, 3, CA), f32)
    s_h = np.zeros((1, 128, CA), f32)
    s_ffn = np.zeros((1, 128, 2, 2 * DFF), f32)
    for c in range(NCORES):
        s, hf = c // 2, c % 2
        r = R[c]
        y_p[s, hf * 1024:(hf + 1) * 1024] = r["y"][0:1024]
        y_s[16 * c:16 * c + 16] = r["y"][1024:1152].reshape(16, 8, D)
        if hf == 1:
            p_conf[0, s] = r["p_conf"]
            p_lruc[0, s] = r["p_lruc"]
            p_h[0, s] = r["p_h"][0]
            p_ffn[0, s] = r["p_ffn"]
        s_conf[0, 16 * c:16 * c + 16] = r["s_conf"].reshape(16, 30, CA)
        s_lruc[0, 16 * c:16 * c + 16] = r["s_lruc"].reshape(16, 3, CA)
        s_h[0, 16 * c:16 * c + 16] = r["s_h"]
        s_ffn[0, 16 * c:16 * c + 16] = r["s_ffn"].reshape(16, 2, 2 * DFF)
    return (y_p, y_s, p_conf, p_lruc, p_h, p_ffn, s_conf, s_lruc, s_h, s_ffn)
```

```python
import numpy as np
from contextlib import ExitStack
import concourse.bass as bass
import concourse.mybir as mybir
from concourse.bass_utils import run_bass_kernel_spmd

F32 = mybir.dt.float32
BF16 = mybir.dt.bfloat16
AF = mybir.ActivationFunctionType
ALU = mybir.AluOpType

NCORES = 8
D = 2048
CA = 1024
DFF = 6144
NPRE = 992
NACT = 1184
HALO = 32
NALL = 2176
EPS = 1e-6

V_CW, V_CB, V_LG, V_LB, V_RW, V_RB, V_BA, V_BX, V_LAM, V_FW, V_FB, V_N = 0, 248, 256, 264, 272, 304, 312, 320, 328, 336, 624, 720


DEBUG = {}


class _Stop(Exception):
    pass


class Op:
    __slots__ = ("eng", "fn", "deps", "sem", "val", "cidx", "needs_inc", "is_dma")

    def __init__(self, eng, fn, is_dma):
        self.eng = eng
        self.fn = fn
        self.deps = []
        self.sem = None
        self.val = None
        self.cidx = -1
        self.needs_inc = False
        self.is_dma = is_dma


class SemRing:
    def __init__(self, nc, es, name, n):
        self.sems = [es.enter_context(nc.semaphore(f"{name}{i}")) for i in range(n)]
        self.n = 0
        self.last = [None] * n

    def next(self):
        i = self.n % len(self.sems)
        v = 16 * (self.n // len(self.sems) + 1)
        self.n += 1
        return i, self.sems[i], v


class Sched:
    NAMES = ["pe", "act", "dve", "pool", "sp"]

    def __init__(self, nc, es):
        self.nc = nc
        self.es = es
        self.streams = {n: [] for n in self.NAMES}
        self.last_w = {}
        self.readers = {}
        self.prog = {n: es.enter_context(nc.semaphore("prog_" + n)) for n in self.NAMES}
        self.ncomp = {n: 0 for n in self.NAMES}
        self.out_dmas = []
        self.stopped = False
        self.gen = 0
        self.fence_ops = []
        self.key_gen = {}
        self.dma_latest = {}

    def fence(self):
        f = []
        for n in self.NAMES:
            for o in reversed(self.streams[n]):
                if not o.is_dma:
                    f.append(o)
                    break
        f.extend(self.dma_latest.values())
        self.fence_ops = f
        self.gen += 1

    def _collect(self, reads, writes):
        deps = []
        for k in list(reads) + list(writes):
            if k not in self.key_gen:
                self.key_gen[k] = self.gen
                if self.gen > 0:
                    for p in self.fence_ops:
                        deps.append((p, "raw"))
        for k in reads:
            w = self.last_w.get(k)
            if w is not None:
                deps.append((w, "raw"))
        for k in writes:
            w = self.last_w.get(k)
            if w is not None:
                deps.append((w, "waw"))
            for r in self.readers.get(k, ()):
                deps.append((r, "war"))
        return deps

    def _register(self, o, reads, writes):
        for k in reads:
            self.readers.setdefault(k, []).append(o)
        for k in writes:
            self.last_w[k] = o
            self.readers[k] = []

    def op(self, eng, fn, reads=(), writes=()):
        if self.stopped:
            return None
        o = Op(eng, fn, False)
        o.cidx = self.ncomp[eng]
        self.ncomp[eng] += 1
        for p, kind in self._collect(reads, writes):
            if p is o or p in o.deps:
                continue
            if (not p.is_dma) and p.eng == eng:
                if eng == "pe" or kind != "raw" or (o.cidx - p.cidx) > 2:
                    continue
            o.deps.append(p)
        self._register(o, reads, writes)
        self.streams[eng].append(o)
        return o

    def dma(self, queue, ring, fn, reads=(), writes=(), is_out=False):
        if self.stopped:
            return None
        o = Op(queue, fn, True)
        i, sem, val = ring.next()
        o.sem, o.val = sem, val
        for p, kind in self._collect(reads, writes):
            if p not in o.deps:
                o.deps.append(p)
        if ring.last[i] is not None and ring.last[i] not in o.deps:
            o.deps.append(ring.last[i])
        ring.last[i] = o
        self.dma_latest[sem.num] = o
        self._register(o, reads, writes)
        self.streams[queue].append(o)
        if is_out:
            self.out_dmas.append(o)
        return o

    def emit(self):
        nc = self.nc
        fin = Op("sp", None, False)
        fin.deps = [o for n in self.NAMES for o in self.streams[n] if o.is_dma]
        self.streams["sp"].append(fin)
        for n in self.NAMES:
            for o in self.streams[n]:
                for p in o.deps:
                    p.needs_inc = True
        for n in self.NAMES:
            c = 0
            for o in self.streams[n]:
                if (not o.is_dma) and o.needs_inc:
                    c += 1
                    o.sem, o.val = self.prog[n], c
        block = self.es.enter_context(nc.Block())
        deco = dict(pe=block.tensor, act=block.scalar, dve=block.vector, pool=block.gpsimd, sp=block.sync)
        for n in self.NAMES:
            def body(e, n=n):
                known = {}
                for o in self.streams[n]:
                    for p in o.deps:
                        key = p.sem.num
                        if known.get(key, 0) >= p.val:
                            continue
                        e.wait_ge(p.sem, p.val)
                        known[key] = p.val
                    if o.fn is None:
                        continue
                    ins = o.fn(e)
                    if o.is_dma:
                        ins.then_inc(o.sem, 16)
                    elif o.needs_inc:
                        ins.then_inc(o.sem, 1)
            deco[n](body)


def build_program():
    nc = bass.Bass("TRN2", target_bir_lowering=False)

    def din(name, shape):
        return nc.dram_tensor(name, list(shape), F32, kind="ExternalInput").ap()

    def dout(name, shape):
        return nc.dram_tensor(name, list(shape), F32, kind="ExternalOutput").ap()

    xall = din("xall", [NALL, D])
    w_in = din("w_in", [D, 4096])
    w_out = din("w_out", [D, D])
    w_up = din("w_up", [D, 2 * DFF])
    w_down = din("w_down", [DFF, D])
    wa_d = din("wa", [8, 128, 128])
    wx_d = din("wx", [8, 128, 128])
    vecs_d = din("vecs", [128, V_N])
    grows_d = din("grows", [4, D])
    flags_d = din("flags", [128, 8])
    ident_d = din("ident", [128, 128])
    st_conf_d = din("st_conf", [480, CA])
    st_lruc_d = din("st_lruc", [48, CA])
    st_h_d = din("st_h", [16, CA])
    st_ffn_d = din("st_ffn", [32, 2 * DFF])

    y_d = dout("y", [1152, D])
    p_conf_d = dout("p_conf", [30, CA])
    p_lruc_d = dout("p_lruc", [3, CA])
    p_h_d = dout("p_h", [1, CA])
    p_ffn_d = dout("p_ffn", [2, 2 * DFF])
    s_conf_d = dout("s_conf", [480, CA])
    s_lruc_d = dout("s_lruc", [48, CA])
    s_h_d = dout("s_h", [16, CA])
    s_ffn_d = dout("s_ffn", [32, 2 * DFF])

    h_scr = nc.dram_tensor("h_scr", [1152, D], F32, kind="Internal").ap()
    f_scr = nc.dram_tensor("f_scr", [1152, D], F32, kind="Internal").ap()

    with ExitStack() as es:
      S = Sched(nc, es)
      try:

        def sb(name, shape, dt=F32, stack=es):
            return stack.enter_context(nc.sbuf_tensor("sb_" + name, list(shape), dt))

        ps = [es.enter_context(nc.psum_tensor(f"ps{b}", [128, 512], F32)) for b in range(8)]
        pstate = {"b": 0}

        busy = [False] * 8

        def bank():
            for _ in range(8):
                b = pstate["b"]
                pstate["b"] = (b + 1) % 8
                if not busy[b]:
                    busy[b] = True
                    return b
            raise RuntimeError("no free PSUM bank: consumer of a previous tile must be registered first")

        def bank_at(b):
            assert not busy[b], f"bank {b} busy"
            busy[b] = True
            return b

        def rel(*bs):
            for b in bs:
                assert busy[b]
                busy[b] = False

        def PK(b):
            return ("ps", b)

        r_setup = SemRing(nc, es, "su", 16)
        r_x = SemRing(nc, es, "xl", 4)
        r_w = SemRing(nc, es, "wl", 8)
        r_st = SemRing(nc, es, "st", 4)
        r_ld = SemRing(nc, es, "ld", 4)
        r_out = SemRing(nc, es, "ot", 8)

        def act_copy(dst, src):
            return lambda e: e.copy(out=dst, in_=src)

        def dve_copy(dst, src):
            return lambda e: e.tensor_copy(dst, src)

        def evac(eng, dst, src):
            return act_copy(dst, src) if eng == "act" else dve_copy(dst, src)

        dbg_outs = []

        def dbg_dump(name, ap, shape, dt, keys):
            if not DEBUG.get("on") or S.stopped:
                return
            d = nc.dram_tensor("dbg_" + name, list(shape), dt, kind="ExternalOutput").ap()
            S.dma("sp", r_out, lambda e: e.dma_start(out=d, in_=ap), reads=keys, is_out=True)

        dbgf = sb("dbgf", [128, 2304], F32) if DEBUG.get("on") else None
        dcount = {"n": 0}

        def dbg_dump16(name, ap, n, keys):
            if not DEBUG.get("on") or S.stopped:
                return
            d = nc.dram_tensor("dbg_" + name, [128, n], F32, kind="ExternalOutput").ap()
            S.op("act", act_copy(dbgf[:, 0:n], ap), reads=list(keys), writes=["dbgf"])
            S.dma("sp", r_out, lambda e: e.dma_start(out=d, in_=dbgf[:, 0:n]), reads=["dbgf"], is_out=True)

        def dbg_stop(tag):
            if DEBUG.get("on") and DEBUG.get("stop") == tag:
                S.stopped = True

        ident = sb("ident", [128, 128])
        identb = sb("identb", [128, 128], BF16)
        ones = sb("ones", [128, 128])
        vecs = sb("vecs", [128, V_N])
        flags = sb("flags", [128, 8])
        cvec = sb("cvec", [128, 8])
        tmpv = sb("tmpv", [128, 8])
        xn_act = sb("xn_act", [128, 16, NACT], BF16)

        S.dma("sp", r_setup, lambda e: e.dma_start(out=ident[:], in_=ident_d), writes=["ident"])
        S.dma("sp", r_setup, lambda e: e.dma_start(out=vecs[:], in_=vecs_d), writes=["vecs"])
        S.dma("sp", r_setup, lambda e: e.dma_start(out=flags[:], in_=flags_d), writes=["flags"])
        S.op("dve", dve_copy(identb[:], ident[:]), reads=["ident"], writes=["identb"])
        S.op("dve", lambda e: e.memset(ones[:], 1.0), writes=["ones"])
        S.op("act", lambda e: e.activation(out=tmpv[:], in_=vecs[:, V_LAM:V_LAM + 8], func=AF.Exp, scale=-1.0),
             reads=["vecs"], writes=["tmpv"])
        S.op("act", lambda e: e.activation(out=cvec[:], in_=tmpv[:], func=AF.Ln, bias=1.0),
             reads=["tmpv"], writes=["cvec0"])
        S.op("dve", lambda e: e.tensor_scalar(cvec[:], cvec[:], -8.0, None, ALU.mult),
             reads=["cvec0"], writes=["cvec"])

        def tr_out(src_ap, M, dst_ap, src_keys, dst_key):
            b = bank()
            S.op("pe", lambda e: e.transpose(ps[b][0:M, 0:128], src_ap, ident[:]),
                 reads=list(src_keys) + ["ident"], writes=[PK(b)])
            S.op("act", act_copy(dst_ap, ps[b][0:M, 0:128]), reads=[PK(b)], writes=[dst_key])
            rel(b)

        PRE_T = [(0, 496), (496, 992)]
        ACT_T = [(0, 352), (352, 704), (704, 1056), (1056, 1184)]
        ALL_T = [(0, 512), (512, 1024), (1024, 1536), (1536, 2048), (2048, 2176)]

        def xn_keys_pre(c0, c1, k):
            return [("xn", i, k // 4) for i in range(c0 // 128, (c1 - 1) // 128 + 1)]

        def xn_keys_act(a0, a1, k):
            keys = []
            if a0 < 32:
                keys.append(("xn", 70, k // 4))
            lo = max(a0, 32)
            if a1 > 32:
                for i in range(8 + (lo - 32) // 128, 8 + (a1 - 1 - 32) // 128 + 1):
                    keys.append(("xn", i, k // 4))
            return keys

        def load_w_pair(slot, c0, c1, buf, key):
            for q, c in enumerate((c0, c1)):
                S.dma("pool", r_w, lambda e, q=q, c=c: e.dma_start(
                    out=buf[:, slot, q], in_=w_in[:, c:c + 128].rearrange("(k p) c -> p k c", p=128)),
                    writes=[(key, slot, q)])

        with ExitStack() as es12:
            catB = sb("catB", [128, 8, NACT], BF16, es12)

            def cat(k):
                return catA[:, k] if k < 8 else catB[:, k - 8]

            with ExitStack() as esB:
                xn_pre = sb("xn_pre", [128, 16, NPRE], BF16, esB)
                wbuf = sb("wbufB", [128, 3, 2, 16, 128], BF16, esB)

                def loadB(jj):
                    load_w_pair(jj % 3, 2048 + 128 * jj, 3072 + 128 * jj, wbuf, "wB")

                loadB(0)
                loadB(1)
                with ExitStack() as es0:
                    grow = sb("grow0", [128, D], F32, es0)
                    xt = [sb(f"p0xt{i}", [128, D], F32, es0) for i in range(4)]
                    xs = [sb(f"p0xs{i}", [128, D], BF16, es0) for i in range(4)]
                    ss = [sb(f"p0ss{i}", [128, 2], F32, es0) for i in range(4)]
                    S.dma("sp", r_setup, lambda e: e.dma_start(out=grow[:], in_=grows_d[0:1, :].partition_broadcast(128)),
                          writes=["grow0"])
                    def p0_A(i):
                        sl = i % 4
                        S.dma("sp", r_x, lambda e, i=i, sl=sl: e.dma_start(out=xt[sl][:], in_=xall[128 * i:128 * i + 128, :]),
                              writes=[("xt", sl)])
                        S.op("act", lambda e, sl=sl: e.activation(out=xs[sl][:], in_=xt[sl][:], func=AF.Square,
                                                                   accum_out=ss[sl][:, 0:1]),
                             reads=[("xt", sl)], writes=[("xs", sl), ("ss0", sl)])
                        S.op("act", lambda e, sl=sl: e.activation(out=ss[sl][:, 1:2], in_=ss[sl][:, 0:1], func=AF.Sqrt,
                                                                   scale=1.0 / D, bias=EPS),
                             reads=[("ss0", sl)], writes=[("ss1", sl)])
                        S.op("dve", lambda e, sl=sl: e.reciprocal(ss[sl][:, 1:2], ss[sl][:, 1:2]),
                             reads=[("ss1", sl)], writes=[("ss2", sl)])
                        S.op("dve", lambda e, sl=sl: e.scalar_tensor_tensor(xs[sl][:], xt[sl][:], ss[sl][:, 1:2], grow[:],
                                                                             ALU.mult, ALU.mult),
                             reads=[("xt", sl), ("ss2", sl), "grow0"], writes=[("xs", sl)])

                    def p0_B(i):
                        sl = i % 4
                        for kq in range(4):
                            b = bank()
                            pb = ps[b][:].bitcast(BF16)
                            for j in range(4):
                                k = kq * 4 + j
                                S.op("pe", lambda e, sl=sl, k=k, j=j, pb=pb: e.transpose(
                                    pb[:, j * 128:(j + 1) * 128], xs[sl][:, k * 128:(k + 1) * 128], identb[:]),
                                    reads=[("xs", sl), "identb"], writes=[PK(b)])
                            pv = pb[:, 0:512].rearrange("p (j t) -> p j t", t=128)
                            eng = "act" if kq % 2 == 0 else "dve"
                            kk = slice(kq * 4, kq * 4 + 4)
                            if i < 7:
                                S.op(eng, evac(eng, xn_pre[:, kk, 128 * i:128 * i + 128], pv),
                                     reads=[PK(b)], writes=[("xn", i, kq)])
                            elif i == 7:
                                S.op(eng, evac(eng, xn_pre[:, kk, 896:992], pv[:, :, 0:96]),
                                     reads=[PK(b)], writes=[("xn", 7, kq)])
                                S.op(eng, evac(eng, xn_act[:, kk, 0:32], pv[:, :, 96:128]),
                                     reads=[PK(b)], writes=[("xn", 70, kq)])
                            else:
                                a0 = 32 + 128 * (i - 8)
                                S.op(eng, evac(eng, xn_act[:, kk, a0:a0 + 128], pv),
                                     reads=[PK(b)], writes=[("xn", i, kq)])
                            rel(b)

                    p0_A(0)
                    for i in range(17):
                        if i + 1 < 17:
                            p0_A(i + 1)
                        p0_B(i)

                    if DEBUG.get("on") and DEBUG.get("stop") == "p0":
                        allxn = [("xn", i, kq) for i in list(range(17)) + [70] for kq in range(4)]
                        dbg_dump("xt0", xt[0][:], [128, D], F32, [("xt", 0)])
                        dbg_dump("ss0", ss[0][:], [128, 2], F32, [("ss2", 0)])
                        dbg_dump16("xs0", xs[0][:], D, [("xs", 0)])
                        dbg_dump("identf", ident[:], [128, 128], F32, ["ident"])
                        dbg_dump16("identb", identb[:], 128, ["identb"])
                        dbg_dump16("xn_act0", xn_act[:, 0, :], NACT, allxn)
                        dbg_dump16("xn_act5", xn_act[:, 5, :], NACT, allxn)
                        dbg_stop("p0")
                S.fence()
                with ExitStack() as esBw:
                    wab = sb("wab", [128, 8, 128], BF16, esBw)
                    wxb = sb("wxb", [128, 8, 128], BF16, esBw)
                    bxe = [sb(f"bxe{i}", [128, 2051 + 176], F32, esBw) for i in range(2)]
                    xcs_ = [sb(f"xc{i}", [128, NALL], F32, esBw) for i in range(2)]
                    xcb = sb("xcb", [128, NALL], BF16, esBw)
                    t1 = sb("t1", [128, NALL], F32, esBw)
                    t2 = sb("t2", [128, NALL], F32, esBw)
                    t3 = sb("t3", [128, NALL], F32, esBw)
                    gbg = [sb(f"gbg{i}", [128, NACT], F32, esBw) for i in range(2)]
                    small = sb("smallB", [128, 8], F32, esBw)
                    lruc_fm = sb("lruc_fm", [128, 8, 51], F32, esBw)
                    hst_fm = sb("hst_fm", [128, 8, 17], F32, esBw)
                    slruc_fm = sb("slruc_fm", [128, 8, 16, 3], F32, esBw)
                    slruh_fm = sb("slruh_fm", [128, 8, 16, 1], F32, esBw)
                    st_tm = sb("st_tmB", [64, CA], F32, esBw)
                    stg = sb("stgB", [64, CA], F32, esBw)

                    S.dma("pool", r_setup, lambda e: e.dma_start(out=wab[:], in_=wa_d.rearrange("h i j -> i h j")), writes=["wab"])
                    S.dma("pool", r_setup, lambda e: e.dma_start(out=wxb[:], in_=wx_d.rearrange("h i j -> i h j")), writes=["wxb"])
                    S.dma("sp", r_setup, lambda e: e.dma_start(out=st_tm[0:48, :], in_=st_lruc_d), writes=["st_tm_a"])
                    S.dma("sp", r_setup, lambda e: e.dma_start(out=st_tm[48:64, :], in_=st_h_d), writes=["st_tm_b"])
                    S.op("dve", lambda e: e.memset(bxe[0][:, 0:3], 0.0), writes=[("bxe", 0)])
                    S.op("dve", lambda e: e.memset(bxe[1][:, 0:3], 0.0), writes=[("bxe", 1)])
                    for j in range(8):
                        b = bank()
                        S.op("pe", lambda e, b=b, j=j: e.transpose(ps[b][0:128, 0:64], st_tm[0:64, j * 128:(j + 1) * 128],
                                                                   ident[0:64, 0:64]),
                             reads=["st_tm_a", "st_tm_b", "ident"], writes=[PK(b)])
                        S.op("act", act_copy(slruc_fm[:, j, :, :], ps[b][:, 0:48].rearrange("p (s k) -> p s k", k=3)),
                             reads=[PK(b)], writes=[("slruc", j)])
                        S.op("act", act_copy(slruh_fm[:, j, :, 0], ps[b][:, 48:64]),
                             reads=[PK(b)], writes=[("slruh", j)])
                        rel(b)

                    def lru_pe_in(j):
                        slot = j % 3
                        res = {"bx_pre": [], "bx_act": []}
                        for (c0, c1) in PRE_T:
                            b = bank()
                            for k in range(16):
                                S.op("pe", lambda e, b=b, k=k, c0=c0, c1=c1, slot=slot: e.matmul(
                                    ps[b][:, 0:c1 - c0], wbuf[:, slot, 0, k, :], xn_pre[:, k, c0:c1], start=(k == 0), stop=(k == 15)),
                                    reads=[("wB", slot, 0)] + xn_keys_pre(c0, c1, k), writes=[PK(b)])
                            res["bx_pre"].append((b, c0, c1))
                        for (a0, a1) in ACT_T:
                            b = bank()
                            for k in range(16):
                                S.op("pe", lambda e, b=b, k=k, a0=a0, a1=a1, slot=slot: e.matmul(
                                    ps[b][:, 0:a1 - a0], wbuf[:, slot, 0, k, :], xn_act[:, k, a0:a1], start=(k == 0), stop=(k == 15)),
                                    reads=[("wB", slot, 0)] + xn_keys_act(a0, a1, k), writes=[PK(b)])
                            res["bx_act"].append((b, a0, a1))
                        return res

                    def lru_pe_bg(j):
                        slot = j % 3
                        out = []
                        for (a0, a1) in ACT_T:
                            b = bank()
                            for k in range(16):
                                S.op("pe", lambda e, b=b, k=k, a0=a0, a1=a1, slot=slot: e.matmul(
                                    ps[b][:, 0:a1 - a0], wbuf[:, slot, 1, k, :], xn_act[:, k, a0:a1], start=(k == 0), stop=(k == 15)),
                                    reads=[("wB", slot, 1)] + xn_keys_act(a0, a1, k), writes=[PK(b)])
                            out.append((b, a0, a1))
                        return out

                    def lru_head(j, pe_in):
                        bx = bxe[j % 2]
                        bxs = bx[:, 2051:2227].rearrange("p (s t) -> p s t", t=11)
                        evk = [("bxe", j % 2)]
                        for n, (b, c0, c1) in enumerate(pe_in["bx_pre"]):
                            key = ("bxe_p", j % 2, n)
                            S.op("act", act_copy(bx[:, 3 + c0:3 + c1], ps[b][:, 0:c1 - c0]), reads=[PK(b)], writes=[key])
                            rel(b)
                            evk.append(key)
                        for n, (b, a0, a1) in enumerate(pe_in["bx_act"]):
                            key = ("bxe_a", j % 2, n)
                            if a1 <= 1056:
                                S.op("dve", dve_copy(bx[:, 3 + NPRE + a0:3 + NPRE + a1], ps[b][:, 0:a1 - a0]), reads=[PK(b)], writes=[key])
                            else:
                                S.op("dve", dve_copy(bxs[:, :, 3:11], ps[b][:, 0:128].rearrange("p (s t) -> p s t", t=8)),
                                     reads=[PK(b)], writes=[key])
                            rel(b)
                            evk.append(key)
                        S.op("dve", dve_copy(bxs[:, :, 0:3], slruc_fm[:, j, :, :]), reads=[("slruc", j)], writes=[("bxe_s", j % 2)])
                        evk.append(("bxe_s", j % 2))
                        S.op("act", act_copy(lruc_fm[:, j, 0:3], bx[:, 2048:2051]), reads=evk, writes=[("lruc_fm", j, 0)])
                        S.op("act", act_copy(lruc_fm[:, j, 3:51].rearrange("p (s k) -> p s k", k=3), bxs[:, :, 8:11]),
                             reads=evk, writes=[("lruc_fm", j, 1)])
                        return evk

                    def lru_conv(j, evk):
                        xc = xcs_[j % 2]
                        xkp, xks = ("xc_p", j % 2), ("xc_s", j % 2)
                        bx = bxe[j % 2]
                        bxs = bx[:, 2051:2227].rearrange("p (s t) -> p s t", t=11)
                        wv = lambda k: vecs[:, V_RW + j * 4 + k:V_RW + j * 4 + k + 1]
                        bv = vecs[:, V_RB + j:V_RB + j + 1]
                        xcs = xc[:, 2048:2176].rearrange("p (s t) -> p s t", t=8)
                        S.op("dve", lambda e: e.tensor_scalar(xc[:, 0:2048], bx[:, 0:2048], wv(0), bv, ALU.mult, ALU.add),
                             reads=evk + ["vecs"], writes=[xkp])
                        S.op("dve", lambda e: e.tensor_scalar(xcs, bxs[:, :, 0:8], wv(0), bv, ALU.mult, ALU.add),
                             reads=evk + ["vecs"], writes=[xks])
                        for k in range(1, 4):
                            S.op("dve", lambda e, k=k: e.scalar_tensor_tensor(xc[:, 0:2048], bx[:, k:k + 2048], wv(k), xc[:, 0:2048], ALU.mult, ALU.add),
                                 reads=evk + [xkp], writes=[xkp])
                            S.op("dve", lambda e, k=k: e.scalar_tensor_tensor(xcs, bxs[:, :, k:k + 8], wv(k), xcs, ALU.mult, ALU.add),
                                 reads=evk + [xks], writes=[xks])
                        S.op("act", act_copy(xcb[:], xc[:]), reads=[xkp, xks], writes=["xcb"])

                    def lru_gates(j):
                        for n, (c0, c1) in enumerate(ALL_T):
                            b = bank()
                            S.op("pe", lambda e, b=b, c0=c0, c1=c1: e.matmul(ps[b][:, 0:c1 - c0], wab[:, j, :], xcb[:, c0:c1], start=True, stop=True),
                                 reads=["wab", "xcb"], writes=[PK(b)])
                            S.op("act", lambda e, b=b, c0=c0, c1=c1: e.activation(out=t1[:, c0:c1], in_=ps[b][:, 0:c1 - c0], func=AF.Sigmoid,
                                                                                  bias=vecs[:, V_BA + j:V_BA + j + 1]),
                                 reads=[PK(b), "vecs"], writes=[("t1", n), "a"])
                            rel(b)
                        for n, (c0, c1) in enumerate(ALL_T):
                            b = bank()
                            S.op("pe", lambda e, b=b, c0=c0, c1=c1: e.matmul(ps[b][:, 0:c1 - c0], wxb[:, j, :], xcb[:, c0:c1], start=True, stop=True),
                                 reads=["wxb", "xcb"], writes=[PK(b)])
                            S.op("act", lambda e, b=b, c0=c0, c1=c1: e.activation(out=t3[:, c0:c1], in_=ps[b][:, 0:c1 - c0], func=AF.Sigmoid,
                                                                                  bias=vecs[:, V_BX + j:V_BX + j + 1]),
                                 reads=[PK(b), "vecs"], writes=[("t3", n), "bt", "ixc"])
                            rel(b)

                    def lru_gelu(j, bgb):
                        gb_ = gbg[j % 2]
                        for n, (b, a0, a1) in enumerate(bgb):
                            S.op("act", lambda e, b=b, a0=a0, a1=a1: e.activation(out=gb_[:, a0:a1], in_=ps[b][:, 0:a1 - a0], func=AF.Gelu_apprx_tanh),
                                 reads=[PK(b)], writes=[("gbg", j % 2, n)])
                            rel(b)

                    def lru_tail_act1(j):
                        t1k = [("t1", n) for n in range(5)]
                        S.op("act", lambda e: e.activation(out=t1[:], in_=t1[:], func=AF.Exp, scale=cvec[:, j:j + 1]),
                             reads=t1k + ["cvec"], writes=["a"])
                        S.op("act", lambda e: e.activation(out=t2[:], in_=t1[:], func=AF.Square), reads=["a", "t2"], writes=["t2"])

                    def lru_tail(j):
                        xc = xcs_[j % 2]
                        t3k = [("t3", n) for n in range(5)]
                        S.op("dve", lambda e: e.tensor_scalar(t2[:], t2[:], 1.0, None, ALU.min), reads=["t2"], writes=["t2"])
                        S.op("act", lambda e: e.activation(out=t2[:], in_=t2[:], func=AF.Sqrt, scale=-1.0, bias=1.0),
                             reads=["t2"], writes=["t2"])
                        S.op("dve", lambda e: e.tensor_tensor(t3[:], t3[:], xc[:], ALU.mult), reads=t3k + [("xc_p", j % 2), ("xc_s", j % 2)], writes=["ixc"])
                        S.op("dve", dve_copy(small[:, 0:1], t3[:, 0:1]), reads=["ixc"], writes=["sm0"])
                        S.op("dve", dve_copy(small[:, 1:2], t3[:, 1024:1025]), reads=["ixc"], writes=["sm1"])
                        S.op("dve", lambda e: e.tensor_tensor(t3[:], t3[:], t2[:], ALU.mult), reads=["ixc", "t2", "sm0", "sm1"], writes=["bt"])
                        S.op("dve", lambda e: e.tensor_scalar(small[:, 2:3], t2[:, 0:1], flags[:, 2:3], flags[:, 1:2], ALU.mult, ALU.add),
                             reads=["t2", "flags"], writes=["sm2"])
                        S.op("dve", lambda e: e.tensor_tensor(t3[:, 0:1], small[:, 0:1], small[:, 2:3], ALU.mult),
                             reads=["sm0", "sm2", "bt"], writes=["bt"])
                        S.op("dve", lambda e: e.tensor_scalar(t3[:, 0:1024], t3[:, 0:1024], flags[:, 0:1], None, ALU.mult),
                             reads=["bt", "flags"], writes=["bt"])
                        S.op("dve", lambda e: e.tensor_scalar(small[:, 3:4], t2[:, 1024:1025], flags[:, 4:5], flags[:, 3:4], ALU.mult, ALU.add),
                             reads=["t2", "flags"], writes=["sm3"])
                        S.op("dve", lambda e: e.tensor_tensor(t3[:, 1024:1025], small[:, 1:2], small[:, 3:4], ALU.mult),
                             reads=["sm1", "sm3", "bt"], writes=["bt"])
                        av = t1[:, 2048:2176].rearrange("p (s t) -> p s t", t=8)[:, :, 0:1]
                        btv = t3[:, 2048:2176].rearrange("p (s t) -> p s t", t=8)[:, :, 0:1]
                        h0 = slruh_fm[:, j, :, :]
                        S.op("dve", lambda e: e.tensor_tensor(av, av, h0, ALU.mult), reads=["a", ("slruh", j)], writes=["a"])
                        S.op("dve", lambda e: e.tensor_tensor(btv, btv, av, ALU.add), reads=["a", "bt"], writes=["bt"])
                        S.op("dve", lambda e: e.memset(av, 0.0), reads=["bt"], writes=["a"])
                        S.op("dve", lambda e: e.tensor_tensor_scan(t2[:], t1[:], t3[:], 0.0, ALU.mult, ALU.add),
                             reads=["a", "bt", "t2"], writes=["t2"])
                        S.op("pool", lambda e: e.tensor_copy(hst_fm[:, j, 0:1], t2[:, 2047:2048]), reads=["t2"], writes=[("hst", j, 0)])
                        S.op("pool", lambda e: e.tensor_copy(hst_fm[:, j, 1:17], t2[:, 2048:2176].rearrange("p (s t) -> p s t", t=8)[:, :, 7]),
                             reads=["t2"], writes=[("hst", j, 1)])
                        S.op("dve", lambda e: e.tensor_tensor(catB[:, j, :], t2[:, NPRE:NALL], gbg[j % 2][:], ALU.mult),
                             reads=["t2"] + [("gbg", j % 2, n) for n in range(4)], writes=[("cat", 8 + j)])

                    evks = {}
                    evks[0] = lru_head(0, lru_pe_in(0))
                    lru_gelu(0, lru_pe_bg(0))
                    lru_conv(0, evks[0])
                    for j in range(8):
                        if j + 2 < 8:
                            loadB(j + 2)
                        if j + 1 < 8:
                            evks[j + 1] = lru_head(j + 1, lru_pe_in(j + 1))
                            lru_gelu(j + 1, lru_pe_bg(j + 1))
                        lru_gates(j)
                        lru_tail_act1(j)
                        if j + 1 < 8 and not DEBUG.get("on"):
                            lru_conv(j + 1, evks[j + 1])
                        lru_tail(j)
                        if j == 0 and DEBUG.get("on"):
                            allxn = [("xn", i, kq) for i in list(range(17)) + [70] for kq in range(4)]
                            dbg_dump16("xn_act0", xn_act[:, 0, :], NACT, allxn)
                            dbg_dump16("xn_act15", xn_act[:, 15, :], NACT, allxn)
                            dbg_dump16("xn_pre3", xn_pre[:, 3, :], NPRE, allxn)
                            dbg_dump16("w0", wbuf[:, 0, 0].rearrange("p k c -> p (k c)"), 2048, [("wB", 0, 0)])
                            dbg_dump("bxe0", bxe[0][:], [128, 2227], F32, [("bxe", 0), ("bxe_s", 0)] + [("bxe_p", 0, n) for n in range(2)] + [("bxe_a", 0, n) for n in range(4)])
                            dbg_dump("xc", xcs_[0][:], [128, NALL], F32, [("xc_p", 0), ("xc_s", 0)])
                            dbg_dump("a", t1[:], [128, NALL], F32, ["a"])
                            dbg_dump("h", t2[:], [128, NALL], F32, ["t2"])
                            dbg_dump("bt", t3[:], [128, NALL], F32, ["bt"])
                            dbg_dump("gbg", gbg[0][:], [128, NACT], F32, [("gbg", 0, n) for n in range(4)])
                            dbg_dump16("catB0", catB[:, 0, :], NACT, [("cat", 8)])
                            dbg_stop("head0")
                        if j + 1 < 8 and DEBUG.get("on"):
                            lru_conv(j + 1, evks[j + 1])

                    for j in range(8):
                        tr_out(lruc_fm[:, j, :], 51, stg[0:51, j * 128:(j + 1) * 128], [("lruc_fm", j, 0), ("lruc_fm", j, 1)], ("stgB", j))
                    kk = [("stgB", j) for j in range(8)]
                    S.dma("sp", r_out, lambda e: e.dma_start(out=p_lruc_d, in_=stg[0:3, :]), reads=kk, is_out=True)
                    S.dma("sp", r_out, lambda e: e.dma_start(out=s_lruc_d, in_=stg[3:51, :]), reads=kk, is_out=True)
                    for j in range(8):
                        tr_out(hst_fm[:, j, :], 17, st_tm[0:17, j * 128:(j + 1) * 128], [("hst", j, 0), ("hst", j, 1)], ("stgH", j))
                    kk = [("stgH", j) for j in range(8)]
                    S.dma("sp", r_out, lambda e: e.dma_start(out=p_h_d, in_=st_tm[0:1, :]), reads=kk + ["st_tm_a", "st_tm_b"], is_out=True)
                    S.dma("sp", r_out, lambda e: e.dma_start(out=s_h_d, in_=st_tm[1:17, :]), reads=kk + ["st_tm_a", "st_tm_b"], is_out=True)

            S.fence()
            with ExitStack() as esA12:
                catA = sb("catA", [128, 8, NACT], BF16, esA12)
                with ExitStack() as esA:
                    caA = sb("caA", [128, 8, NACT], F32, esA)
                    wbufA = sb("wbufA", [128, 3, 2, 16, 128], BF16, esA)
                    A1 = sb("A1", [128, NACT], F32, esA)
                    A2 = sb("A2", [128, NACT], F32, esA)
                    A3 = sb("A3", [128, NACT], F32, esA)
                    A3b = sb("A3b", [128, NACT], F32, esA)
                    UE = 1086 + 608
                    ubf = [sb(f"ubf{i}", [128, UE], BF16, esA) for i in range(2)]
                    diag = sb("diag", [128, 31, 128], BF16, esA)
                    sconf_fm = sb("sconf_fm", [128, 8, 38, 16], F32, esA)
                    pconf_fm = sb("pconf_fm", [128, 8, 30], F32, esA)
                    st_tmA = sb("st_tmA", [120, CA], F32, esA)
                    stgS = sb("stgS", [128, CA], F32, esA)

                    st_tmA2 = sb("st_tmA2", [120, CA], F32, esA)

                    def conf_state_in():
                        for q in range(4):
                            stb = st_tmA if q % 2 == 0 else st_tmA2
                            sk = "st_tmA" if q % 2 == 0 else "st_tmA2"
                            S.dma("sp", r_ld, lambda e, q=q, stb=stb: e.dma_start(out=stb[:, :], in_=st_conf_d[120 * q:120 * q + 120, :]),
                                  writes=[sk])
                            for j in range(8):
                                b = bank()
                                S.op("pe", lambda e, b=b, j=j, stb=stb: e.transpose(ps[b][0:128, 0:120], stb[0:120, j * 128:(j + 1) * 128],
                                                                           ident[0:120, 0:120]),
                                     reads=[sk, "ident"], writes=[PK(b)])
                                S.op("act", act_copy(sconf_fm[:, j, 0:30, 4 * q:4 * q + 4].rearrange("p k s -> p s k"), ps[b][:, 0:120].rearrange("p (s k) -> p s k", k=30)),
                                     reads=[PK(b)], writes=[("sconf_old", j, q)])
                                rel(b)

                    for i in range(2):
                        S.op("dve", lambda e, i=i: e.memset(ubf[i][:, 0:30], 0.0), writes=[("ubf", i)])

                    def loadA(jj):
                        load_w_pair(jj % 3, 128 * jj, 1024 + 128 * jj, wbufA, "wA")

                    def convA_pe_in(j, q):
                        slot = j % 3
                        lst = []
                        if True:
                            for (a0, a1) in ACT_T:
                                b = bank()
                                for k in range(16):
                                    S.op("pe", lambda e, b=b, k=k, a0=a0, a1=a1, slot=slot, q=q: e.matmul(
                                        ps[b][:, 0:a1 - a0], wbufA[:, slot, q, k, :], xn_act[:, k, a0:a1], start=(k == 0), stop=(k == 15)),
                                        reads=[("wA", slot, q)] + xn_keys_act(a0, a1, k), writes=[PK(b)])
                                lst.append((b, a0, a1))
                        return lst

                    def convA_mid1(j, vb, gb):
                        for k in range(31):
                            S.op("dve", lambda e, k=k: e.tensor_scalar(diag[:, k, :], identb[:], vecs[:, V_CW + j * 31 + k:V_CW + j * 31 + k + 1], None, ALU.mult),
                                 reads=["identb", "vecs"], writes=["diag"])
                        for n, (b, a0, a1) in enumerate(gb):
                            S.op("act", lambda e, b=b, a0=a0, a1=a1: e.activation(out=A1[:, a0:a1], in_=ps[b][:, 0:a1 - a0], func=AF.Sigmoid),
                                 reads=[PK(b)], writes=[("A1", n)])
                            rel(b)
                        for n, (b, a0, a1) in enumerate(vb):
                            S.op("dve", lambda e, b=b, a0=a0, a1=a1: e.tensor_tensor(A2[:, a0:a1], ps[b][:, 0:a1 - a0], A1[:, a0:a1], ALU.mult),
                                 reads=[PK(b), ("A1", n)], writes=[("A2", n)])
                            rel(b)

                    def convA_mid2(j):
                        u = ubf[j % 2]
                        ufk = [("A2", n) for n in range(4)]
                        S.op("act", act_copy(sconf_fm[:, j, 30:38, :], A2[:, 1056:1184].rearrange("p (s t) -> p t s", t=8)),
                             reads=ufk, writes=[("sconf_new", j)])
                        S.op("act", act_copy(u[:, 30:1086], A2[:, 0:1056]), reads=ufk, writes=[("ubf_p", j % 2)])
                        S.op("act", act_copy(u[:, 1086:UE], sconf_fm[:, j, :, :].rearrange("p t s -> p (t s)")),
                             reads=[("sconf_new", j)] + [("sconf_old", j, q) for q in range(4)], writes=[("ubf_s", j % 2)])
                        S.op("act", act_copy(pconf_fm[:, j, :], A2[:, 1026:1056]), reads=ufk, writes=[("pconf", j)])

                    CONV_T = [(0, 352), (352, 704), (704, 1056)]

                    def convA_pe_conv(j):
                        u = ubf[j % 2]
                        outb = []
                        rk = ["diag", ("ubf_p", j % 2), ("ubf_s", j % 2), ("ubf", j % 2)]
                        for (a0, a1) in CONV_T:
                            b = bank()
                            for k in range(31):
                                S.op("pe", lambda e, b=b, k=k, a0=a0, a1=a1: e.matmul(
                                    ps[b][:, 0:a1 - a0], diag[:, k, :], u[:, a0 + k:a1 + k], start=(k == 0), stop=(k == 30)),
                                    reads=rk, writes=[PK(b)])
                            outb.append((b, a0, a1))
                        b = bank()
                        for k in range(31):
                            S.op("pe", lambda e, b=b, k=k: e.matmul(
                                ps[b][:, 0:128], diag[:, k, :], u[:, 1086 + 16 * k:1086 + 16 * k + 128], start=(k == 0), stop=(k == 30)),
                                reads=rk, writes=[PK(b)])
                        outb.append((b, 1056, 1184))
                        return outb

                    def convA_tail(j, outb):
                        for n, (b, a0, a1) in enumerate(outb):
                            if n < 3:
                                dst, src = caA[:, j, a0:a1], ps[b][:, 0:a1 - a0]
                            else:
                                dst = caA[:, j, a0:a1].rearrange("p (s t) -> p t s", t=8)
                                src = ps[b][:, 0:128].rearrange("p (t s) -> p t s", s=16)
                            S.op("act", lambda e, dst=dst, src=src: e.activation(out=dst, in_=src, func=AF.Identity,
                                                                                  bias=vecs[:, V_CB + j:V_CB + j + 1]),
                                 reads=[PK(b), "vecs"], writes=[("caA", j, n)])
                            rel(b)
                        cak = [("caA", j, n) for n in range(4)]
                        a1k = [("A1", n) for n in range(4)]
                        if j == 0:
                            S.op("dve", dve_copy(A3b[:], caA[:, 0, :]), reads=cak, writes=["A3b"])
                            S.op("act", lambda e: e.activation(out=A3[:], in_=caA[:, 0, :], func=AF.Square), reads=cak, writes=["A3"])
                        else:
                            S.op("act", lambda e: e.activation(out=A1[:], in_=caA[:, j, :], func=AF.Square), reads=cak, writes=a1k)
                            S.op("dve", lambda e: e.tensor_tensor(A3b[:], A3b[:], caA[:, j, :], ALU.add), reads=cak + ["A3b"], writes=["A3b"])
                            S.op("dve", lambda e: e.tensor_tensor(A3[:], A3[:], A1[:], ALU.add), reads=a1k + ["A3"], writes=["A3"])

                    loadA(0)
                    loadA(1)
                    gbs = {0: convA_pe_in(0, 1)}
                    for j in range(8):
                        if j + 2 < 8:
                            loadA(j + 2)
                        vb = convA_pe_in(j, 0)
                        convA_mid1(j, vb, gbs[j])
                        if j + 1 < 8:
                            gbs[j + 1] = convA_pe_in(j + 1, 1)
                        if j == 0:
                            conf_state_in()
                        convA_mid2(j)
                        convA_tail(j, convA_pe_conv(j))

                    for j in range(8):
                        tr_out(pconf_fm[:, j, :], 30, st_tmA[0:30, j * 128:(j + 1) * 128], [("pconf", j)], "st_tmA")
                    S.dma("sp", r_out, lambda e: e.dma_start(out=p_conf_d, in_=st_tmA[0:30, :]), reads=["st_tmA"], is_out=True)
                    s_conf_v = s_conf_d.rearrange("(s k) c -> k s c", k=30)
                    for bi in range(4):
                        nk = 8 if bi < 3 else 6
                        for j in range(8):
                            tr_out(sconf_fm[:, j, 8 + 8 * bi:8 + 8 * bi + nk, :].rearrange("p t s -> p (t s)"), nk * 16,
                                   stgS[0:nk * 16, j * 128:(j + 1) * 128],
                                   [("sconf_new", j)] + [("sconf_old", j, qq) for qq in range(4)], "stgS")
                        for kl in range(nk):
                            S.dma("sp", r_out, lambda e, bi=bi, kl=kl: e.dma_start(out=s_conf_v[8 * bi + kl], in_=stgS[16 * kl:16 * kl + 16, :]),
                                  reads=["stgS"], is_out=True)

                    sumb, sq_banks = [], []
                    for n, (a0, a1) in enumerate(ACT_T):
                        b = bank()
                        S.op("pe", lambda e, b=b, a0=a0, a1=a1: e.matmul(ps[b][:, 0:a1 - a0], ones[:], A3b[:, a0:a1], start=True, stop=True),
                             reads=["ones", "A3b"], writes=[PK(b)])
                        sumb.append(b)
                        b = bank()
                        S.op("pe", lambda e, b=b, a0=a0, a1=a1: e.matmul(ps[b][:, 0:a1 - a0], ones[:], A3[:, a0:a1], start=True, stop=True),
                             reads=["ones", "A3"], writes=[PK(b)])
                        sq_banks.append(b)
                    for n, (a0, a1) in enumerate(ACT_T):
                        b = sumb[n]
                        b2 = sq_banks[n]
                        S.op("act", lambda e, b=b, a0=a0, a1=a1: e.activation(out=A1[:, a0:a1], in_=ps[b][:, 0:a1 - a0], func=AF.Copy, scale=1.0 / CA),
                             reads=[PK(b)], writes=[("A1", n)])
                        S.op("dve", lambda e, a0=a0, a1=a1: e.tensor_tensor(A3[:, a0:a1], A1[:, a0:a1], A1[:, a0:a1], ALU.mult),
                             reads=[("A1", n)], writes=[("A3m", n), "A3"])
                        S.op("dve", lambda e, b2=b2, a0=a0, a1=a1: e.scalar_tensor_tensor(A2[:, a0:a1], ps[b2][:, 0:a1 - a0], 1.0 / CA, A3[:, a0:a1], ALU.mult, ALU.subtract),
                             reads=[PK(b2), ("A3m", n), ("A2", n)], writes=[("A2", n)])
                        rel(b, b2)
                        S.op("act", lambda e, a0=a0, a1=a1: e.activation(out=A2[:, a0:a1], in_=A2[:, a0:a1], func=AF.Sqrt, bias=EPS),
                             reads=[("A2", n)], writes=[("A2", n)])
                        S.op("dve", lambda e, a0=a0, a1=a1: e.reciprocal(A2[:, a0:a1], A2[:, a0:a1]),
                             reads=[("A2", n)], writes=[("A2", n)])
                    mk = [("A1", n) for n in range(4)]
                    rk2 = [("A2", n) for n in range(4)]
                    for j in range(8):
                        zb = A3 if j % 2 == 0 else A3b
                        zk = "A3" if j % 2 == 0 else "A3b"
                        S.op("dve", lambda e, j=j, zb=zb: e.tensor_tensor(zb[:], caA[:, j, :], A1[:], ALU.subtract),
                             reads=mk + [("caA", j, n) for n in range(4)] + [("A3m", n) for n in range(4)] + [zk], writes=[zk])
                        S.op("dve", lambda e, zb=zb: e.tensor_tensor(zb[:], zb[:], A2[:], ALU.mult), reads=rk2 + [zk], writes=[zk])
                        S.op("act", lambda e, j=j, zb=zb: e.activation(out=catA[:, j, :], in_=zb[:], func=AF.Silu,
                                                                scale=vecs[:, V_LG + j:V_LG + j + 1], bias=vecs[:, V_LB + j:V_LB + j + 1]),
                             reads=[zk, "vecs"], writes=[("cat", j)])

                S.fence()
                xn2 = xn_act
                with ExitStack() as es2:
                    wo = sb("wo", [128, 16, D], BF16, es2)
                    grow1 = sb("grow1", [128, D], F32, es2)
                    grow2 = sb("grow2", [128, D], F32, es2)
                    xt2 = [sb(f"xt2_{i}", [128, D], F32, es2) for i in range(2)]
                    ht = sb("ht", [128, D], F32, es2)
                    xs2 = sb("xs2", [128, D], BF16, es2)
                    ss2 = sb("ss2", [128, 8], F32, es2)
                    S.dma("sp", r_setup, lambda e: e.dma_start(out=grow1[:], in_=grows_d[1:2, :].partition_broadcast(128)), writes=["grow1"])
                    S.dma("sp", r_setup, lambda e: e.dma_start(out=grow2[:], in_=grows_d[2:3, :].partition_broadcast(128)), writes=["grow2"])
                    for fg in range(4):
                        for kh in range(2):
                            S.dma("pool", r_w, lambda e, fg=fg, kh=kh: e.dma_start(
                                out=wo[:, 8 * kh:8 * kh + 8, fg * 512:(fg + 1) * 512],
                                in_=w_out[1024 * kh:1024 * kh + 1024, fg * 512:(fg + 1) * 512].rearrange("(k p) c -> p k c", p=128)),
                                writes=[("wo", fg, kh)])
                    tiles = [(NPRE, 32, 0, None)] + [(1024 + 128 * t, 128, 32 + 128 * t, 128 * t) for t in range(9)]
                    xn_all_keys = [("xn", i, kq) for i in list(range(8, 17)) + [70] for kq in range(4)]
                    junk2 = sb("junk2", [128, D], BF16, es2)

                    def p2_mm(ti):
                        r0, M, a0, hr = tiles[ti]
                        sl = ti % 2
                        S.dma("sp", r_x, lambda e: e.dma_start(out=xt2[sl][0:M, :], in_=xall[r0:r0 + M, :]), writes=[("xt2", sl)])
                        mb = []
                        for fg in range(4):
                            b = bank_at((ti % 2) * 4 + fg)
                            for k in range(16):
                                S.op("pe", lambda e, b=b, k=k, fg=fg: e.matmul(
                                    ps[b][0:M, :], cat(k)[:, a0:a0 + M], wo[:, k, fg * 512:(fg + 1) * 512], start=(k == 0), stop=(k == 15)),
                                    reads=[("cat", k), ("wo", fg, k // 8)], writes=[PK(b)])
                            mb.append(b)
                        return mb

                    def p2_norm(ti, mb):
                        r0, M, a0, hr = tiles[ti]
                        sl = ti % 2
                        for fg, b in enumerate(mb):
                            S.op("act", lambda e, b=b, fg=fg: e.activation(out=junk2[0:M, fg * 512:(fg + 1) * 512], in_=ps[b][0:M, :], func=AF.Square,
                                                                            accum_out=ss2[0:M, fg:fg + 1]),
                                 reads=[PK(b)], writes=[("ss2a", fg)])
                        S.op("dve", lambda e: e.tensor_reduce(ss2[0:M, 4:5], ss2[0:M, 0:4], mybir.AxisListType.X, ALU.add),
                             reads=[("ss2a", fg) for fg in range(4)], writes=["ss2b"])
                        S.op("act", lambda e: e.activation(out=ss2[0:M, 5:6], in_=ss2[0:M, 4:5], func=AF.Sqrt, scale=1.0 / D, bias=EPS),
                             reads=["ss2b"], writes=["ss2c"])
                        S.op("dve", lambda e: e.reciprocal(ss2[0:M, 5:6], ss2[0:M, 5:6]), reads=["ss2c"], writes=["ss2d"])
                        for fg, b in enumerate(mb):
                            S.op("dve", lambda e, b=b, fg=fg: e.scalar_tensor_tensor(ht[0:M, fg * 512:(fg + 1) * 512], ps[b][0:M, :], ss2[0:M, 5:6],
                                                                                  grow1[0:M, fg * 512:(fg + 1) * 512], ALU.mult, ALU.mult),
                                 reads=[PK(b), "ss2d", "grow1"], writes=[("ht0", fg), "ht"])
                            rel(b)
                        S.op("dve", lambda e: e.tensor_tensor(ht[0:M, :], ht[0:M, :], xt2[sl][0:M, :], ALU.add),
                             reads=[("ht0", fg) for fg in range(4)] + [("xt2", sl)], writes=["ht"])
                        if hr is not None:
                            S.dma("sp", r_st, lambda e: e.dma_start(out=h_scr[hr:hr + 128, :], in_=ht[:, :]), reads=["ht"], writes=[("h_scr", hr)])

                    def p2_xn2(ti):
                        r0, M, a0, hr = tiles[ti]
                        S.op("act", lambda e: e.activation(out=junk2[0:M, :], in_=ht[0:M, :], func=AF.Square, accum_out=ss2[0:M, 6:7]),
                             reads=["ht"], writes=["ss2e"])
                        S.op("act", lambda e: e.activation(out=ss2[0:M, 7:8], in_=ss2[0:M, 6:7], func=AF.Sqrt, scale=1.0 / D, bias=EPS),
                             reads=["ss2e"], writes=["ss2f"])
                        S.op("dve", lambda e: e.reciprocal(ss2[0:M, 7:8], ss2[0:M, 7:8]), reads=["ss2f"], writes=["ss2g"])
                        S.op("dve", lambda e: e.scalar_tensor_tensor(xs2[0:M, :], ht[0:M, :], ss2[0:M, 7:8], grow2[0:M, :], ALU.mult, ALU.mult),
                             reads=["ht", "ss2g", "grow2"], writes=["xs2"])
                        for kq in range(4):
                            b = bank_at((ti % 2) * 4 + kq)
                            pb = ps[b][:].bitcast(BF16)
                            for jq in range(4):
                                k = kq * 4 + jq
                                S.op("pe", lambda e, k=k, jq=jq, pb=pb: e.transpose(pb[:, jq * 128:jq * 128 + M], xs2[0:M, k * 128:(k + 1) * 128], identb[0:M, 0:M]),
                                     reads=["xs2", "identb"], writes=[PK(b)])
                            pv = pb[:, 0:512].rearrange("p (j t) -> p j t", t=128)
                            eng = "act" if kq % 2 == 0 else "dve"
                            S.op(eng, evac(eng, xn2[:, kq * 4:kq * 4 + 4, a0:a0 + M], pv[:, :, 0:M]),
                                 reads=[PK(b)], writes=[("xn2", ti, kq)] + (xn_all_keys if ti == 0 and kq < 2 else []))
                            rel(b)

                    mbs = {0: p2_mm(0)}
                    for ti in range(len(tiles)):
                        p2_norm(ti, mbs[ti])
                        if ti + 1 < len(tiles):
                            mbs[ti + 1] = p2_mm(ti + 1)
                        p2_xn2(ti)

        def xn2_keys(a0, a1, k):
            keys = []
            if a0 < 32:
                keys.append(("xn2", 0, k // 4))
            lo = max(a0, 32)
            if a1 > 32:
                for t in range((lo - 32) // 128, (a1 - 1 - 32) // 128 + 1):
                    keys.append(("xn2", 1 + t, k // 4))
            return keys

        S.fence()
        with ExitStack() as es3:
            ffn_fm = sb("ffn_fm", [128, 96, 34], F32, es3)
            hmid = sb("hmid", [128, 24, 1152], BF16, es3)
            wpool = sb("wpool", [128, 6, 4096], BF16, es3)
            EXT = 1026 + 160
            ge = [sb(f"ge{i}", [128, EXT], F32, es3) for i in range(2)]
            ve = [sb(f"ve{i}", [128, EXT], F32, es3) for i in range(2)]
            og = sb("og", [128, 1152], F32, es3)
            ov = sb("ov", [128, 1152], F32, es3)
            dstg = [sb(f"dstg{i}", [128, 512], F32, es3) for i in range(3)]
            prv = [sb(f"prv{i}", [128, 512], F32, es3) for i in range(2)]
            fsts = [sb(f"fst{i}", [34, 1024], F32, es3) for i in range(2)]

            UP_T = [(30, 542), (542, 1054), (1054, 1184)]
            upstate = {"n": 0}

            def load_up(jg):
                unit = upstate["n"] % 6
                upstate["n"] += 1
                for q, c in enumerate((128 * jg, DFF + 128 * jg)):
                    S.dma("pool", r_w, lambda e, q=q, c=c, unit=unit: e.dma_start(
                        out=wpool[:, unit, q * 2048:(q + 1) * 2048].rearrange("p (k c) -> p k c", c=128),
                        in_=w_up[:, c:c + 128].rearrange("(k p) c -> p k c", p=128)),
                        writes=[("wp", unit, q)])
                return unit

            def up_pe(jg, unit):
                res = []
                for q in range(2):
                    wv = wpool[:, unit, q * 2048:(q + 1) * 2048].rearrange("p (k c) -> p k c", c=128)
                    lst = []
                    for (a0, a1) in UP_T:
                        b = bank()
                        for k in range(16):
                            S.op("pe", lambda e, b=b, k=k, a0=a0, a1=a1, wv=wv: e.matmul(
                                ps[b][:, 0:a1 - a0], wv[:, k, :], xn2[:, k, a0:a1], start=(k == 0), stop=(k == 15)),
                                reads=[("wp", unit, q)] + xn2_keys(a0, a1, k), writes=[PK(b)])
                        lst.append(b)
                    res.append(lst)
                return res

            def up_evac(jg, jj, banks):
                for q, (exts, ekey) in enumerate(((ge, "ge"), (ve, "ve"))):
                    ext = exts[jj % 2]
                    ch = jg + 48 * q
                    bA, bB, bC = banks[q]
                    exs = ext[:, 1026:EXT].rearrange("p (s t) -> p s t", t=10)
                    eng = "act" if q == 0 else "dve"
                    sfx = (ekey, jj % 2)
                    S.op(eng, evac(eng, ext[:, 0:512], ps[bA][:, 0:512]), reads=[PK(bA)], writes=[sfx + (0,)])
                    S.op(eng, evac(eng, ext[:, 512:1024], ps[bB][:, 0:512]), reads=[PK(bB)], writes=[sfx + (1,)])
                    S.op("act", act_copy(ext[:, 1024:1026], ps[bC][:, 0:2]), reads=[PK(bC)], writes=[sfx + (2,)])
                    S.op("act", act_copy(exs[:, :, 2:10], ps[bC][:, 2:130].rearrange("p (s t) -> p s t", t=8)), reads=[PK(bC)], writes=[sfx + (3,)])
                    rel(bA, bB, bC)
                    S.op("act", act_copy(exs[:, :, 0:2], ffn_fm[:, ch, 0:32].rearrange("p (s k) -> p s k", k=2)), reads=[("ffn_fm", ch)], writes=[sfx + (4,)])
                    S.op("dve", lambda e, ext=ext: e.tensor_scalar(ext[:, 0:2], ext[:, 0:2], flags[:, 0:1], None, ALU.mult),
                         reads=[sfx + (0,), "flags"], writes=[sfx + (0,)])
                    ek = [sfx + (n,) for n in range(5)]
                    S.op("act", act_copy(ffn_fm[:, ch, 0:32].rearrange("p (s k) -> p s k", k=2), exs[:, :, 8:10]), reads=ek, writes=[("ffn_new", ch)])
                    S.op("act", act_copy(ffn_fm[:, ch, 32:34], ext[:, 1024:1026]), reads=ek, writes=[("ffn_newp", ch)])

            def up_tail(jg, jj):
                for q, (exts, o, ekey, okey) in enumerate(((ge, og, "ge", "og"), (ve, ov, "ve", "ov"))):
                    ext = exts[jj % 2]
                    ch = jg + 48 * q
                    exs = ext[:, 1026:EXT].rearrange("p (s t) -> p s t", t=10)
                    ek = [(ekey, jj % 2, n) for n in range(5)]
                    wv = lambda k, ch=ch: vecs[:, V_FW + ch * 3 + k:V_FW + ch * 3 + k + 1]
                    bv = vecs[:, V_FB + ch:V_FB + ch + 1]
                    os_ = o[:, 1024:1152].rearrange("p (s t) -> p s t", t=8)
                    S.op("dve", lambda e, ext=ext, o=o, wv=wv, bv=bv: e.tensor_scalar(o[:, 0:1024], ext[:, 0:1024], wv(0), bv, ALU.mult, ALU.add),
                         reads=ek + ["vecs", okey], writes=[(okey, 0)])
                    S.op("dve", lambda e, exs=exs, os_=os_, wv=wv, bv=bv: e.tensor_scalar(os_, exs[:, :, 0:8], wv(0), bv, ALU.mult, ALU.add),
                         reads=ek + ["vecs", okey], writes=[(okey, 1)])
                    for k in (1, 2):
                        S.op("dve", lambda e, ext=ext, o=o, wv=wv, k=k: e.scalar_tensor_tensor(o[:, 0:1024], ext[:, k:k + 1024], wv(k), o[:, 0:1024], ALU.mult, ALU.add),
                             reads=ek + [(okey, 0)], writes=[(okey, 0)])
                        S.op("dve", lambda e, exs=exs, os_=os_, wv=wv, k=k: e.scalar_tensor_tensor(os_, exs[:, :, k:k + 8], wv(k), os_, ALU.mult, ALU.add),
                             reads=ek + [(okey, 1)], writes=[(okey, 1)])
                S.op("act", lambda e: e.activation(out=og[:], in_=og[:], func=AF.Gelu_apprx_tanh), reads=[("og", 0), ("og", 1)], writes=["og"])
                S.op("dve", lambda e, jj=jj: e.tensor_tensor(hmid[:, jj, :], og[:], ov[:], ALU.mult),
                     reads=["og", ("ov", 0), ("ov", 1)], writes=[("hmid", jj), "ov"])

            def load_down(g, fg, half):
                units = (0, 1, 2) if half == 0 else (3, 4, 5)
                for u3, unit in enumerate(units):
                    for hh in range(2):
                        kk0 = u3 * 8 + hh * 4
                        r0 = g * 3072 + kk0 * 128
                        S.dma("pool", r_w, lambda e, unit=unit, hh=hh, r0=r0, fg=fg: e.dma_start(
                            out=wpool[:, unit, hh * 2048:(hh + 1) * 2048].rearrange("p (k c) -> p k c", c=512),
                            in_=w_down[r0:r0 + 512, fg * 512:(fg + 1) * 512].rearrange("(k p) c -> p k c", p=128)),
                            writes=[("wp", unit, hh)])
                return units

            def down_phase(g):
                cnt = 0
                units = {0: load_down(g, 0, 0), 1: load_down(g, 1, 1)}
                for fg in range(4):
                    un = units[fg]
                    for tt in range(9):
                        b = bank()
                        for kk in range(24):
                            unit = un[kk // 8]
                            hh = (kk % 8) // 4
                            wv = wpool[:, unit, hh * 2048:(hh + 1) * 2048].rearrange("p (k c) -> p k c", c=512)
                            S.op("pe", lambda e, b=b, kk=kk, wv=wv, tt=tt: e.matmul(
                                ps[b][:, :], hmid[:, kk, tt * 128:(tt + 1) * 128], wv[:, kk % 4, :], start=(kk == 0), stop=(kk == 23)),
                                reads=[("wp", unit, hh), ("hmid", kk)], writes=[PK(b)])
                        ds = dstg[cnt % 3]
                        dkey = ("dstg", cnt % 3)
                        fkey = ("f_scr", tt, fg)
                        dst = f_scr[tt * 128:(tt + 1) * 128, fg * 512:(fg + 1) * 512]
                        if g == 0:
                            S.op("act", act_copy(ds[:], ps[b][:, :]), reads=[PK(b)], writes=[dkey])
                            rel(b)
                        else:
                            pv = prv[cnt % 2]
                            pkey = ("prv", cnt % 2)
                            S.dma("sp", r_ld, lambda e, pv=pv, dst=dst: e.dma_start(out=pv[:], in_=dst), reads=[fkey], writes=[pkey])
                            S.op("dve", lambda e, ds=ds, pv=pv, b=b: e.tensor_tensor(ds[:], ps[b][:, :], pv[:], ALU.add),
                                 reads=[PK(b), pkey], writes=[dkey])
                            rel(b)
                        S.dma("sp", r_st, lambda e, ds=ds, dst=dst: e.dma_start(out=dst, in_=ds[:]), reads=[dkey], writes=[fkey])
                        cnt += 1
                    if fg + 2 < 4:
                        units[fg + 2] = load_down(g, fg + 2, fg % 2)

            def ffn_state_in(qs):
                for qi, q in enumerate(qs):
                    fst = fsts[qi % 2]
                    fk = ("fst", qi % 2)
                    S.dma("sp", r_ld, lambda e, q=q, fst=fst: e.dma_start(out=fst[0:32, :], in_=st_ffn_d[:, 1024 * q:1024 * (q + 1)]), writes=[fk])
                    for jj in range(8):
                        ch = q * 8 + jj
                        b = bank()
                        S.op("pe", lambda e, b=b, jj=jj, fst=fst: e.transpose(ps[b][0:128, 0:32], fst[0:32, jj * 128:(jj + 1) * 128], ident[0:32, 0:32]),
                             reads=[fk, "ident"], writes=[PK(b)])
                        S.op("act", act_copy(ffn_fm[:, ch, 0:32], ps[b][:, 0:32]),
                             reads=[PK(b)], writes=[("ffn_fm", ch)])
                        rel(b)

            for g in range(2):
                pend = {}
                pend[0] = load_up(24 * g)
                pend[1] = load_up(24 * g + 1)
                pend[2] = load_up(24 * g + 2)
                banks = {0: up_pe(24 * g, pend[0])}
                if g == 0:
                    ffn_state_in((0, 6))
                for jj in range(24):
                    if g == 0 and jj in (1, 3, 5, 7, 9):
                        qq = (jj + 1) // 2
                        ffn_state_in((qq, 6 + qq))
                    if jj + 3 < 24:
                        pend[jj + 3] = load_up(24 * g + jj + 3)
                    up_evac(24 * g + jj, jj, banks[jj])
                    if jj + 1 < 24:
                        banks[jj + 1] = up_pe(24 * g + jj + 1, pend[jj + 1])
                    up_tail(24 * g + jj, jj)
                if g == 1:
                    for q in range(12):
                        fst = fsts[q % 2]
                        fk = ("fst", q % 2)
                        for jj in range(8):
                            ch = q * 8 + jj
                            tr_out(ffn_fm[:, ch, :], 34, fst[0:34, jj * 128:(jj + 1) * 128], [("ffn_new", ch), ("ffn_newp", ch)], fk)
                        S.dma("sp", r_out, lambda e, q=q, fst=fst: e.dma_start(out=s_ffn_d[:, 1024 * q:1024 * (q + 1)], in_=fst[0:32, :]), reads=[fk], is_out=True)
                        S.dma("sp", r_out, lambda e, q=q, fst=fst: e.dma_start(out=p_ffn_d[:, 1024 * q:1024 * (q + 1)], in_=fst[32:34, :]), reads=[fk], is_out=True)
                down_phase(g)

        S.fence()
        with ExitStack() as es5:
            grow3 = sb("grow3", [128, D], F32, es5)
            ft = [sb(f"ft{i}", [128, D], F32, es5) for i in range(4)]
            hh_ = [sb(f"hh{i}", [128, D], F32, es5) for i in range(4)]
            junk = sb("junk5", [128, D], BF16, es5)
            ss5 = sb("ss5", [128, 8], F32, es5)
            S.dma("sp", r_setup, lambda e: e.dma_start(out=grow3[:], in_=grows_d[3:4, :].partition_broadcast(128)), writes=["grow3"])
            def p5_load(tt):
                sl = tt % 4
                S.dma("sp", r_x, lambda e: e.dma_start(out=ft[sl][:], in_=f_scr[tt * 128:(tt + 1) * 128, :]),
                      reads=[("f_scr", tt, fg) for fg in range(4)], writes=[("ft", sl)])
                S.dma("sp", r_ld, lambda e: e.dma_start(out=hh_[sl][:], in_=h_scr[tt * 128:(tt + 1) * 128, :]),
                      reads=[("h_scr", 128 * tt)], writes=[("hh", sl)])

            def p5_comp(tt):
                sl = tt % 4
                c0 = 2 * sl
                S.op("act", lambda e: e.activation(out=junk[:], in_=ft[sl][:], func=AF.Square, accum_out=ss5[:, c0:c0 + 1]),
                     reads=[("ft", sl)], writes=[("ss5a", sl)])
                S.op("act", lambda e: e.activation(out=ss5[:, c0 + 1:c0 + 2], in_=ss5[:, c0:c0 + 1], func=AF.Sqrt, scale=1.0 / D, bias=EPS),
                     reads=[("ss5a", sl)], writes=[("ss5b", sl)])
                S.op("dve", lambda e: e.reciprocal(ss5[:, c0 + 1:c0 + 2], ss5[:, c0 + 1:c0 + 2]), reads=[("ss5b", sl)], writes=[("ss5c", sl)])
                S.op("dve", lambda e: e.scalar_tensor_tensor(ft[sl][:], ft[sl][:], ss5[:, c0 + 1:c0 + 2], grow3[:], ALU.mult, ALU.mult),
                     reads=[("ft", sl), ("ss5c", sl), "grow3"], writes=[("ft", sl)])
                S.op("dve", lambda e: e.tensor_tensor(ft[sl][:], ft[sl][:], hh_[sl][:], ALU.add),
                     reads=[("ft", sl), ("hh", sl)], writes=[("ft", sl)])
                S.dma("act", r_out, lambda e: e.dma_start(out=y_d[tt * 128:(tt + 1) * 128, :], in_=ft[sl][:]),
                      reads=[("ft", sl)], is_out=True)

            for tt in range(3):
                p5_load(tt)
            for tt in range(9):
                if tt + 3 < 9:
                    p5_load(tt + 3)
                p5_comp(tt)

      except _Stop:
        pass
      S.emit()
    return nc


_CACHE = {}


def _fm(v):
    v = np.asarray(v, dtype=np.float32).reshape(-1, 128)
    return np.ascontiguousarray(v.T)


def kernel(x_prompt, x_sample, state_conf_conv, state_lru_conv, state_lru_h, state_ffn_conv,
           g_mix_pre, g_mix_post, w_in, conf_dw_w, conf_dw_b, conf_ln_g, conf_ln_b,
           lru_conv_w, lru_conv_b, lru_wa, lru_ba, lru_wx, lru_bx, lru_lambda, w_out,
           g_ffn_pre, g_ffn_post, w_up, ffn_dw_w, ffn_dw_b, w_down):
    f32 = np.float32
    x_prompt = np.asarray(x_prompt, f32)
    x_sample = np.asarray(x_sample, f32)
    vecs = np.zeros((128, V_N), f32)
    cw = np.asarray(conf_dw_w, f32)[0].reshape(31, 8, 128)
    vecs[:, V_CW:V_CW + 248] = cw.transpose(2, 1, 0).reshape(128, 248)
    vecs[:, V_CB:V_CB + 8] = _fm(conf_dw_b[0])
    vecs[:, V_LG:V_LG + 8] = _fm(conf_ln_g[0])
    vecs[:, V_LB:V_LB + 8] = _fm(conf_ln_b[0])
    rw = np.asarray(lru_conv_w, f32)[0].reshape(4, 8, 128)
    vecs[:, V_RW:V_RW + 32] = rw.transpose(2, 1, 0).reshape(128, 32)
    vecs[:, V_RB:V_RB + 8] = _fm(lru_conv_b[0])
    vecs[:, V_BA:V_BA + 8] = _fm(lru_ba[0])
    vecs[:, V_BX:V_BX + 8] = _fm(lru_bx[0])
    vecs[:, V_LAM:V_LAM + 8] = _fm(lru_lambda[0])
    fw = np.asarray(ffn_dw_w, f32)[0].reshape(3, 96, 128)
    vecs[:, V_FW:V_FW + 288] = fw.transpose(2, 1, 0).reshape(128, 288)
    vecs[:, V_FB:V_FB + 96] = _fm(ffn_dw_b[0])
    grows = np.ascontiguousarray(np.stack([np.asarray(g_mix_pre, f32)[0], np.asarray(g_mix_post, f32)[0],
                                           np.asarray(g_ffn_pre, f32)[0], np.asarray(g_ffn_post, f32)[0]]))
    ident = np.eye(128, dtype=f32)
    shared = dict(
        w_in=np.ascontiguousarray(np.asarray(w_in, f32)[0]), w_out=np.ascontiguousarray(np.asarray(w_out, f32)[0]),
        w_up=np.ascontiguousarray(np.asarray(w_up, f32)[0]), w_down=np.ascontiguousarray(np.asarray(w_down, f32)[0]),
        wa=np.ascontiguousarray(np.asarray(lru_wa, f32)[0]), wx=np.ascontiguousarray(np.asarray(lru_wx, f32)[0]),
        vecs=vecs, grows=grows, ident=ident)
    in_maps = []
    for c in range(NCORES):
        s, hf = c // 2, c % 2
        xall = np.zeros((NALL, D), f32)
        if hf == 1:
            xall[0:1024] = x_prompt[s, 0:1024]
        xall[1024:2048] = x_prompt[s, hf * 1024:(hf + 1) * 1024]
        xall[2048:2176] = x_sample[16 * c:16 * c + 16].reshape(128, D)
        fl = np.zeros((128, 8), f32)
        fl[:, 0] = float(hf)
        fl[:, 1] = float(hf)
        fl[:, 2] = 1.0 - float(hf)
        fl[:, 3] = 1.0 - float(hf)
        fl[:, 4] = float(hf)
        m = dict(shared)
        m.update(
            xall=xall, flags=fl,
            st_conf=np.ascontiguousarray(np.asarray(state_conf_conv, f32)[0, 16 * c:16 * c + 16].reshape(480, CA)),
            st_lruc=np.ascontiguousarray(np.asarray(state_lru_conv, f32)[0, 16 * c:16 * c + 16].reshape(48, CA)),
            st_h=np.ascontiguousarray(np.asarray(state_lru_h, f32)[0, 16 * c:16 * c + 16].reshape(16, CA)),
            st_ffn=np.ascontiguousarray(np.asarray(state_ffn_conv, f32)[0, 16 * c:16 * c + 16].reshape(32, 2 * DFF)),
        )
        in_maps.append(m)
    if "nc" not in _CACHE:
        _CACHE["nc"] = build_program()
    nc = _CACHE["nc"]
    res = run_bass_kernel_spmd(nc, in_maps, core_ids=list(range(NCORES)))
    R = res.results
    y_p = np.zeros((4, 2048, D), f32)
    y_s = np.zeros((128, 8, D), f32)
    p_conf = np.zeros((1, 4, 30, CA), f32)
    p_lruc = np.zeros((1, 4, 3, CA), f32)
    p_h = np.zeros((1, 4, CA), f32)
    p_ffn = np.zeros((1, 4, 2, 2 * DFF), f32)
    s_conf = np.zeros((1, 128, 30, CA), f32)
    s_lruc = np.zeros((1, 128, 3, CA), f32)
    s_h = np.zeros((1, 128, CA), f32)
    s_ffn = np.zeros((1, 128, 2, 2 * DFF), f32)
    for c in range(NCORES):
        s, hf = c // 2, c % 2
        r = R[c]
        y_p[s, hf * 1024:(hf + 1) * 1024] = r["y"][0:1024]
        y_s[16 * c:16 * c + 16] = r["y"][1024:1152].reshape(16, 8, D)
        if hf == 1:
            p_conf[0, s] = r["p_conf"]
            p_lruc[0, s] = r["p_lruc"]
            p_h[0, s] = r["p_h"][0]
            p_ffn[0, s] = r["p_ffn"]
        s_conf[0, 16 * c:16 * c + 16] = r["s_conf"].reshape(16, 30, CA)
        s_lruc[0, 16 * c:16 * c + 16] = r["s_lruc"].reshape(16, 3, CA)
        s_h[0, 16 * c:16 * c + 16] = r["s_h"]
        s_ffn[0, 16 * c:16 * c + 16] = r["s_ffn"].reshape(16, 2, 2 * DFF)
    return (y_p, y_s, p_conf, p_lruc, p_h, p_ffn, s_conf, s_lruc, s_h, s_ffn)
```

```python
import numpy as np
from contextlib import ExitStack
import concourse.bass as bass
import concourse.mybir as mybir
from concourse.bass_utils import run_bass_kernel_spmd

F32 = mybir.dt.float32
BF16 = mybir.dt.bfloat16
AF = mybir.ActivationFunctionType
ALU = mybir.AluOpType

NCORES = 8
D = 2048
CA = 1024
DFF = 6144
NPRE = 992
NACT = 1184
HALO = 32
NALL = 2176
EPS = 1e-6

V_CW, V_CB, V_LG, V_LB, V_RW, V_RB, V_BA, V_BX, V_LAM, V_FW, V_FB, V_N = 0, 248, 256, 264, 272, 304, 312, 320, 328, 336, 624, 720


DEBUG = {}


class _Stop(Exception):
    pass


class Op:
    __slots__ = ("eng", "fn", "deps", "sem", "val", "cidx", "needs_inc", "is_dma")

    def __init__(self, eng, fn, is_dma):
        self.eng = eng
        self.fn = fn
        self.deps = []
        self.sem = None
        self.val = None
        self.cidx = -1
        self.needs_inc = False
        self.is_dma = is_dma


class SemRing:
    def __init__(self, nc, es, name, n):
        self.sems = [es.enter_context(nc.semaphore(f"{name}{i}")) for i in range(n)]
        self.n = 0
        self.last = [None] * n

    def next(self):
        i = self.n % len(self.sems)
        v = 16 * (self.n // len(self.sems) + 1)
        self.n += 1
        return i, self.sems[i], v


class Sched:
    NAMES = ["pe", "act", "dve", "pool", "sp"]

    def __init__(self, nc, es):
        self.nc = nc
        self.es = es
        self.streams = {n: [] for n in self.NAMES}
        self.last_w = {}
        self.readers = {}
        self.prog = {n: es.enter_context(nc.semaphore("prog_" + n)) for n in self.NAMES}
        self.ncomp = {n: 0 for n in self.NAMES}
        self.out_dmas = []
        self.stopped = False
        self.gen = 0
        self.fence_ops = []
        self.key_gen = {}
        self.dma_latest = {}

    def fence(self):
        f = []
        for n in self.NAMES:
            for o in reversed(self.streams[n]):
                if not o.is_dma:
                    f.append(o)
                    break
        f.extend(self.dma_latest.values())
        self.fence_ops = f
        self.gen += 1

    def _collect(self, reads, writes):
        deps = []
        for k in list(reads) + list(writes):
            if k not in self.key_gen:
                self.key_gen[k] = self.gen
                if self.gen > 0:
                    for p in self.fence_ops:
                        deps.append((p, "raw"))
        for k in reads:
            w = self.last_w.get(k)
            if w is not None:
                deps.append((w, "raw"))
        for k in writes:
            w = self.last_w.get(k)
            if w is not None:
                deps.append((w, "waw"))
            for r in self.readers.get(k, ()):
                deps.append((r, "war"))
        return deps

    def _register(self, o, reads, writes):
        for k in reads:
            self.readers.setdefault(k, []).append(o)
        for k in writes:
            self.last_w[k] = o
            self.readers[k] = []

    def op(self, eng, fn, reads=(), writes=()):
        if self.stopped:
            return None
        o = Op(eng, fn, False)
        o.cidx = self.ncomp[eng]
        self.ncomp[eng] += 1
        for p, kind in self._collect(reads, writes):
            if p is o or p in o.deps:
                continue
            if (not p.is_dma) and p.eng == eng:
                if eng == "pe" or kind != "raw" or (o.cidx - p.cidx) > 2:
                    continue
            o.deps.append(p)
        self._register(o, reads, writes)
        self.streams[eng].append(o)
        return o

    def dma(self, queue, ring, fn, reads=(), writes=(), is_out=False):
        if self.stopped:
            return None
        o = Op(queue, fn, True)
        i, sem, val = ring.next()
        o.sem, o.val = sem, val
        for p, kind in self._collect(reads, writes):
            if p not in o.deps:
                o.deps.append(p)
        if ring.last[i] is not None and ring.last[i] not in o.deps:
            o.deps.append(ring.last[i])
        ring.last[i] = o
        self.dma_latest[sem.num] = o
        self._register(o, reads, writes)
        self.streams[queue].append(o)
        if is_out:
            self.out_dmas.append(o)
        return o

    def emit(self):
        nc = self.nc
        fin = Op("sp", None, False)
        fin.deps = [o for n in self.NAMES for o in self.streams[n] if o.is_dma]
        self.streams["sp"].append(fin)
        for n in self.NAMES:
            for o in self.streams[n]:
                for p in o.deps:
                    p.needs_inc = True
        for n in self.NAMES:
            c = 0
            for o in self.streams[n]:
                if (not o.is_dma) and o.needs_inc:
                    c += 1
                    o.sem, o.val = self.prog[n], c
        block = self.es.enter_context(nc.Block())
        deco = dict(pe=block.tensor, act=block.scalar, dve=block.vector, pool=block.gpsimd, sp=block.sync)
        for n in self.NAMES:
            def body(e, n=n):
                known = {}
                for o in self.streams[n]:
                    for p in o.deps:
                        key = p.sem.num
                        if known.get(key, 0) >= p.val:
                            continue
                        e.wait_ge(p.sem, p.val)
                        known[key] = p.val
                    if o.fn is None:
                        continue
                    ins = o.fn(e)
                    if o.is_dma:
                        ins.then_inc(o.sem, 16)
                    elif o.needs_inc:
                        ins.then_inc(o.sem, 1)
            deco[n](body)


def build_program():
    nc = bass.Bass("TRN2", target_bir_lowering=False)

    def din(name, shape):
        return nc.dram_tensor(name, list(shape), F32, kind="ExternalInput").ap()

    def dout(name, shape):
        return nc.dram_tensor(name, list(shape), F32, kind="ExternalOutput").ap()

    xall = din("xall", [NALL, D])
    w_in = din("w_in", [D, 4096])
    w_out = din("w_out", [D, D])
    w_up = din("w_up", [D, 2 * DFF])
    w_down = din("w_down", [DFF, D])
    wa_d = din("wa", [8, 128, 128])
    wx_d = din("wx", [8, 128, 128])
    vecs_d = din("vecs", [128, V_N])
    grows_d = din("grows", [4, D])
    flags_d = din("flags", [128, 8])
    ident_d = din("ident", [128, 128])
    st_conf_d = din("st_conf", [480, CA])
    st_lruc_d = din("st_lruc", [48, CA])
    st_h_d = din("st_h", [16, CA])
    st_ffn_d = din("st_ffn", [32, 2 * DFF])

    y_d = dout("y", [1152, D])
    p_conf_d = dout("p_conf", [30, CA])
    p_lruc_d = dout("p_lruc", [3, CA])
    p_h_d = dout("p_h", [1, CA])
    p_ffn_d = dout("p_ffn", [2, 2 * DFF])
    s_conf_d = dout("s_conf", [480, CA])
    s_lruc_d = dout("s_lruc", [48, CA])
    s_h_d = dout("s_h", [16, CA])
    s_ffn_d = dout("s_ffn", [32, 2 * DFF])

    h_scr = nc.dram_tensor("h_scr", [1152, D], F32, kind="Internal").ap()
    f_scr = nc.dram_tensor("f_scr", [1152, D], F32, kind="Internal").ap()

    with ExitStack() as es:
      S = Sched(nc, es)
      try:

        def sb(name, shape, dt=F32, stack=es):
            return stack.enter_context(nc.sbuf_tensor("sb_" + name, list(shape), dt))

        ps = [es.enter_context(nc.psum_tensor(f"ps{b}", [128, 512], F32)) for b in range(8)]
        pstate = {"b": 0}

        busy = [False] * 8

        def bank():
            for _ in range(8):
                b = pstate["b"]
                pstate["b"] = (b + 1) % 8
                if not busy[b]:
                    busy[b] = True
                    return b
            raise RuntimeError("no free PSUM bank: consumer of a previous tile must be registered first")

        def bank_at(b):
            assert not busy[b], f"bank {b} busy"
            busy[b] = True
            return b

        def rel(*bs):
            for b in bs:
                assert busy[b]
                busy[b] = False

        def PK(b):
            return ("ps", b)

        r_setup = SemRing(nc, es, "su", 16)
        r_x = SemRing(nc, es, "xl", 4)
        r_w = SemRing(nc, es, "wl", 8)
        r_st = SemRing(nc, es, "st", 4)
        r_ld = SemRing(nc, es, "ld", 4)
        r_out = SemRing(nc, es, "ot", 8)

        def act_copy(dst, src):
            return lambda e: e.copy(out=dst, in_=src)

        def dve_copy(dst, src):
            return lambda e: e.tensor_copy(dst, src)

        def evac(eng, dst, src):
            return act_copy(dst, src) if eng == "act" else dve_copy(dst, src)

        dbg_outs = []

        def dbg_dump(name, ap, shape, dt, keys):
            if not DEBUG.get("on") or S.stopped:
                return
            d = nc.dram_tensor("dbg_" + name, list(shape), dt, kind="ExternalOutput").ap()
            S.dma("sp", r_out, lambda e: e.dma_start(out=d, in_=ap), reads=keys, is_out=True)

        dbgf = sb("dbgf", [128, 2304], F32) if DEBUG.get("on") else None
        dcount = {"n": 0}

        def dbg_dump16(name, ap, n, keys):
            if not DEBUG.get("on") or S.stopped:
                return
            d = nc.dram_tensor("dbg_" + name, [128, n], F32, kind="ExternalOutput").ap()
            S.op("act", act_copy(dbgf[:, 0:n], ap), reads=list(keys), writes=["dbgf"])
            S.dma("sp", r_out, lambda e: e.dma_start(out=d, in_=dbgf[:, 0:n]), reads=["dbgf"], is_out=True)

        def dbg_stop(tag):
            if DEBUG.get("on") and DEBUG.get("stop") == tag:
                S.stopped = True

        ident = sb("ident", [128, 128])
        identb = sb("identb", [128, 128], BF16)
        ones = sb("ones", [128, 128])
        vecs = sb("vecs", [128, V_N])
        flags = sb("flags", [128, 8])
        cvec = sb("cvec", [128, 8])
        tmpv = sb("tmpv", [128, 8])
        xn_act = sb("xn_act", [128, 16, NACT], BF16)

        S.dma("sp", r_setup, lambda e: e.dma_start(out=ident[:], in_=ident_d), writes=["ident"])
        S.dma("sp", r_setup, lambda e: e.dma_start(out=vecs[:], in_=vecs_d), writes=["vecs"])
        S.dma("sp", r_setup, lambda e: e.dma_start(out=flags[:], in_=flags_d), writes=["flags"])
        S.op("dve", dve_copy(identb[:], ident[:]), reads=["ident"], writes=["identb"])
        S.op("dve", lambda e: e.memset(ones[:], 1.0), writes=["ones"])
        S.op("act", lambda e: e.activation(out=tmpv[:], in_=vecs[:, V_LAM:V_LAM + 8], func=AF.Exp, scale=-1.0),
             reads=["vecs"], writes=["tmpv"])
        S.op("act", lambda e: e.activation(out=cvec[:], in_=tmpv[:], func=AF.Ln, bias=1.0),
             reads=["tmpv"], writes=["cvec0"])
        S.op("dve", lambda e: e.tensor_scalar(cvec[:], cvec[:], -8.0, None, ALU.mult),
             reads=["cvec0"], writes=["cvec"])

        def tr_out(src_ap, M, dst_ap, src_keys, dst_key):
            b = bank()
            S.op("pe", lambda e: e.transpose(ps[b][0:M, 0:128], src_ap, ident[:]),
                 reads=list(src_keys) + ["ident"], writes=[PK(b)])
            S.op("act", act_copy(dst_ap, ps[b][0:M, 0:128]), reads=[PK(b)], writes=[dst_key])
            rel(b)

        PRE_T = [(0, 496), (496, 992)]
        ACT_T = [(0, 352), (352, 704), (704, 1056), (1056, 1184)]
        ALL_T = [(0, 512), (512, 1024), (1024, 1536), (1536, 2048), (2048, 2176)]

        def xn_keys_pre(c0, c1, k):
            return [("xn", i, k // 4) for i in range(c0 // 128, (c1 - 1) // 128 + 1)]

        def xn_keys_act(a0, a1, k):
            keys = []
            if a0 < 32:
                keys.append(("xn", 70, k // 4))
            lo = max(a0, 32)
            if a1 > 32:
                for i in range(8 + (lo - 32) // 128, 8 + (a1 - 1 - 32) // 128 + 1):
                    keys.append(("xn", i, k // 4))
            return keys

        def load_w_pair(slot, c0, c1, buf, key):
            for q, c in enumerate((c0, c1)):
                S.dma("pool", r_w, lambda e, q=q, c=c: e.dma_start(
                    out=buf[:, slot, q], in_=w_in[:, c:c + 128].rearrange("(k p) c -> p k c", p=128)),
                    writes=[(key, slot, q)])

        with ExitStack() as es12:
            catB = sb("catB", [128, 8, NACT], BF16, es12)

            def cat(k):
                return catA[:, k] if k < 8 else catB[:, k - 8]

            with ExitStack() as esB:
                xn_pre = sb("xn_pre", [128, 16, NPRE], BF16, esB)
                wbuf = sb("wbufB", [128, 3, 2, 16, 128], BF16, esB)

                def loadB(jj):
                    load_w_pair(jj % 3, 2048 + 128 * jj, 3072 + 128 * jj, wbuf, "wB")

                loadB(0)
                loadB(1)
                with ExitStack() as es0:
                    grow = sb("grow0", [128, D], F32, es0)
                    xt = [sb(f"p0xt{i}", [128, D], F32, es0) for i in range(4)]
                    xs = [sb(f"p0xs{i}", [128, D], BF16, es0) for i in range(4)]
                    ss = [sb(f"p0ss{i}", [128, 2], F32, es0) for i in range(4)]
                    S.dma("sp", r_setup, lambda e: e.dma_start(out=grow[:], in_=grows_d[0:1, :].partition_broadcast(128)),
                          writes=["grow0"])
                    def p0_A(i):
                        sl = i % 4
                        S.dma("sp", r_x, lambda e, i=i, sl=sl: e.dma_start(out=xt[sl][:], in_=xall[128 * i:128 * i + 128, :]),
                              writes=[("xt", sl)])
                        S.op("act", lambda e, sl=sl: e.activation(out=xs[sl][:], in_=xt[sl][:], func=AF.Square,
                                                                   accum_out=ss[sl][:, 0:1]),
                             reads=[("xt", sl)], writes=[("xs", sl), ("ss0", sl)])
                        S.op("act", lambda e, sl=sl: e.activation(out=ss[sl][:, 1:2], in_=ss[sl][:, 0:1], func=AF.Sqrt,
                                                                   scale=1.0 / D, bias=EPS),
                             reads=[("ss0", sl)], writes=[("ss1", sl)])
                        S.op("dve", lambda e, sl=sl: e.reciprocal(ss[sl][:, 1:2], ss[sl][:, 1:2]),
                             reads=[("ss1", sl)], writes=[("ss2", sl)])
                        S.op("dve", lambda e, sl=sl: e.scalar_tensor_tensor(xs[sl][:], xt[sl][:], ss[sl][:, 1:2], grow[:],
                                                                             ALU.mult, ALU.mult),
                             reads=[("xt", sl), ("ss2", sl), "grow0"], writes=[("xs", sl)])

                    def p0_B(i):
                        sl = i % 4
                        for kq in range(4):
                            b = bank()
                            pb = ps[b][:].bitcast(BF16)
                            for j in range(4):
                                k = kq * 4 + j
                                S.op("pe", lambda e, sl=sl, k=k, j=j, pb=pb: e.transpose(
                                    pb[:, j * 128:(j + 1) * 128], xs[sl][:, k * 128:(k + 1) * 128], identb[:]),
                                    reads=[("xs", sl), "identb"], writes=[PK(b)])
                            pv = pb[:, 0:512].rearrange("p (j t) -> p j t", t=128)
                            eng = "act" if kq % 2 == 0 else "dve"
                            kk = slice(kq * 4, kq * 4 + 4)
                            if i < 7:
                                S.op(eng, evac(eng, xn_pre[:, kk, 128 * i:128 * i + 128], pv),
                                     reads=[PK(b)], writes=[("xn", i, kq)])
                            elif i == 7:
                                S.op(eng, evac(eng, xn_pre[:, kk, 896:992], pv[:, :, 0:96]),
                                     reads=[PK(b)], writes=[("xn", 7, kq)])
                                S.op(eng, evac(eng, xn_act[:, kk, 0:32], pv[:, :, 96:128]),
                                     reads=[PK(b)], writes=[("xn", 70, kq)])
                            else:
                                a0 = 32 + 128 * (i - 8)
                                S.op(eng, evac(eng, xn_act[:, kk, a0:a0 + 128], pv),
                                     reads=[PK(b)], writes=[("xn", i, kq)])
                            rel(b)

                    p0_A(0)
                    for i in range(17):
                        if i + 1 < 17:
                            p0_A(i + 1)
                        p0_B(i)

                    if DEBUG.get("on") and DEBUG.get("stop") == "p0":
                        allxn = [("xn", i, kq) for i in list(range(17)) + [70] for kq in range(4)]
                        dbg_dump("xt0", xt[0][:], [128, D], F32, [("xt", 0)])
                        dbg_dump("ss0", ss[0][:], [128, 2], F32, [("ss2", 0)])
                        dbg_dump16("xs0", xs[0][:], D, [("xs", 0)])
                        dbg_dump("identf", ident[:], [128, 128], F32, ["ident"])
                        dbg_dump16("identb", identb[:], 128, ["identb"])
                        dbg_dump16("xn_act0", xn_act[:, 0, :], NACT, allxn)
                        dbg_dump16("xn_act5", xn_act[:, 5, :], NACT, allxn)
                        dbg_stop("p0")
                S.fence()
                with ExitStack() as esBw:
                    wab = sb("wab", [128, 8, 128], BF16, esBw)
                    wxb = sb("wxb", [128, 8, 128], BF16, esBw)
                    bxe = [sb(f"bxe{i}", [128, 2051 + 176], F32, esBw) for i in range(2)]
                    xcs_ = [sb(f"xc{i}", [128, NALL], F32, esBw) for i in range(2)]
                    xcb = sb("xcb", [128, NALL], BF16, esBw)
                    t1 = sb("t1", [128, NALL], F32, esBw)
                    t2 = sb("t2", [128, NALL], F32, esBw)
                    t3 = sb("t3", [128, NALL], F32, esBw)
                    gbg = [sb(f"gbg{i}", [128, NACT], F32, esBw) for i in range(2)]
                    small = sb("smallB", [128, 8], F32, esBw)
                    lruc_fm = sb("lruc_fm", [128, 8, 51], F32, esBw)
                    hst_fm = sb("hst_fm", [128, 8, 17], F32, esBw)
                    slruc_fm = sb("slruc_fm", [128, 8, 16, 3], F32, esBw)
                    slruh_fm = sb("slruh_fm", [128, 8, 16, 1], F32, esBw)
                    st_tm = sb("st_tmB", [64, CA], F32, esBw)
                    stg = sb("stgB", [64, CA], F32, esBw)

                    S.dma("pool", r_setup, lambda e: e.dma_start(out=wab[:], in_=wa_d.rearrange("h i j -> i h j")), writes=["wab"])
                    S.dma("pool", r_setup, lambda e: e.dma_start(out=wxb[:], in_=wx_d.rearrange("h i j -> i h j")), writes=["wxb"])
                    S.dma("sp", r_setup, lambda e: e.dma_start(out=st_tm[0:48, :], in_=st_lruc_d), writes=["st_tm_a"])
                    S.dma("sp", r_setup, lambda e: e.dma_start(out=st_tm[48:64, :], in_=st_h_d), writes=["st_tm_b"])
                    S.op("dve", lambda e: e.memset(bxe[0][:, 0:3], 0.0), writes=[("bxe", 0)])
                    S.op("dve", lambda e: e.memset(bxe[1][:, 0:3], 0.0), writes=[("bxe", 1)])
                    for j in range(8):
                        b = bank()
                        S.op("pe", lambda e, b=b, j=j: e.transpose(ps[b][0:128, 0:64], st_tm[0:64, j * 128:(j + 1) * 128],
                                                                   ident[0:64, 0:64]),
                             reads=["st_tm_a", "st_tm_b", "ident"], writes=[PK(b)])
                        S.op("act", act_copy(slruc_fm[:, j, :, :], ps[b][:, 0:48].rearrange("p (s k) -> p s k", k=3)),
                             reads=[PK(b)], writes=[("slruc", j)])
                        S.op("act", act_copy(slruh_fm[:, j, :, 0], ps[b][:, 48:64]),
                             reads=[PK(b)], writes=[("slruh", j)])
                        rel(b)

                    def lru_pe_in(j):
                        slot = j % 3
                        res = {"bx_pre": [], "bx_act": []}
                        for (c0, c1) in PRE_T:
                            b = bank()
                            for k in range(16):
                                S.op("pe", lambda e, b=b, k=k, c0=c0, c1=c1, slot=slot: e.matmul(
                                    ps[b][:, 0:c1 - c0], wbuf[:, slot, 0, k, :], xn_pre[:, k, c0:c1], start=(k == 0), stop=(k == 15)),
                                    reads=[("wB", slot, 0)] + xn_keys_pre(c0, c1, k), writes=[PK(b)])
                            res["bx_pre"].append((b, c0, c1))
                        for (a0, a1) in ACT_T:
                            b = bank()
                            for k in range(16):
                                S.op("pe", lambda e, b=b, k=k, a0=a0, a1=a1, slot=slot: e.matmul(
                                    ps[b][:, 0:a1 - a0], wbuf[:, slot, 0, k, :], xn_act[:, k, a0:a1], start=(k == 0), stop=(k == 15)),
                                    reads=[("wB", slot, 0)] + xn_keys_act(a0, a1, k), writes=[PK(b)])
                            res["bx_act"].append((b, a0, a1))
                        return res

                    def lru_pe_bg(j):
                        slot = j % 3
                        out = []
                        for (a0, a1) in ACT_T:
                            b = bank()
                            for k in range(16):
                                S.op("pe", lambda e, b=b, k=k, a0=a0, a1=a1, slot=slot: e.matmul(
                                    ps[b][:, 0:a1 - a0], wbuf[:, slot, 1, k, :], xn_act[:, k, a0:a1], start=(k == 0), stop=(k == 15)),
                                    reads=[("wB", slot, 1)] + xn_keys_act(a0, a1, k), writes=[PK(b)])
                            out.append((b, a0, a1))
                        return out

                    def lru_head(j, pe_in):
                        bx = bxe[j % 2]
                        bxs = bx[:, 2051:2227].rearrange("p (s t) -> p s t", t=11)
                        evk = [("bxe", j % 2)]
                        for n, (b, c0, c1) in enumerate(pe_in["bx_pre"]):
                            key = ("bxe_p", j % 2, n)
                            S.op("act", act_copy(bx[:, 3 + c0:3 + c1], ps[b][:, 0:c1 - c0]), reads=[PK(b)], writes=[key])
                            rel(b)
                            evk.append(key)
                        for n, (b, a0, a1) in enumerate(pe_in["bx_act"]):
                            key = ("bxe_a", j % 2, n)
                            if a1 <= 1056:
                                S.op("dve", dve_copy(bx[:, 3 + NPRE + a0:3 + NPRE + a1], ps[b][:, 0:a1 - a0]), reads=[PK(b)], writes=[key])
                            else:
                                S.op("dve", dve_copy(bxs[:, :, 3:11], ps[b][:, 0:128].rearrange("p (s t) -> p s t", t=8)),
                                     reads=[PK(b)], writes=[key])
                            rel(b)
                            evk.append(key)
                        S.op("dve", dve_copy(bxs[:, :, 0:3], slruc_fm[:, j, :, :]), reads=[("slruc", j)], writes=[("bxe_s", j % 2)])
                        evk.append(("bxe_s", j % 2))
                        S.op("act", act_copy(lruc_fm[:, j, 0:3], bx[:, 2048:2051]), reads=evk, writes=[("lruc_fm", j, 0)])
                        S.op("act", act_copy(lruc_fm[:, j, 3:51].rearrange("p (s k) -> p s k", k=3), bxs[:, :, 8:11]),
                             reads=evk, writes=[("lruc_fm", j, 1)])
                        return evk

                    def lru_conv(j, evk):
                        xc = xcs_[j % 2]
                        xkp, xks = ("xc_p", j % 2), ("xc_s", j % 2)
                        bx = bxe[j % 2]
                        bxs = bx[:, 2051:2227].rearrange("p (s t) -> p s t", t=11)
                        wv = lambda k: vecs[:, V_RW + j * 4 + k:V_RW + j * 4 + k + 1]
                        bv = vecs[:, V_RB + j:V_RB + j + 1]
                        xcs = xc[:, 2048:2176].rearrange("p (s t) -> p s t", t=8)
                        S.op("dve", lambda e: e.tensor_scalar(xc[:, 0:2048], bx[:, 0:2048], wv(0), bv, ALU.mult, ALU.add),
                             reads=evk + ["vecs"], writes=[xkp])
                        S.op("dve", lambda e: e.tensor_scalar(xcs, bxs[:, :, 0:8], wv(0), bv, ALU.mult, ALU.add),
                             reads=evk + ["vecs"], writes=[xks])
                        for k in range(1, 4):
                            S.op("dve", lambda e, k=k: e.scalar_tensor_tensor(xc[:, 0:2048], bx[:, k:k + 2048], wv(k), xc[:, 0:2048], ALU.mult, ALU.add),
                                 reads=evk + [xkp], writes=[xkp])
                            S.op("dve", lambda e, k=k: e.scalar_tensor_tensor(xcs, bxs[:, :, k:k + 8], wv(k), xcs, ALU.mult, ALU.add),
                                 reads=evk + [xks], writes=[xks])
                        S.op("act", act_copy(xcb[:], xc[:]), reads=[xkp, xks], writes=["xcb"])

                    def lru_gates(j):
                        for n, (c0, c1) in enumerate(ALL_T):
                            b = bank()
                            S.op("pe", lambda e, b=b, c0=c0, c1=c1: e.matmul(ps[b][:, 0:c1 - c0], wab[:, j, :], xcb[:, c0:c1], start=True, stop=True),
                                 reads=["wab", "xcb"], writes=[PK(b)])
                            S.op("act", lambda e, b=b, c0=c0, c1=c1: e.activation(out=t1[:, c0:c1], in_=ps[b][:, 0:c1 - c0], func=AF.Sigmoid,
                                                                                  bias=vecs[:, V_BA + j:V_BA + j + 1]),
                                 reads=[PK(b), "vecs"], writes=[("t1", n), "a"])
                            rel(b)
                        for n, (c0, c1) in enumerate(ALL_T):
                            b = bank()
                            S.op("pe", lambda e, b=b, c0=c0, c1=c1: e.matmul(ps[b][:, 0:c1 - c0], wxb[:, j, :], xcb[:, c0:c1], start=True, stop=True),
                                 reads=["wxb", "xcb"], writes=[PK(b)])
                            S.op("act", lambda e, b=b, c0=c0, c1=c1: e.activation(out=t3[:, c0:c1], in_=ps[b][:, 0:c1 - c0], func=AF.Sigmoid,
                                                                                  bias=vecs[:, V_BX + j:V_BX + j + 1]),
                                 reads=[PK(b), "vecs"], writes=[("t3", n), "bt", "ixc"])
                            rel(b)

                    def lru_gelu(j, bgb):
                        gb_ = gbg[j % 2]
                        for n, (b, a0, a1) in enumerate(bgb):
                            S.op("act", lambda e, b=b, a0=a0, a1=a1: e.activation(out=gb_[:, a0:a1], in_=ps[b][:, 0:a1 - a0], func=AF.Gelu_apprx_tanh),
                                 reads=[PK(b)], writes=[("gbg", j % 2, n)])
                            rel(b)

                    def lru_tail_act1(j):
                        t1k = [("t1", n) for n in range(5)]
                        S.op("act", lambda e: e.activation(out=t1[:], in_=t1[:], func=AF.Exp, scale=cvec[:, j:j + 1]),
                             reads=t1k + ["cvec"], writes=["a"])
                        S.op("act", lambda e: e.activation(out=t2[:], in_=t1[:], func=AF.Square), reads=["a", "t2"], writes=["t2"])

                    def lru_tail(j):
                        xc = xcs_[j % 2]
                        t3k = [("t3", n) for n in range(5)]
                        S.op("dve", lambda e: e.tensor_scalar(t2[:], t2[:], 1.0, None, ALU.min), reads=["t2"], writes=["t2"])
                        S.op("act", lambda e: e.activation(out=t2[:], in_=t2[:], func=AF.Sqrt, scale=-1.0, bias=1.0),
                             reads=["t2"], writes=["t2"])
                        S.op("dve", lambda e: e.tensor_tensor(t3[:], t3[:], xc[:], ALU.mult), reads=t3k + [("xc_p", j % 2), ("xc_s", j % 2)], writes=["ixc"])
                        S.op("dve", dve_copy(small[:, 0:1], t3[:, 0:1]), reads=["ixc"], writes=["sm0"])
                        S.op("dve", dve_copy(small[:, 1:2], t3[:, 1024:1025]), reads=["ixc"], writes=["sm1"])
                        S.op("dve", lambda e: e.tensor_tensor(t3[:], t3[:], t2[:], ALU.mult), reads=["ixc", "t2", "sm0", "sm1"], writes=["bt"])
                        S.op("dve", lambda e: e.tensor_scalar(small[:, 2:3], t2[:, 0:1], flags[:, 2:3], flags[:, 1:2], ALU.mult, ALU.add),
                             reads=["t2", "flags"], writes=["sm2"])
                        S.op("dve", lambda e: e.tensor_tensor(t3[:, 0:1], small[:, 0:1], small[:, 2:3], ALU.mult),
                             reads=["sm0", "sm2", "bt"], writes=["bt"])
                        S.op("dve", lambda e: e.tensor_scalar(t3[:, 0:1024], t3[:, 0:1024], flags[:, 0:1], None, ALU.mult),
                             reads=["bt", "flags"], writes=["bt"])
                        S.op("dve", lambda e: e.tensor_scalar(small[:, 3:4], t2[:, 1024:1025], flags[:, 4:5], flags[:, 3:4], ALU.mult, ALU.add),
                             reads=["t2", "flags"], writes=["sm3"])
                        S.op("dve", lambda e: e.tensor_tensor(t3[:, 1024:1025], small[:, 1:2], small[:, 3:4], ALU.mult),
                             reads=["sm1", "sm3", "bt"], writes=["bt"])
                        av = t1[:, 2048:2176].rearrange("p (s t) -> p s t", t=8)[:, :, 0:1]
                        btv = t3[:, 2048:2176].rearrange("p (s t) -> p s t", t=8)[:, :, 0:1]
                        h0 = slruh_fm[:, j, :, :]
                        S.op("dve", lambda e: e.tensor_tensor(av, av, h0, ALU.mult), reads=["a", ("slruh", j)], writes=["a"])
                        S.op("dve", lambda e: e.tensor_tensor(btv, btv, av, ALU.add), reads=["a", "bt"], writes=["bt"])
                        S.op("dve", lambda e: e.memset(av, 0.0), reads=["bt"], writes=["a"])
                        S.op("dve", lambda e: e.tensor_tensor_scan(t2[:], t1[:], t3[:], 0.0, ALU.mult, ALU.add),
                             reads=["a", "bt", "t2"], writes=["t2"])
                        S.op("dve", dve_copy(hst_fm[:, j, 0:1], t2[:, 2047:2048]), reads=["t2"], writes=[("hst", j, 0)])
                        S.op("dve", dve_copy(hst_fm[:, j, 1:17], t2[:, 2048:2176].rearrange("p (s t) -> p s t", t=8)[:, :, 7]),
                             reads=["t2"], writes=[("hst", j, 1)])
                        S.op("dve", lambda e: e.tensor_tensor(catB[:, j, :], t2[:, NPRE:NALL], gbg[j % 2][:], ALU.mult),
                             reads=["t2"] + [("gbg", j % 2, n) for n in range(4)], writes=[("cat", 8 + j)])

                    evks = {}
                    evks[0] = lru_head(0, lru_pe_in(0))
                    lru_gelu(0, lru_pe_bg(0))
                    lru_conv(0, evks[0])
                    for j in range(8):
                        if j + 2 < 8:
                            loadB(j + 2)
                        if j + 1 < 8:
                            evks[j + 1] = lru_head(j + 1, lru_pe_in(j + 1))
                            lru_gelu(j + 1, lru_pe_bg(j + 1))
                        lru_gates(j)
                        lru_tail_act1(j)
                        if j + 1 < 8 and not DEBUG.get("on"):
                            lru_conv(j + 1, evks[j + 1])
                        lru_tail(j)
                        if j == 0 and DEBUG.get("on"):
                            allxn = [("xn", i, kq) for i in list(range(17)) + [70] for kq in range(4)]
                            dbg_dump16("xn_act0", xn_act[:, 0, :], NACT, allxn)
                            dbg_dump16("xn_act15", xn_act[:, 15, :], NACT, allxn)
                            dbg_dump16("xn_pre3", xn_pre[:, 3, :], NPRE, allxn)
                            dbg_dump16("w0", wbuf[:, 0, 0].rearrange("p k c -> p (k c)"), 2048, [("wB", 0, 0)])
                            dbg_dump("bxe0", bxe[0][:], [128, 2227], F32, [("bxe", 0), ("bxe_s", 0)] + [("bxe_p", 0, n) for n in range(2)] + [("bxe_a", 0, n) for n in range(4)])
                            dbg_dump("xc", xcs_[0][:], [128, NALL], F32, [("xc_p", 0), ("xc_s", 0)])
                            dbg_dump("a", t1[:], [128, NALL], F32, ["a"])
                            dbg_dump("h", t2[:], [128, NALL], F32, ["t2"])
                            dbg_dump("bt", t3[:], [128, NALL], F32, ["bt"])
                            dbg_dump("gbg", gbg[0][:], [128, NACT], F32, [("gbg", 0, n) for n in range(4)])
                            dbg_dump16("catB0", catB[:, 0, :], NACT, [("cat", 8)])
                            dbg_stop("head0")
                        if j + 1 < 8 and DEBUG.get("on"):
                            lru_conv(j + 1, evks[j + 1])

                    for j in range(8):
                        tr_out(lruc_fm[:, j, :], 51, stg[0:51, j * 128:(j + 1) * 128], [("lruc_fm", j, 0), ("lruc_fm", j, 1)], ("stgB", j))
                    kk = [("stgB", j) for j in range(8)]
                    S.dma("sp", r_out, lambda e: e.dma_start(out=p_lruc_d, in_=stg[0:3, :]), reads=kk, is_out=True)
                    S.dma("sp", r_out, lambda e: e.dma_start(out=s_lruc_d, in_=stg[3:51, :]), reads=kk, is_out=True)
                    for j in range(8):
                        tr_out(hst_fm[:, j, :], 17, st_tm[0:17, j * 128:(j + 1) * 128], [("hst", j, 0), ("hst", j, 1)], ("stgH", j))
                    kk = [("stgH", j) for j in range(8)]
                    S.dma("sp", r_out, lambda e: e.dma_start(out=p_h_d, in_=st_tm[0:1, :]), reads=kk + ["st_tm_a", "st_tm_b"], is_out=True)
                    S.dma("sp", r_out, lambda e: e.dma_start(out=s_h_d, in_=st_tm[1:17, :]), reads=kk + ["st_tm_a", "st_tm_b"], is_out=True)

            S.fence()
            with ExitStack() as esA12:
                catA = sb("catA", [128, 8, NACT], BF16, esA12)
                with ExitStack() as esA:
                    caA = sb("caA", [128, 8, NACT], F32, esA)
                    wbufA = sb("wbufA", [128, 3, 2, 16, 128], BF16, esA)
                    A1 = sb("A1", [128, NACT], F32, esA)
                    A2 = sb("A2", [128, NACT], F32, esA)
                    A3 = sb("A3", [128, NACT], F32, esA)
                    A3b = sb("A3b", [128, NACT], F32, esA)
                    UE = 1086 + 608
                    ubf = [sb(f"ubf{i}", [128, UE], BF16, esA) for i in range(2)]
                    diag = sb("diag", [128, 31, 128], BF16, esA)
                    sconf_fm = sb("sconf_fm", [128, 8, 38, 16], F32, esA)
                    pconf_fm = sb("pconf_fm", [128, 8, 30], F32, esA)
                    st_tmA = sb("st_tmA", [120, CA], F32, esA)
                    stgS = sb("stgS", [128, CA], F32, esA)

                    st_tmA2 = sb("st_tmA2", [120, CA], F32, esA)

                    def conf_state_in():
                        for q in range(4):
                            stb = st_tmA if q % 2 == 0 else st_tmA2
                            sk = "st_tmA" if q % 2 == 0 else "st_tmA2"
                            S.dma("sp", r_ld, lambda e, q=q, stb=stb: e.dma_start(out=stb[:, :], in_=st_conf_d[120 * q:120 * q + 120, :]),
                                  writes=[sk])
                            for j in range(8):
                                b = bank()
                                S.op("pe", lambda e, b=b, j=j, stb=stb: e.transpose(ps[b][0:128, 0:120], stb[0:120, j * 128:(j + 1) * 128],
                                                                           ident[0:120, 0:120]),
                                     reads=[sk, "ident"], writes=[PK(b)])
                                S.op("act", act_copy(sconf_fm[:, j, 0:30, 4 * q:4 * q + 4].rearrange("p k s -> p s k"), ps[b][:, 0:120].rearrange("p (s k) -> p s k", k=30)),
                                     reads=[PK(b)], writes=[("sconf_old", j, q)])
                                rel(b)

                    for i in range(2):
                        S.op("dve", lambda e, i=i: e.memset(ubf[i][:, 0:30], 0.0), writes=[("ubf", i)])

                    def loadA(jj):
                        load_w_pair(jj % 3, 128 * jj, 1024 + 128 * jj, wbufA, "wA")

                    def convA_pe_in(j, q):
                        slot = j % 3
                        lst = []
                        if True:
                            for (a0, a1) in ACT_T:
                                b = bank()
                                for k in range(16):
                                    S.op("pe", lambda e, b=b, k=k, a0=a0, a1=a1, slot=slot, q=q: e.matmul(
                                        ps[b][:, 0:a1 - a0], wbufA[:, slot, q, k, :], xn_act[:, k, a0:a1], start=(k == 0), stop=(k == 15)),
                                        reads=[("wA", slot, q)] + xn_keys_act(a0, a1, k), writes=[PK(b)])
                                lst.append((b, a0, a1))
                        return lst

                    def convA_mid1(j, vb, gb):
                        for k in range(31):
                            S.op("dve", lambda e, k=k: e.tensor_scalar(diag[:, k, :], identb[:], vecs[:, V_CW + j * 31 + k:V_CW + j * 31 + k + 1], None, ALU.mult),
                                 reads=["identb", "vecs"], writes=["diag"])
                        for n, (b, a0, a1) in enumerate(gb):
                            S.op("act", lambda e, b=b, a0=a0, a1=a1: e.activation(out=A1[:, a0:a1], in_=ps[b][:, 0:a1 - a0], func=AF.Sigmoid),
                                 reads=[PK(b)], writes=[("A1", n)])
                            rel(b)
                        for n, (b, a0, a1) in enumerate(vb):
                            S.op("dve", lambda e, b=b, a0=a0, a1=a1: e.tensor_tensor(A2[:, a0:a1], ps[b][:, 0:a1 - a0], A1[:, a0:a1], ALU.mult),
                                 reads=[PK(b), ("A1", n)], writes=[("A2", n)])
                            rel(b)

                    def convA_mid2(j):
                        u = ubf[j % 2]
                        ufk = [("A2", n) for n in range(4)]
                        S.op("act", act_copy(sconf_fm[:, j, 30:38, :], A2[:, 1056:1184].rearrange("p (s t) -> p t s", t=8)),
                             reads=ufk, writes=[("sconf_new", j)])
                        S.op("act", act_copy(u[:, 30:1086], A2[:, 0:1056]), reads=ufk, writes=[("ubf_p", j % 2)])
                        S.op("act", act_copy(u[:, 1086:UE], sconf_fm[:, j, :, :].rearrange("p t s -> p (t s)")),
                             reads=[("sconf_new", j)] + [("sconf_old", j, q) for q in range(4)], writes=[("ubf_s", j % 2)])
                        S.op("act", act_copy(pconf_fm[:, j, :], A2[:, 1026:1056]), reads=ufk, writes=[("pconf", j)])

                    CONV_T = [(0, 352), (352, 704), (704, 1056)]

                    def convA_pe_conv(j):
                        u = ubf[j % 2]
                        outb = []
                        rk = ["diag", ("ubf_p", j % 2), ("ubf_s", j % 2), ("ubf", j % 2)]
                        for (a0, a1) in CONV_T:
                            b = bank()
                            for k in range(31):
                                S.op("pe", lambda e, b=b, k=k, a0=a0, a1=a1: e.matmul(
                                    ps[b][:, 0:a1 - a0], diag[:, k, :], u[:, a0 + k:a1 + k], start=(k == 0), stop=(k == 30)),
                                    reads=rk, writes=[PK(b)])
                            outb.append((b, a0, a1))
                        b = bank()
                        for k in range(31):
                            S.op("pe", lambda e, b=b, k=k: e.matmul(
                                ps[b][:, 0:128], diag[:, k, :], u[:, 1086 + 16 * k:1086 + 16 * k + 128], start=(k == 0), stop=(k == 30)),
                                reads=rk, writes=[PK(b)])
                        outb.append((b, 1056, 1184))
                        return outb

                    def convA_tail(j, outb):
                        for n, (b, a0, a1) in enumerate(outb):
                            if n < 3:
                                dst, src = caA[:, j, a0:a1], ps[b][:, 0:a1 - a0]
                            else:
                                dst = caA[:, j, a0:a1].rearrange("p (s t) -> p t s", t=8)
                                src = ps[b][:, 0:128].rearrange("p (t s) -> p t s", s=16)
                            S.op("act", lambda e, dst=dst, src=src: e.activation(out=dst, in_=src, func=AF.Identity,
                                                                                  bias=vecs[:, V_CB + j:V_CB + j + 1]),
                                 reads=[PK(b), "vecs"], writes=[("caA", j, n)])
                            rel(b)
                        cak = [("caA", j, n) for n in range(4)]
                        a1k = [("A1", n) for n in range(4)]
                        if j == 0:
                            S.op("dve", dve_copy(A3b[:], caA[:, 0, :]), reads=cak, writes=["A3b"])
                            S.op("act", lambda e: e.activation(out=A3[:], in_=caA[:, 0, :], func=AF.Square), reads=cak, writes=["A3"])
                        else:
                            S.op("act", lambda e: e.activation(out=A1[:], in_=caA[:, j, :], func=AF.Square), reads=cak, writes=a1k)
                            S.op("dve", lambda e: e.tensor_tensor(A3b[:], A3b[:], caA[:, j, :], ALU.add), reads=cak + ["A3b"], writes=["A3b"])
                            S.op("dve", lambda e: e.tensor_tensor(A3[:], A3[:], A1[:], ALU.add), reads=a1k + ["A3"], writes=["A3"])

                    loadA(0)
                    loadA(1)
                    gbs = {0: convA_pe_in(0, 1)}
                    for j in range(8):
                        if j + 2 < 8:
                            loadA(j + 2)
                        vb = convA_pe_in(j, 0)
                        convA_mid1(j, vb, gbs[j])
                        if j + 1 < 8:
                            gbs[j + 1] = convA_pe_in(j + 1, 1)
                        if j == 0:
                            conf_state_in()
                        convA_mid2(j)
                        convA_tail(j, convA_pe_conv(j))

                    for j in range(8):
                        tr_out(pconf_fm[:, j, :], 30, st_tmA[0:30, j * 128:(j + 1) * 128], [("pconf", j)], "st_tmA")
                    S.dma("sp", r_out, lambda e: e.dma_start(out=p_conf_d, in_=st_tmA[0:30, :]), reads=["st_tmA"], is_out=True)
                    s_conf_v = s_conf_d.rearrange("(s k) c -> k s c", k=30)
                    for bi in range(4):
                        nk = 8 if bi < 3 else 6
                        for j in range(8):
                            tr_out(sconf_fm[:, j, 8 + 8 * bi:8 + 8 * bi + nk, :].rearrange("p t s -> p (t s)"), nk * 16,
                                   stgS[0:nk * 16, j * 128:(j + 1) * 128],
                                   [("sconf_new", j)] + [("sconf_old", j, qq) for qq in range(4)], "stgS")
                        for kl in range(nk):
                            S.dma("sp", r_out, lambda e, bi=bi, kl=kl: e.dma_start(out=s_conf_v[8 * bi + kl], in_=stgS[16 * kl:16 * kl + 16, :]),
                                  reads=["stgS"], is_out=True)

                    sumb, sq_banks = [], []
                    for n, (a0, a1) in enumerate(ACT_T):
                        b = bank()
                        S.op("pe", lambda e, b=b, a0=a0, a1=a1: e.matmul(ps[b][:, 0:a1 - a0], ones[:], A3b[:, a0:a1], start=True, stop=True),
                             reads=["ones", "A3b"], writes=[PK(b)])
                        sumb.append(b)
                        b = bank()
                        S.op("pe", lambda e, b=b, a0=a0, a1=a1: e.matmul(ps[b][:, 0:a1 - a0], ones[:], A3[:, a0:a1], start=True, stop=True),
                             reads=["ones", "A3"], writes=[PK(b)])
                        sq_banks.append(b)
                    for n, (a0, a1) in enumerate(ACT_T):
                        b = sumb[n]
                        b2 = sq_banks[n]
                        S.op("act", lambda e, b=b, a0=a0, a1=a1: e.activation(out=A1[:, a0:a1], in_=ps[b][:, 0:a1 - a0], func=AF.Copy, scale=1.0 / CA),
                             reads=[PK(b)], writes=[("A1", n)])
                        S.op("dve", lambda e, a0=a0, a1=a1: e.tensor_tensor(A3[:, a0:a1], A1[:, a0:a1], A1[:, a0:a1], ALU.mult),
                             reads=[("A1", n)], writes=[("A3m", n), "A3"])
                        S.op("dve", lambda e, b2=b2, a0=a0, a1=a1: e.scalar_tensor_tensor(A2[:, a0:a1], ps[b2][:, 0:a1 - a0], 1.0 / CA, A3[:, a0:a1], ALU.mult, ALU.subtract),
                             reads=[PK(b2), ("A3m", n), ("A2", n)], writes=[("A2", n)])
                        rel(b, b2)
                        S.op("act", lambda e, a0=a0, a1=a1: e.activation(out=A2[:, a0:a1], in_=A2[:, a0:a1], func=AF.Sqrt, bias=EPS),
                             reads=[("A2", n)], writes=[("A2", n)])
                        S.op("dve", lambda e, a0=a0, a1=a1: e.reciprocal(A2[:, a0:a1], A2[:, a0:a1]),
                             reads=[("A2", n)], writes=[("A2", n)])
                    mk = [("A1", n) for n in range(4)]
                    rk2 = [("A2", n) for n in range(4)]
                    for j in range(8):
                        zb = A3 if j % 2 == 0 else A3b
                        zk = "A3" if j % 2 == 0 else "A3b"
                        S.op("dve", lambda e, j=j, zb=zb: e.tensor_tensor(zb[:], caA[:, j, :], A1[:], ALU.subtract),
                             reads=mk + [("caA", j, n) for n in range(4)] + [("A3m", n) for n in range(4)] + [zk], writes=[zk])
                        S.op("dve", lambda e, zb=zb: e.tensor_tensor(zb[:], zb[:], A2[:], ALU.mult), reads=rk2 + [zk], writes=[zk])
                        S.op("act", lambda e, j=j, zb=zb: e.activation(out=catA[:, j, :], in_=zb[:], func=AF.Silu,
                                                                scale=vecs[:, V_LG + j:V_LG + j + 1], bias=vecs[:, V_LB + j:V_LB + j + 1]),
                             reads=[zk, "vecs"], writes=[("cat", j)])

                S.fence()
                xn2 = xn_act
                with ExitStack() as es2:
                    wo = sb("wo", [128, 16, D], BF16, es2)
                    grow1 = sb("grow1", [128, D], F32, es2)
                    grow2 = sb("grow2", [128, D], F32, es2)
                    xt2 = [sb(f"xt2_{i}", [128, D], F32, es2) for i in range(2)]
                    ht = sb("ht", [128, D], F32, es2)
                    xs2 = sb("xs2", [128, D], BF16, es2)
                    ss2 = sb("ss2", [128, 8], F32, es2)
                    S.dma("sp", r_setup, lambda e: e.dma_start(out=grow1[:], in_=grows_d[1:2, :].partition_broadcast(128)), writes=["grow1"])
                    S.dma("sp", r_setup, lambda e: e.dma_start(out=grow2[:], in_=grows_d[2:3, :].partition_broadcast(128)), writes=["grow2"])
                    for fg in range(4):
                        for kh in range(2):
                            S.dma("pool", r_w, lambda e, fg=fg, kh=kh: e.dma_start(
                                out=wo[:, 8 * kh:8 * kh + 8, fg * 512:(fg + 1) * 512],
                                in_=w_out[1024 * kh:1024 * kh + 1024, fg * 512:(fg + 1) * 512].rearrange("(k p) c -> p k c", p=128)),
                                writes=[("wo", fg, kh)])
                    tiles = [(NPRE, 32, 0, None)] + [(1024 + 128 * t, 128, 32 + 128 * t, 128 * t) for t in range(9)]
                    xn_all_keys = [("xn", i, kq) for i in list(range(8, 17)) + [70] for kq in range(4)]
                    junk2 = sb("junk2", [128, D], BF16, es2)

                    def p2_mm(ti):
                        r0, M, a0, hr = tiles[ti]
                        sl = ti % 2
                        S.dma("sp", r_x, lambda e: e.dma_start(out=xt2[sl][0:M, :], in_=xall[r0:r0 + M, :]), writes=[("xt2", sl)])
                        mb = []
                        for fg in range(4):
                            b = bank_at((ti % 2) * 4 + fg)
                            for k in range(16):
                                S.op("pe", lambda e, b=b, k=k, fg=fg: e.matmul(
                                    ps[b][0:M, :], cat(k)[:, a0:a0 + M], wo[:, k, fg * 512:(fg + 1) * 512], start=(k == 0), stop=(k == 15)),
                                    reads=[("cat", k), ("wo", fg, k // 8)], writes=[PK(b)])
                            mb.append(b)
                        return mb

                    def p2_norm(ti, mb):
                        r0, M, a0, hr = tiles[ti]
                        sl = ti % 2
                        for fg, b in enumerate(mb):
                            S.op("act", lambda e, b=b, fg=fg: e.activation(out=junk2[0:M, fg * 512:(fg + 1) * 512], in_=ps[b][0:M, :], func=AF.Square,
                                                                            accum_out=ss2[0:M, fg:fg + 1]),
                                 reads=[PK(b)], writes=[("ss2a", fg)])
                        S.op("dve", lambda e: e.tensor_reduce(ss2[0:M, 4:5], ss2[0:M, 0:4], mybir.AxisListType.X, ALU.add),
                             reads=[("ss2a", fg) for fg in range(4)], writes=["ss2b"])
                        S.op("act", lambda e: e.activation(out=ss2[0:M, 5:6], in_=ss2[0:M, 4:5], func=AF.Sqrt, scale=1.0 / D, bias=EPS),
                             reads=["ss2b"], writes=["ss2c"])
                        S.op("dve", lambda e: e.reciprocal(ss2[0:M, 5:6], ss2[0:M, 5:6]), reads=["ss2c"], writes=["ss2d"])
                        for fg, b in enumerate(mb):
                            S.op("dve", lambda e, b=b, fg=fg: e.scalar_tensor_tensor(ht[0:M, fg * 512:(fg + 1) * 512], ps[b][0:M, :], ss2[0:M, 5:6],
                                                                                  grow1[0:M, fg * 512:(fg + 1) * 512], ALU.mult, ALU.mult),
                                 reads=[PK(b), "ss2d", "grow1"], writes=[("ht0", fg), "ht"])
                            rel(b)
                        S.op("dve", lambda e: e.tensor_tensor(ht[0:M, :], ht[0:M, :], xt2[sl][0:M, :], ALU.add),
                             reads=[("ht0", fg) for fg in range(4)] + [("xt2", sl)], writes=["ht"])
                        if hr is not None:
                            S.dma("sp", r_st, lambda e: e.dma_start(out=h_scr[hr:hr + 128, :], in_=ht[:, :]), reads=["ht"], writes=[("h_scr", hr)])

                    def p2_xn2(ti):
                        r0, M, a0, hr = tiles[ti]
                        S.op("act", lambda e: e.activation(out=junk2[0:M, :], in_=ht[0:M, :], func=AF.Square, accum_out=ss2[0:M, 6:7]),
                             reads=["ht"], writes=["ss2e"])
                        S.op("act", lambda e: e.activation(out=ss2[0:M, 7:8], in_=ss2[0:M, 6:7], func=AF.Sqrt, scale=1.0 / D, bias=EPS),
                             reads=["ss2e"], writes=["ss2f"])
                        S.op("dve", lambda e: e.reciprocal(ss2[0:M, 7:8], ss2[0:M, 7:8]), reads=["ss2f"], writes=["ss2g"])
                        S.op("dve", lambda e: e.scalar_tensor_tensor(xs2[0:M, :], ht[0:M, :], ss2[0:M, 7:8], grow2[0:M, :], ALU.mult, ALU.mult),
                             reads=["ht", "ss2g", "grow2"], writes=["xs2"])
                        for kq in range(4):
                            b = bank_at((ti % 2) * 4 + kq)
                            pb = ps[b][:].bitcast(BF16)
                            for jq in range(4):
                                k = kq * 4 + jq
                                S.op("pe", lambda e, k=k, jq=jq, pb=pb: e.transpose(pb[:, jq * 128:jq * 128 + M], xs2[0:M, k * 128:(k + 1) * 128], identb[0:M, 0:M]),
                                     reads=["xs2", "identb"], writes=[PK(b)])
                            pv = pb[:, 0:512].rearrange("p (j t) -> p j t", t=128)
                            eng = "act" if kq % 2 == 0 else "dve"
                            S.op(eng, evac(eng, xn2[:, kq * 4:kq * 4 + 4, a0:a0 + M], pv[:, :, 0:M]),
                                 reads=[PK(b)], writes=[("xn2", ti, kq)] + (xn_all_keys if ti == 0 and kq < 2 else []))
                            rel(b)

                    mbs = {0: p2_mm(0)}
                    for ti in range(len(tiles)):
                        p2_norm(ti, mbs[ti])
                        if ti + 1 < len(tiles):
                            mbs[ti + 1] = p2_mm(ti + 1)
                        p2_xn2(ti)

        def xn2_keys(a0, a1, k):
            keys = []
            if a0 < 32:
                keys.append(("xn2", 0, k // 4))
            lo = max(a0, 32)
            if a1 > 32:
                for t in range((lo - 32) // 128, (a1 - 1 - 32) // 128 + 1):
                    keys.append(("xn2", 1 + t, k // 4))
            return keys

        S.fence()
        with ExitStack() as es3:
            ffn_fm = sb("ffn_fm", [128, 96, 34], F32, es3)
            hmid = sb("hmid", [128, 24, 1152], BF16, es3)
            wpool = sb("wpool", [128, 6, 4096], BF16, es3)
            EXT = 1026 + 160
            ge = [sb(f"ge{i}", [128, EXT], F32, es3) for i in range(2)]
            ve = [sb(f"ve{i}", [128, EXT], F32, es3) for i in range(2)]
            og = sb("og", [128, 1152], F32, es3)
            ov = sb("ov", [128, 1152], F32, es3)
            dstg = [sb(f"dstg{i}", [128, 512], F32, es3) for i in range(3)]
            prv = [sb(f"prv{i}", [128, 512], F32, es3) for i in range(2)]
            fsts = [sb(f"fst{i}", [34, 1024], F32, es3) for i in range(2)]

            UP_T = [(30, 542), (542, 1054), (1054, 1184)]
            upstate = {"n": 0}

            def load_up(jg):
                unit = upstate["n"] % 6
                upstate["n"] += 1
                for q, c in enumerate((128 * jg, DFF + 128 * jg)):
                    S.dma("pool", r_w, lambda e, q=q, c=c, unit=unit: e.dma_start(
                        out=wpool[:, unit, q * 2048:(q + 1) * 2048].rearrange("p (k c) -> p k c", c=128),
                        in_=w_up[:, c:c + 128].rearrange("(k p) c -> p k c", p=128)),
                        writes=[("wp", unit, q)])
                return unit

            def up_pe(jg, unit):
                res = []
                for q in range(2):
                    wv = wpool[:, unit, q * 2048:(q + 1) * 2048].rearrange("p (k c) -> p k c", c=128)
                    lst = []
                    for (a0, a1) in UP_T:
                        b = bank()
                        for k in range(16):
                            S.op("pe", lambda e, b=b, k=k, a0=a0, a1=a1, wv=wv: e.matmul(
                                ps[b][:, 0:a1 - a0], wv[:, k, :], xn2[:, k, a0:a1], start=(k == 0), stop=(k == 15)),
                                reads=[("wp", unit, q)] + xn2_keys(a0, a1, k), writes=[PK(b)])
                        lst.append(b)
                    res.append(lst)
                return res

            def up_evac(jg, jj, banks):
                for q, (exts, ekey) in enumerate(((ge, "ge"), (ve, "ve"))):
                    ext = exts[jj % 2]
                    ch = jg + 48 * q
                    bA, bB, bC = banks[q]
                    exs = ext[:, 1026:EXT].rearrange("p (s t) -> p s t", t=10)
                    eng = "act" if q == 0 else "dve"
                    sfx = (ekey, jj % 2)
                    S.op(eng, evac(eng, ext[:, 0:512], ps[bA][:, 0:512]), reads=[PK(bA)], writes=[sfx + (0,)])
                    S.op(eng, evac(eng, ext[:, 512:1024], ps[bB][:, 0:512]), reads=[PK(bB)], writes=[sfx + (1,)])
                    S.op("act", act_copy(ext[:, 1024:1026], ps[bC][:, 0:2]), reads=[PK(bC)], writes=[sfx + (2,)])
                    S.op("act", act_copy(exs[:, :, 2:10], ps[bC][:, 2:130].rearrange("p (s t) -> p s t", t=8)), reads=[PK(bC)], writes=[sfx + (3,)])
                    rel(bA, bB, bC)
                    S.op("act", act_copy(exs[:, :, 0:2], ffn_fm[:, ch, 0:32].rearrange("p (s k) -> p s k", k=2)), reads=[("ffn_fm", ch)], writes=[sfx + (4,)])
                    S.op("dve", lambda e, ext=ext: e.tensor_scalar(ext[:, 0:2], ext[:, 0:2], flags[:, 0:1], None, ALU.mult),
                         reads=[sfx + (0,), "flags"], writes=[sfx + (0,)])
                    ek = [sfx + (n,) for n in range(5)]
                    S.op("act", act_copy(ffn_fm[:, ch, 0:32].rearrange("p (s k) -> p s k", k=2), exs[:, :, 8:10]), reads=ek, writes=[("ffn_new", ch)])
                    S.op("act", act_copy(ffn_fm[:, ch, 32:34], ext[:, 1024:1026]), reads=ek, writes=[("ffn_newp", ch)])

            def up_tail(jg, jj):
                for q, (exts, o, ekey, okey) in enumerate(((ge, og, "ge", "og"), (ve, ov, "ve", "ov"))):
                    ext = exts[jj % 2]
                    ch = jg + 48 * q
                    exs = ext[:, 1026:EXT].rearrange("p (s t) -> p s t", t=10)
                    ek = [(ekey, jj % 2, n) for n in range(5)]
                    wv = lambda k, ch=ch: vecs[:, V_FW + ch * 3 + k:V_FW + ch * 3 + k + 1]
                    bv = vecs[:, V_FB + ch:V_FB + ch + 1]
                    os_ = o[:, 1024:1152].rearrange("p (s t) -> p s t", t=8)
                    S.op("dve", lambda e, ext=ext, o=o, wv=wv, bv=bv: e.tensor_scalar(o[:, 0:1024], ext[:, 0:1024], wv(0), bv, ALU.mult, ALU.add),
                         reads=ek + ["vecs", okey], writes=[(okey, 0)])
                    S.op("dve", lambda e, exs=exs, os_=os_, wv=wv, bv=bv: e.tensor_scalar(os_, exs[:, :, 0:8], wv(0), bv, ALU.mult, ALU.add),
                         reads=ek + ["vecs", okey], writes=[(okey, 1)])
                    for k in (1, 2):
                        S.op("dve", lambda e, ext=ext, o=o, wv=wv, k=k: e.scalar_tensor_tensor(o[:, 0:1024], ext[:, k:k + 1024], wv(k), o[:, 0:1024], ALU.mult, ALU.add),
                             reads=ek + [(okey, 0)], writes=[(okey, 0)])
                        S.op("dve", lambda e, exs=exs, os_=os_, wv=wv, k=k: e.scalar_tensor_tensor(os_, exs[:, :, k:k + 8], wv(k), os_, ALU.mult, ALU.add),
                             reads=ek + [(okey, 1)], writes=[(okey, 1)])
                S.op("act", lambda e: e.activation(out=og[:], in_=og[:], func=AF.Gelu_apprx_tanh), reads=[("og", 0), ("og", 1)], writes=["og"])
                S.op("dve", lambda e, jj=jj: e.tensor_tensor(hmid[:, jj, :], og[:], ov[:], ALU.mult),
                     reads=["og", ("ov", 0), ("ov", 1)], writes=[("hmid", jj), "ov"])

            def load_down(g, fg, half):
                units = (0, 1, 2) if half == 0 else (3, 4, 5)
                for u3, unit in enumerate(units):
                    for hh in range(2):
                        kk0 = u3 * 8 + hh * 4
                        r0 = g * 3072 + kk0 * 128
                        S.dma("pool", r_w, lambda e, unit=unit, hh=hh, r0=r0, fg=fg: e.dma_start(
                            out=wpool[:, unit, hh * 2048:(hh + 1) * 2048].rearrange("p (k c) -> p k c", c=512),
                            in_=w_down[r0:r0 + 512, fg * 512:(fg + 1) * 512].rearrange("(k p) c -> p k c", p=128)),
                            writes=[("wp", unit, hh)])
                return units

            def down_phase(g):
                cnt = 0
                units = {0: load_down(g, 0, 0), 1: load_down(g, 1, 1)}
                for fg in range(4):
                    un = units[fg]
                    for tt in range(9):
                        b = bank()
                        for kk in range(24):
                            unit = un[kk // 8]
                            hh = (kk % 8) // 4
                            wv = wpool[:, unit, hh * 2048:(hh + 1) * 2048].rearrange("p (k c) -> p k c", c=512)
                            S.op("pe", lambda e, b=b, kk=kk, wv=wv, tt=tt: e.matmul(
                                ps[b][:, :], hmid[:, kk, tt * 128:(tt + 1) * 128], wv[:, kk % 4, :], start=(kk == 0), stop=(kk == 23)),
                                reads=[("wp", unit, hh), ("hmid", kk)], writes=[PK(b)])
                        ds = dstg[cnt % 3]
                        dkey = ("dstg", cnt % 3)
                        fkey = ("f_scr", tt, fg)
                        dst = f_scr[tt * 128:(tt + 1) * 128, fg * 512:(fg + 1) * 512]
                        if g == 0:
                            S.op("act", act_copy(ds[:], ps[b][:, :]), reads=[PK(b)], writes=[dkey])
                            rel(b)
                        else:
                            pv = prv[cnt % 2]
                            pkey = ("prv", cnt % 2)
                            S.dma("sp", r_ld, lambda e, pv=pv, dst=dst: e.dma_start(out=pv[:], in_=dst), reads=[fkey], writes=[pkey])
                            S.op("dve", lambda e, ds=ds, pv=pv, b=b: e.tensor_tensor(ds[:], ps[b][:, :], pv[:], ALU.add),
                                 reads=[PK(b), pkey], writes=[dkey])
                            rel(b)
                        S.dma("sp", r_st, lambda e, ds=ds, dst=dst: e.dma_start(out=dst, in_=ds[:]), reads=[dkey], writes=[fkey])
                        cnt += 1
                    if fg + 2 < 4:
                        units[fg + 2] = load_down(g, fg + 2, fg % 2)

            def ffn_state_in(qs):
                for qi, q in enumerate(qs):
                    fst = fsts[qi % 2]
                    fk = ("fst", qi % 2)
                    S.dma("sp", r_ld, lambda e, q=q, fst=fst: e.dma_start(out=fst[0:32, :], in_=st_ffn_d[:, 1024 * q:1024 * (q + 1)]), writes=[fk])
                    for jj in range(8):
                        ch = q * 8 + jj
                        b = bank()
                        S.op("pe", lambda e, b=b, jj=jj, fst=fst: e.transpose(ps[b][0:128, 0:32], fst[0:32, jj * 128:(jj + 1) * 128], ident[0:32, 0:32]),
                             reads=[fk, "ident"], writes=[PK(b)])
                        S.op("act", act_copy(ffn_fm[:, ch, 0:32], ps[b][:, 0:32]),
                             reads=[PK(b)], writes=[("ffn_fm", ch)])
                        rel(b)

            for g in range(2):
                pend = {}
                pend[0] = load_up(24 * g)
                pend[1] = load_up(24 * g + 1)
                pend[2] = load_up(24 * g + 2)
                banks = {0: up_pe(24 * g, pend[0])}
                if g == 0:
                    ffn_state_in((0, 6))
                for jj in range(24):
                    if g == 0 and jj in (1, 3, 5, 7, 9):
                        qq = (jj + 1) // 2
                        ffn_state_in((qq, 6 + qq))
                    if jj + 3 < 24:
                        pend[jj + 3] = load_up(24 * g + jj + 3)
                    up_evac(24 * g + jj, jj, banks[jj])
                    if jj + 1 < 24:
                        banks[jj + 1] = up_pe(24 * g + jj + 1, pend[jj + 1])
                    up_tail(24 * g + jj, jj)
                if g == 1:
                    for q in range(12):
                        fst = fsts[q % 2]
                        fk = ("fst", q % 2)
                        for jj in range(8):
                            ch = q * 8 + jj
                            tr_out(ffn_fm[:, ch, :], 34, fst[0:34, jj * 128:(jj + 1) * 128], [("ffn_new", ch), ("ffn_newp", ch)], fk)
                        S.dma("sp", r_out, lambda e, q=q, fst=fst: e.dma_start(out=s_ffn_d[:, 1024 * q:1024 * (q + 1)], in_=fst[0:32, :]), reads=[fk], is_out=True)
                        S.dma("sp", r_out, lambda e, q=q, fst=fst: e.dma_start(out=p_ffn_d[:, 1024 * q:1024 * (q + 1)], in_=fst[32:34, :]), reads=[fk], is_out=True)
                down_phase(g)

        S.fence()
        with ExitStack() as es5:
            grow3 = sb("grow3", [128, D], F32, es5)
            ft = [sb(f"ft{i}", [128, D], F32, es5) for i in range(4)]
            hh_ = [sb(f"hh{i}", [128, D], F32, es5) for i in range(4)]
            junk = sb("junk5", [128, D], BF16, es5)
            ss5 = sb("ss5", [128, 8], F32, es5)
            S.dma("sp", r_setup, lambda e: e.dma_start(out=grow3[:], in_=grows_d[3:4, :].partition_broadcast(128)), writes=["grow3"])
            def p5_load(tt):
                sl = tt % 4
                S.dma("sp", r_x, lambda e: e.dma_start(out=ft[sl][:], in_=f_scr[tt * 128:(tt + 1) * 128, :]),
                      reads=[("f_scr", tt, fg) for fg in range(4)], writes=[("ft", sl)])
                S.dma("sp", r_ld, lambda e: e.dma_start(out=hh_[sl][:], in_=h_scr[tt * 128:(tt + 1) * 128, :]),
                      reads=[("h_scr", 128 * tt)], writes=[("hh", sl)])

            def p5_comp(tt):
                sl = tt % 4
                c0 = 2 * sl
                S.op("act", lambda e: e.activation(out=junk[:], in_=ft[sl][:], func=AF.Square, accum_out=ss5[:, c0:c0 + 1]),
                     reads=[("ft", sl)], writes=[("ss5a", sl)])
                S.op("act", lambda e: e.activation(out=ss5[:, c0 + 1:c0 + 2], in_=ss5[:, c0:c0 + 1], func=AF.Sqrt, scale=1.0 / D, bias=EPS),
                     reads=[("ss5a", sl)], writes=[("ss5b", sl)])
                S.op("dve", lambda e: e.reciprocal(ss5[:, c0 + 1:c0 + 2], ss5[:, c0 + 1:c0 + 2]), reads=[("ss5b", sl)], writes=[("ss5c", sl)])
                S.op("dve", lambda e: e.scalar_tensor_tensor(ft[sl][:], ft[sl][:], ss5[:, c0 + 1:c0 + 2], grow3[:], ALU.mult, ALU.mult),
                     reads=[("ft", sl), ("ss5c", sl), "grow3"], writes=[("ft", sl)])
                S.op("dve", lambda e: e.tensor_tensor(ft[sl][:], ft[sl][:], hh_[sl][:], ALU.add),
                     reads=[("ft", sl), ("hh", sl)], writes=[("ft", sl)])
                S.dma("sp", r_out, lambda e: e.dma_start(out=y_d[tt * 128:(tt + 1) * 128, :], in_=ft[sl][:]),
                      reads=[("ft", sl)], is_out=True)

            for tt in range(3):
                p5_load(tt)
            for tt in range(9):
                if tt + 3 < 9:
                    p5_load(tt + 3)
                p5_comp(tt)

      except _Stop:
        pass
      S.emit()
    return nc


_CACHE = {}


def _fm(v):
    v = np.asarray(v, dtype=np.float32).reshape(-1, 128)
    return np.ascontiguousarray(v.T)


def kernel(x_prompt, x_sample, state_conf_conv, state_lru_conv, state_lru_h, state_ffn_conv,
           g_mix_pre, g_mix_post, w_in, conf_dw_w, conf_dw_b, conf_ln_g, conf_ln_b,
           lru_conv_w, lru_conv_b, lru_wa, lru_ba, lru_wx, lru_bx, lru_lambda, w_out,
           g_ffn_pre, g_ffn_post, w_up, ffn_dw_w, ffn_dw_b, w_down):
    f32 = np.float32
    x_prompt = np.asarray(x_prompt, f32)
    x_sample = np.asarray(x_sample, f32)
    vecs = np.zeros((128, V_N), f32)
    cw = np.asarray(conf_dw_w, f32)[0].reshape(31, 8, 128)
    vecs[:, V_CW:V_CW + 248] = cw.transpose(2, 1, 0).reshape(128, 248)
    vecs[:, V_CB:V_CB + 8] = _fm(conf_dw_b[0])
    vecs[:, V_LG:V_LG + 8] = _fm(conf_ln_g[0])
    vecs[:, V_LB:V_LB + 8] = _fm(conf_ln_b[0])
    rw = np.asarray(lru_conv_w, f32)[0].reshape(4, 8, 128)
    vecs[:, V_RW:V_RW + 32] = rw.transpose(2, 1, 0).reshape(128, 32)
    vecs[:, V_RB:V_RB + 8] = _fm(lru_conv_b[0])
    vecs[:, V_BA:V_BA + 8] = _fm(lru_ba[0])
    vecs[:, V_BX:V_BX + 8] = _fm(lru_bx[0])
    vecs[:, V_LAM:V_LAM + 8] = _fm(lru_lambda[0])
    fw = np.asarray(ffn_dw_w, f32)[0].reshape(3, 96, 128)
    vecs[:, V_FW:V_FW + 288] = fw.transpose(2, 1, 0).reshape(128, 288)
    vecs[:, V_FB:V_FB + 96] = _fm(ffn_dw_b[0])
    grows = np.ascontiguousarray(np.stack([np.asarray(g_mix_pre, f32)[0], np.asarray(g_mix_post, f32)[0],
                                           np.asarray(g_ffn_pre, f32)[0], np.asarray(g_ffn_post, f32)[0]]))
    ident = np.eye(128, dtype=f32)
    shared = dict(
        w_in=np.ascontiguousarray(np.asarray(w_in, f32)[0]), w_out=np.ascontiguousarray(np.asarray(w_out, f32)[0]),
        w_up=np.ascontiguousarray(np.asarray(w_up, f32)[0]), w_down=np.ascontiguousarray(np.asarray(w_down, f32)[0]),
        wa=np.ascontiguousarray(np.asarray(lru_wa, f32)[0]), wx=np.ascontiguousarray(np.asarray(lru_wx, f32)[0]),
        vecs=vecs, grows=grows, ident=ident)
    in_maps = []
    for c in range(NCORES):
        s, hf = c // 2, c % 2
        xall = np.zeros((NALL, D), f32)
        if hf == 1:
            xall[0:1024] = x_prompt[s, 0:1024]
        xall[1024:2048] = x_prompt[s, hf * 1024:(hf + 1) * 1024]
        xall[2048:2176] = x_sample[16 * c:16 * c + 16].reshape(128, D)
        fl = np.zeros((128, 8), f32)
        fl[:, 0] = float(hf)
        fl[:, 1] = float(hf)
        fl[:, 2] = 1.0 - float(hf)
        fl[:, 3] = 1.0 - float(hf)
        fl[:, 4] = float(hf)
        m = dict(shared)
        m.update(
            xall=xall, flags=fl,
            st_conf=np.ascontiguousarray(np.asarray(state_conf_conv, f32)[0, 16 * c:16 * c + 16].reshape(480, CA)),
            st_lruc=np.ascontiguousarray(np.asarray(state_lru_conv, f32)[0, 16 * c:16 * c + 16].reshape(48, CA)),
            st_h=np.ascontiguousarray(np.asarray(state_lru_h, f32)[0, 16 * c:16 * c + 16].reshape(16, CA)),
            st_ffn=np.ascontiguousarray(np.asarray(state_ffn_conv, f32)[0, 16 * c:16 * c + 16].reshape(32, 2 * DFF)),
        )
        in_maps.append(m)
    if "nc" not in _CACHE:
        _CACHE["nc"] = build_program()
    nc = _CACHE["nc"]
    res = run_bass_kernel_spmd(nc, in_maps, core_ids=list(range(NCORES)))
    R = res.results
    y_p = np.zeros((4, 2048, D), f32)
    y_s = np.zeros((128, 8, D), f32)
    p_conf = np.zeros((1, 4, 30, CA), f32)
    p_lruc = np.zeros((1, 4, 3, CA), f32)
    p_h = np.zeros((1, 4, CA), f32)
    p_ffn = np.zeros((1, 4, 2, 2 * DFF), f32)
    s_conf = np.zeros((1, 128, 30, CA), f32)
    s_lruc = np.zeros((1, 128, 3, CA), f32)
    s_h = np.zeros((1, 128, CA), f32)
    s_ffn = np.zeros((1, 128, 2, 2 * DFF), f32)
    for c in range(NCORES):
        s, hf = c // 2, c % 2
        r = R[c]
        y_p[s, hf * 1024:(hf + 1) * 1024] = r["y"][0:1024]
        y_s[16 * c:16 * c + 16] = r["y"][1024:1152].reshape(16, 8, D)
        if hf == 1:
            p_conf[0, s] = r["p_conf"]
            p_lruc[0, s] = r["p_lruc"]
            p_h[0, s] = r["p_h"][0]
            p_ffn[0, s] = r["p_ffn"]
        s_conf[0, 16 * c:16 * c + 16] = r["s_conf"].reshape(16, 30, CA)
        s_lruc[0, 16 * c:16 * c + 16] = r["s_lruc"].reshape(16, 3, CA)
        s_h[0, 16 * c:16 * c + 16] = r["s_h"]
        s_ffn[0, 16 * c:16 * c + 16] = r["s_ffn"].reshape(16, 2, 2 * DFF)
    return (y_p, y_s, p_conf, p_lruc, p_h, p_ffn, s_conf, s_lruc, s_h, s_ffn)
```
